# Optimizing a Trainium2 kernel written in Bass

```python
import jax, jax.numpy as jnp
from jax import lax
import numpy as np

D_MODEL = 4096
BATCH = 4
SEQ = 2048
DEPTH = 1
DEC_BATCH = 128
DEC_SEQ = 8
PAST_LEN = 16384
PAGE_SIZE = 128

MIX_W = D_MODEL
RWKV_W = MIX_W // 2
POOL_W = MIX_W - RWKV_W
HEAD_DIM = 64
N_HEADS = RWKV_W // HEAD_DIM
DECAY_RANK = max(32, round(1.8 * RWKV_W ** 0.5 / 32) * 32)
AAA_RANK = max(32, round(1.8 * RWKV_W ** 0.5 / 32) * 32)
GATE_RANK = max(32, round(0.6 * RWKV_W ** 0.8 / 32) * 32)
SHIFT_W = 3 * RWKV_W + DECAY_RANK + AAA_RANK + GATE_RANK
IN_W = SHIFT_W + POOL_W
SPLITS = (RWKV_W, 2 * RWKV_W, 3 * RWKV_W, 3 * RWKV_W + DECAY_RANK,
          3 * RWKV_W + DECAY_RANK + AAA_RANK)
POOL_WINDOWS = (2, 4, 8, 16)
N_POOL_GROUPS = len(POOL_WINDOWS)
POOL_GW = POOL_W // N_POOL_GROUPS
POOL_BUF = max(POOL_WINDOWS) - 1
MEM_TOKENS = 256
XA_HEADS = 4
XA_HEAD_DIM = 128
XA_W = XA_HEADS * XA_HEAD_DIM
D_FF = 4 * D_MODEL
RMS_EPS = 1e-6
GN_EPS = 64e-5

kernel_name = "rwkv7_pool_hymba_memxattn_step"


def _rmsnorm(x, g):
    xf = x.astype(jnp.float32)
    y = xf * lax.rsqrt(jnp.mean(xf * xf, axis=-1, keepdims=True) + RMS_EPS)
    return (y * g.astype(jnp.float32)).astype(x.dtype)


def _wkv_scan(S0, r, decay, k, v, kk, kka):
    def step(S, inp):
        r_t, w_t, k_t, v_t, kk_t, kka_t = inp
        sa = jnp.einsum("bhvk,bhk->bhv", S, kk_t)
        S = (S * w_t[:, :, None, :] - sa[..., None] * kka_t[:, :, None, :]
             + v_t[..., None] * k_t[:, :, None, :])
        return S, jnp.einsum("bhvk,bhk->bhv", S, r_t)
    xs = tuple(jnp.swapaxes(t, 0, 1) for t in (r, decay, k, v, kk, kka))
    S, y = lax.scan(step, S0, xs)
    return S, jnp.swapaxes(y, 0, 1)


def _mixer(h, shift_prev, pool_prev, wkv_prev, start_pos, lp):
    f32 = jnp.float32
    B, T, _ = h.shape
    z = jnp.einsum("btd,de->bte", h, lp["w_in"])
    zs, zp = z[..., :SHIFT_W], z[..., SHIFT_W:]
    zs_prev = jnp.concatenate([shift_prev[:, None, :].astype(z.dtype), zs[:, :-1]], axis=1)
    zm = zs + lp["mu_shift"] * (zs_prev - zs)
    r, k, v, wd, ad, gd = jnp.split(zm, SPLITS, axis=-1)
    w_log = -jax.nn.softplus(-(lp["w0"] + jnp.tanh(wd) @ lp["w_decay_up"]).astype(f32)) - 0.5
    decay = jnp.exp(-jnp.exp(w_log))
    a = jax.nn.sigmoid((lp["a0"] + ad @ lp["w_a_up"]).astype(f32))
    g = (jax.nn.sigmoid(gd) @ lp["w_g_up"]).astype(f32)
    hd = lambda t: t.reshape(B, T, N_HEADS, HEAD_DIM)
    kk = hd((k * lp["k_k"]).astype(f32))
    kk = kk / jnp.maximum(jnp.sqrt(jnp.sum(kk * kk, axis=-1, keepdims=True)), 1e-12)
    kf = hd(k.astype(f32) * (1.0 + (a - 1.0) * lp["k_a"].astype(f32)))
    rf, vf, ah = hd(r.astype(f32)), hd(v.astype(f32)), hd(a)
    S, y = _wkv_scan(wkv_prev.astype(f32), rf, hd(decay), kf, vf, kk, kk * ah)
    mu = jnp.mean(y, axis=-1, keepdims=True)
    var = jnp.mean(jnp.square(y - mu), axis=-1, keepdims=True)
    y = ((y - mu) * lax.rsqrt(var + GN_EPS)).reshape(B, T, RWKV_W)
    y = y * lp["lnx_g"].astype(f32) + lp["lnx_b"].astype(f32)
    bonus = jnp.sum(rf * kf * lp["r_k"].astype(f32), axis=-1, keepdims=True) * vf
    o_rwkv = ((y + bonus.reshape(B, T, RWKV_W)) * g).astype(h.dtype)
    seq = jnp.concatenate([pool_prev.astype(zp.dtype), zp], axis=1)
    cs = jnp.cumsum(seq.astype(f32), axis=1)
    cs = jnp.concatenate([jnp.zeros((B, 1, POOL_W), f32), cs], axis=1)
    pos = start_pos + jnp.arange(T)
    end = POOL_BUF + 1
    means = []
    for gi, wdw in enumerate(POOL_WINDOWS):
        c0, c1 = gi * POOL_GW, (gi + 1) * POOL_GW
        s = cs[:, end:end + T, c0:c1] - cs[:, end - wdw:end - wdw + T, c0:c1]
        cnt = jnp.minimum(wdw, pos + 1).astype(f32)[None, :, None]
        means.append(s / cnt)
    d = jnp.stack(means, axis=2) - zp.reshape(B, T, N_POOL_GROUPS, POOL_GW).astype(f32)
    o_pool = jnp.einsum("btgc,gce->btge", d.astype(h.dtype), lp["w_pool"])
    o_pool = (o_pool.reshape(B, T, POOL_W) * lp["pool_scale"]).astype(h.dtype)
    out = jnp.concatenate([o_rwkv, o_pool], axis=-1) @ lp["w_out"]
    return out, zs[:, -1], seq[:, -POOL_BUF:], S.astype(wkv_prev.dtype)


def _mem_kv(mem, g, w_mk, w_mv):
    B = mem.shape[0]
    m = _rmsnorm(mem, g)
    mk = jnp.einsum("bmd,de->bme", m, w_mk).reshape(B, MEM_TOKENS, XA_HEADS, XA_HEAD_DIM)
    mv = jnp.einsum("bmd,de->bme", m, w_mv).reshape(B, MEM_TOKENS, XA_HEADS, XA_HEAD_DIM)
    return mk, mv


def _cross_attn(h, mk, mv, w_q, w_o):
    B, T, _ = h.shape
    q = (h @ w_q).reshape(B, T, XA_HEADS, XA_HEAD_DIM)
    s = jnp.einsum("bthd,bmhd->bhtm", q, mk.astype(q.dtype)).astype(jnp.float32)
    p = jax.nn.softmax(s * (XA_HEAD_DIM ** -0.5), axis=-1).astype(h.dtype)
    o = jnp.einsum("bhtm,bmhd->bthd", p, mv.astype(h.dtype)).reshape(B, T, XA_W)
    return o @ w_o


def _block(x, mk, mv, shift_prev, pool_prev, wkv_prev, start_pos, lp):
    mix, sh, pl, wkv = _mixer(_rmsnorm(x, lp["norm_mix_g"]), shift_prev, pool_prev,
                              wkv_prev, start_pos, lp)
    x = x + mix
    x = x + _cross_attn(_rmsnorm(x, lp["norm_xa_g"]), mk, mv, lp["w_xq"], lp["w_xo"])
    hf = _rmsnorm(x, lp["norm_ffn_g"])
    x = x + jnp.square(jax.nn.relu(hf @ lp["w_up"])) @ lp["w_down"]
    return x, sh, pl, wkv


def setup_inputs(seed: int = 0) -> dict:
    key = jax.random.key(seed)
    ks = jax.random.split(key, 48)
    cnt = [0]

    def nk():
        cnt[0] += 1
        return ks[cnt[0] - 1]

    def nrm(shape, scale):
        return jax.random.normal(nk(), shape, jnp.float32) * scale

    def uni(shape, lo, hi):
        return jax.random.uniform(nk(), shape, jnp.float32, lo, hi)

    L = DEPTH
    return {
        "x_prompt": nrm((BATCH, SEQ, D_MODEL), 1.0),
        "x_sample": nrm((DEC_BATCH, DEC_SEQ, D_MODEL), 1.0),
        "mem_prompt": nrm((BATCH, MEM_TOKENS, D_MODEL), 1.0),
        "state_wkv": nrm((L, DEC_BATCH, N_HEADS, HEAD_DIM, HEAD_DIM), 1.0),
        "state_shift": nrm((L, DEC_BATCH, SHIFT_W), 1.0),
        "state_pool": nrm((L, DEC_BATCH, POOL_BUF, POOL_W), 1.0),
        "cache_mem_k": nrm((L, DEC_BATCH, MEM_TOKENS, XA_HEADS, XA_HEAD_DIM), 1.0),
        "cache_mem_v": nrm((L, DEC_BATCH, MEM_TOKENS, XA_HEADS, XA_HEAD_DIM), 1.0),
        "norm_mix_g": 1.0 + nrm((L, D_MODEL), 0.01),
        "w_in": nrm((L, D_MODEL, IN_W), D_MODEL ** -0.5),
        "mu_shift": uni((L, SHIFT_W), 0.0, 1.0),
        "w0": uni((L, RWKV_W), -6.0, 0.0),
        "w_decay_up": nrm((L, DECAY_RANK, RWKV_W), 0.5 * DECAY_RANK ** -0.5),
        "a0": nrm((L, RWKV_W), 0.1),
        "w_a_up": nrm((L, AAA_RANK, RWKV_W), 0.5 * AAA_RANK ** -0.5),
        "w_g_up": nrm((L, GATE_RANK, RWKV_W), GATE_RANK ** -0.5),
        "k_k": 0.85 + nrm((L, RWKV_W), 0.05),
        "k_a": 1.0 + nrm((L, RWKV_W), 0.05),
        "r_k": nrm((L, N_HEADS, HEAD_DIM), 0.1),
        "lnx_g": 1.0 + nrm((L, RWKV_W), 0.01),
        "lnx_b": nrm((L, RWKV_W), 0.01),
        "w_pool": nrm((L, N_POOL_GROUPS, POOL_GW, POOL_GW), POOL_GW ** -0.5),
        "pool_scale": 1.0 + nrm((L, POOL_W), 0.1),
        "w_out": nrm((L, MIX_W, D_MODEL), MIX_W ** -0.5),
        "norm_xa_g": 1.0 + nrm((L, D_MODEL), 0.01),
        "norm_mem_g": 1.0 + nrm((L, D_MODEL), 0.01),
        "w_xq": nrm((L, D_MODEL, XA_W), D_MODEL ** -0.5),
        "w_mk": nrm((L, D_MODEL, XA_W), D_MODEL ** -0.5),
        "w_mv": nrm((L, D_MODEL, XA_W), D_MODEL ** -0.5),
        "w_xo": nrm((L, XA_W, D_MODEL), XA_W ** -0.5),
        "norm_ffn_g": 1.0 + nrm((L, D_MODEL), 0.01),
        "w_up": nrm((L, D_MODEL, D_FF), D_MODEL ** -0.5),
        "w_down": nrm((L, D_FF, D_MODEL), D_FF ** -0.5),
        "norm_final_g": 1.0 + nrm((D_MODEL,), 0.01),
    }


def reference(x_prompt, x_sample, mem_prompt, state_wkv, state_shift, state_pool,
              cache_mem_k, cache_mem_v, norm_mix_g, w_in, mu_shift, w0, w_decay_up, a0,
              w_a_up, w_g_up, k_k, k_a, r_k, lnx_g, lnx_b, w_pool, pool_scale, w_out,
              norm_xa_g, norm_mem_g, w_xq, w_mk, w_mv, w_xo, norm_ffn_g, w_up, w_down,
              norm_final_g):
    xp, xs = x_prompt, x_sample
    Bp = x_prompt.shape[0]
    wkv_p_l, sh_p_l, pl_p_l, mk_p_l, mv_p_l = [], [], [], [], []
    wkv_s_l, sh_s_l, pl_s_l = [], [], []
    for l in range(DEPTH):
        lp = {
            "norm_mix_g": norm_mix_g[l], "w_in": w_in[l], "mu_shift": mu_shift[l],
            "w0": w0[l], "w_decay_up": w_decay_up[l], "a0": a0[l], "w_a_up": w_a_up[l],
            "w_g_up": w_g_up[l], "k_k": k_k[l], "k_a": k_a[l], "r_k": r_k[l],
            "lnx_g": lnx_g[l], "lnx_b": lnx_b[l], "w_pool": w_pool[l],
            "pool_scale": pool_scale[l], "w_out": w_out[l], "norm_xa_g": norm_xa_g[l],
            "w_xq": w_xq[l], "w_xo": w_xo[l], "norm_ffn_g": norm_ffn_g[l],
            "w_up": w_up[l], "w_down": w_down[l],
        }
        mk_p, mv_p = _mem_kv(mem_prompt, norm_mem_g[l], w_mk[l], w_mv[l])
        xp, sh_p, pl_p, wkv_p = _block(
            xp, mk_p, mv_p,
            jnp.zeros((Bp, SHIFT_W), xp.dtype),
            jnp.zeros((Bp, POOL_BUF, POOL_W), xp.dtype),
            jnp.zeros((Bp, N_HEADS, HEAD_DIM, HEAD_DIM), jnp.float32),
            0, lp)
        xs, sh_s, pl_s, wkv_s = _block(
            xs, cache_mem_k[l], cache_mem_v[l], state_shift[l], state_pool[l],
            state_wkv[l], PAST_LEN, lp)
        wkv_p_l.append(wkv_p); sh_p_l.append(sh_p); pl_p_l.append(pl_p)
        mk_p_l.append(mk_p); mv_p_l.append(mv_p)
        wkv_s_l.append(wkv_s); sh_s_l.append(sh_s); pl_s_l.append(pl_s)
    y_prompt = _rmsnorm(xp, norm_final_g)
    y_sample = _rmsnorm(xs, norm_final_g)
    return (y_prompt, y_sample,
            jnp.stack(wkv_p_l), jnp.stack(sh_p_l), jnp.stack(pl_p_l),
            jnp.stack(mk_p_l), jnp.stack(mv_p_l),
            jnp.stack(wkv_s_l), jnp.stack(sh_s_l), jnp.stack(pl_s_l))
```

```python
import contextlib
import math
import numpy as np
import ml_dtypes
import concourse.bass as bass
import concourse.mybir as mybir
from concourse.bass_utils import run_bass_kernel_spmd

F32 = mybir.dt.float32
BF16 = mybir.dt.bfloat16
AF = mybir.ActivationFunctionType
ALU = mybir.AluOpType
AX = mybir.AxisListType

ENGS = ["pe", "act", "dve", "pool", "sp"]
D = 4096
NCH = 32
TA = 896
TB = 1280
NT = TA + TB
OWN = 1152
RW = 2048
SHIFT_W = 6592
IN_W = 8640
C0 = math.exp(-0.5)


class Buf:
    __slots__ = ("name", "w", "r")

    def __init__(self, name=""):
        self.name = name
        self.w = None
        self.r = {}


class Sched:
    def __init__(self, nc, es, ndma=24):
        self.nc = nc
        self.ops = {e: [] for e in ENGS}
        self.cnt = {e: 0 for e in ENGS}
        self.waited = {e: {} for e in ENGS}
        self.esem = {e: es.enter_context(nc.semaphore("s_" + e)) for e in ENGS}
        self.dsem = {}
        self.dnext = {}
        self.ndma = ndma
        for q in ["sp", "pool"]:
            self.dsem[q] = [es.enter_context(nc.semaphore("d_%s_%d" % (q, i))) for i in range(ndma)]
            self.dnext[q] = 0

    def sem_of(self, key):
        if isinstance(key, tuple):
            return self.dsem[key[0]][key[1]]
        return self.esem[key]

    def _deps(self, eng, reads, writes):
        need = {}

        def add(tok, same_ok):
            if tok is None:
                return
            k, v = tok
            if k == eng and (not same_ok or eng == "pe"):
                return
            if need.get(k, 0) < v:
                need[k] = v
        for b in reads:
            add(b.w, True)
        for b in writes:
            add(b.w, True)
            for k, v in b.r.items():
                add((k, v), False)
        for k, v in need.items():
            if self.waited[eng].get(k, 0) < v:
                self.waited[eng][k] = v
                self.ops[eng].append(("wait", k, v))

    def _mark(self, tok, reads, writes):
        k, v = tok
        for b in reads:
            if b.r.get(k, 0) < v:
                b.r[k] = v
        for b in writes:
            b.w = tok
            b.r = {}

    def op(self, eng, fn, reads=(), writes=()):
        return self.group(eng, [fn], reads, writes)

    def group(self, eng, fns, reads=(), writes=()):
        self._deps(eng, reads, writes)
        self.cnt[eng] += 1
        tok = (eng, self.cnt[eng])
        n = len(fns)
        for i, fn in enumerate(fns):
            self.ops[eng].append(("op", fn, i == n - 1))
        self._mark(tok, reads, writes)
        return tok

    def dma(self, q, out_ap, in_ap, reads=(), writes=()):
        self._deps(q, reads, writes)
        i = self.dnext[q]
        self.dnext[q] += 1
        slot = i % self.ndma
        k = i // self.ndma
        key = (q, slot)
        if k > 0 and self.waited[q].get(key, 0) < 16 * k:
            self.waited[q][key] = 16 * k
            self.ops[q].append(("wait", key, 16 * k))
        self.ops[q].append(("dma", out_ap, in_ap, key))
        tok = (key, 16 * (k + 1))
        self._mark(tok, reads, writes)
        return tok

    def _all_last(self):
        last = {}
        for q in ["sp", "pool"]:
            n = self.dnext[q]
            for slot in range(min(n, self.ndma)):
                uses = (n - 1 - slot) // self.ndma + 1
                last[(q, slot)] = 16 * uses
        for e in ["pe", "act", "dve", "pool"]:
            if self.cnt[e] > 0:
                last[e] = self.cnt[e]
        return last

    def barrier(self, engs=ENGS):
        last = self._all_last()
        for e in engs:
            for k, v in last.items():
                if k == e:
                    continue
                if self.waited[e].get(k, 0) < v:
                    self.waited[e][k] = v
                    self.ops[e].append(("wait", k, v))

    def emit(self):
        nc = self.nc
        self.barrier(["sp"])

        def run(name, e):
            sem = self.esem[name]
            for it in self.ops[name]:
                if it[0] == "wait":
                    e.wait_ge(self.sem_of(it[1]), it[2])
                elif it[0] == "op":
                    ins = it[1](e)
                    if it[2]:
                        ins.then_inc(sem, 1)
                else:
                    e.dma_start(out=it[1], in_=it[2]).then_inc(self.sem_of(it[3]), 16)

        with nc.Block() as block:
            @block.sync
            def _(e):
                run("sp", e)

            @block.tensor
            def _(e):
                run("pe", e)

            @block.scalar
            def _(e):
                run("act", e)

            @block.vector
            def _(e):
                run("dve", e)

            @block.gpsimd
            def _(e):
                run("pool", e)


class Arena:
    def __init__(self, nc, es, kbytes):
        self.t = es.enter_context(nc.sbuf_tensor("arena", [128, kbytes * 256], F32))
        self.cap = kbytes * 1024
        self.off = 0

    def alloc(self, shape, dt):
        esz = 2 if dt == BF16 else 4
        n = 1
        for s in shape:
            n *= s
        nb = (n * esz + 63) // 64 * 64
        assert self.off + nb <= self.cap, "SBUF arena overflow %d + %d" % (self.off, nb)
        ap = self.t[:, self.off // 4:(self.off + nb) // 4]
        self.off += nb
        if dt == BF16:
            ap = ap.bitcast(BF16)
        ap = ap[:, 0:n]
        if len(shape) == 2:
            ap = ap.rearrange("p (a b) -> p a b", a=shape[0])
        elif len(shape) == 3:
            ap = ap.rearrange("p (a b c) -> p a b c", a=shape[0], b=shape[1])
        elif len(shape) == 4:
            ap = ap.rearrange("p (a b c d) -> p a b c d", a=shape[0], b=shape[1], c=shape[2])
        return ap

    def mark(self):
        return self.off

    def release(self, m):
        self.S.barrier()
        self.off = m


IN_SHAPES = {
    "xin": [NT, D], "mem": [256, D], "st_wkv": [16, 32, 64, 64], "st_shift": [16, SHIFT_W], "st_pool": [16, 15, RW],
    "ck": [16, 256, 512], "cv": [16, 256, 512], "w_in": [D, IN_W], "w_out": [D, D], "w_pool": [4, 512, 512],
    "w_decay_up": [96, RW], "w_a_up": [96, RW], "w_g_up": [256, RW], "w_xq": [D, 512], "w_mk": [D, 512], "w_mv": [D, 512],
    "w_xo": [512, D], "w_up": [D, 4 * D], "w_down": [4 * D, D],
    "norm_mix_g": [128, 32], "norm_xa_g": [128, 32], "norm_mem_g": [128, 32], "norm_ffn_g": [128, 32], "norm_final_g": [128, 32],
    "mu_shift": [128, 52], "w0": [128, 16], "a0": [128, 16], "k_k": [128, 16], "k_a": [128, 16], "r_k": [128, 16],
    "lnx_g": [128, 16], "lnx_b": [128, 16], "pool_scale": [128, 16],
    "c_identf": [128, 128], "c_identb": [128, 128], "c_masks": [128, 8, 64], "c_bones": [128, 128],
    "norm_final_row": [D], "c_scanmask": [128, 640], "c_invcnt": [4, OWN], "c_seqmask": [128, 8, 64], "c_tokmask": [128, 8], "c_qmask": [128, 16, 128],
}


class K:
    def __getattr__(self, name):
        if name in IN_SHAPES:
            ap = self.nc.dram_tensor(name, list(IN_SHAPES[name]), F32, kind="ExternalInput").ap()
            self.used_inputs.append(name)
            setattr(self, name, ap)
            return ap
        raise AttributeError(name)


def build_program(dbg=None, stop=99):
    nc = bass.Bass("TRN2", target_bir_lowering=False)
    g = K()
    g.nc = nc
    g.used_inputs = []
    g.stop = stop
    I = lambda name, shape, dt=F32: nc.dram_tensor(name, list(shape), dt, kind="ExternalInput").ap()
    O = lambda name, shape, dt=F32: nc.dram_tensor(name, list(shape), dt, kind="ExternalOutput").ap()
    T = lambda name, shape, dt=F32: nc.dram_tensor(name, list(shape), dt).ap()
    g.y_out = O("y_out", [OWN, D])
    g.o_wkv_p = O("o_wkv_p", [32, 64, 64])
    g.o_shift_p = O("o_shift_p", [SHIFT_W])
    g.o_pool_p = O("o_pool_p", [15, RW])
    g.o_mk = O("o_mk", [256, 512])
    g.o_mv = O("o_mv", [256, 512])
    g.o_wkv_s = O("o_wkv_s", [16, 32, 64, 64])
    g.o_shift_s = O("o_shift_s", [16, SHIFT_W])
    g.o_pool_s = O("o_pool_s", [16, 15, RW])
    g.zT = T("zT", [IN_W, NT])
    g.oTd = T("oTd", [D, OWN], BF16)
    g.xs2 = T("xs2", [OWN, D])
    g.xs = T("xs", [OWN, D])
    g.dbg = dbg
    if dbg in ("m1", "m1s"):
        g.d_zT = O("d_zT", [IN_W, NT])
        g.d_hT = O("d_hT", [128, 32, 128])

    with contextlib.ExitStack() as es:
        S = Sched(nc, es)
        g.S = S
        g.A = Arena(nc, es, 204)
        g.A.S = S
        g.ps = es.enter_context(nc.psum_tensor("ps", [128, 4096], F32))
        g.PB = [Buf("psum%d" % i) for i in range(8)]
        setup_consts(g)
        g.A0 = g.A.off
        phase_m1(g)
        if dbg in ("m1", "m1s"):
            S.barrier()
            S.dma("sp", g.d_zT, g.zT)
        else:
            phase_m2(g)
            S.barrier()
            g.A.off = g.A0
            if g.stop > 5:
                phase_m3(g)
            if g.stop > 6:
                phase_m4(g)
            if g.stop > 7:
                phase_x(g)
            if g.stop > 8:
                phase_f(g)
            if dbg == "mix":
                S.barrier()
                S.dma("sp", g.y_out, g.xs if g.stop <= 7 else g.xs2)
        S.emit()
    return nc, g


def bank(g, i, dt=F32):
    ap = g.ps[:, i * 512:(i + 1) * 512]
    if dt == BF16:
        ap = ap.bitcast(BF16)
    return ap


def pvec(ap1d, n_chunks):
    return ap1d.rearrange("(c p) -> p c", p=128)


def setup_consts(g):
    S, A = g.S, g.A
    g.CB = Buf("consts")
    g.identf = A.alloc([128], F32)
    g.identb = A.alloc([128], BF16)
    g.bones = A.alloc([128], BF16)
    g.masks = A.alloc([8, 64], F32)
    S.dma("sp", g.identf, g.c_identf, writes=[g.CB])
    S.dma("pool", g.identb, g.c_identb, writes=[g.CB])
    S.dma("pool", g.bones, g.c_bones, writes=[g.CB])
    S.dma("sp", g.masks, g.c_masks, writes=[g.CB])


def norm_transpose(g, src_rows, ntiles, gvec, gB, hT, hB, tok0=0):
    S, A = g.S, g.A
    m = A.mark()
    xt = [A.alloc([D], F32) for _ in range(2)]
    XB = [Buf("xt0"), Buf("xt1")]
    junk = A.alloc([D], BF16)
    JB = Buf("junk")
    xsb = [A.alloc([D], BF16) for _ in range(2)]
    XS = [Buf("xs0"), Buf("xs1")]
    st = A.alloc([ntiles, 4], F32)
    SB = [Buf("st%d" % i) for i in range(ntiles)]
    for i in range(ntiles):
        b = i % 2
        S.dma("sp", xt[b], src_rows[i], writes=[XB[b]])
        S.op("act", lambda e, b=b, i=i: e.activation(junk, xt[b], AF.Square, accum_out=st[:, i, 0:1]),
             reads=[XB[b]], writes=[JB, SB[i]])
        S.op("act", lambda e, i=i: e.activation(st[:, i, 1:2], st[:, i, 0:1], AF.Sqrt, scale=1.0 / D, bias=1e-6),
             reads=[SB[i]], writes=[SB[i]])
        S.op("dve", lambda e, i=i: e.reciprocal(st[:, i, 2:3], st[:, i, 1:2]), reads=[SB[i]], writes=[SB[i]])
        S.op("act", lambda e, b=b, i=i: e.activation(xsb[b], xt[b], AF.Copy, scale=st[:, i, 2:3]),
             reads=[XB[b], SB[i]], writes=[XS[b]])
        for q in range(4):
            pb = 4 + (i * 4 + q) % 4
            pt = bank(g, pb, BF16)
            S.group("pe", [lambda e, b=b, q=q, j=j, pt=pt: e.transpose(pt[:, j * 128:(j + 1) * 128], xsb[b][:, (q * 8 + j) * 128:(q * 8 + j + 1) * 128], g.identb)
                           for j in range(8)], reads=[XS[b], g.CB], writes=[g.PB[pb]])
            S.op("dve", lambda e, q=q, i=i, pt=pt: e.tensor_tensor(
                hT[:, q * 8:(q + 1) * 8, tok0 + i * 128:tok0 + (i + 1) * 128],
                pt.rearrange("p (a b) -> p a b", a=8),
                gvec[:, q * 8:(q + 1) * 8].unsqueeze(2).to_broadcast([128, 8, 128]), ALU.mult),
                reads=[g.PB[pb], gB], writes=[hB])
    A.release(m)


def col_chunks(lo, hi):
    out = []
    c = lo
    while c < hi:
        if c == 6144 or c == 6240:
            sz = 96
        else:
            sz = 128
        out.append((c, sz))
        c += sz
    return out


def gemm_zT(g, hT, hB, ntok, tok_col0, col_blocks, tgroups):
    S, A = g.S, g.A
    m = A.mark()
    wb = [A.alloc([NCH, 512], BF16) for _ in range(2)]
    WB = [Buf("wb0"), Buf("wb1")]
    stg = [A.alloc([512], F32) for _ in range(4)]
    SG = [Buf("stg%d" % i) for i in range(4)]
    w_v = g.w_in.rearrange("(kc p) c -> p kc c", p=128)
    si = 0
    pi = 0
    for bi, (lo, hi) in enumerate(col_blocks):
        b = bi % 2
        nb = hi - lo
        for q4 in range(4):
            S.dma("pool", wb[b][:, q4 * 8:(q4 + 1) * 8, 0:nb], w_v[:, q4 * 8:(q4 + 1) * 8, lo:hi], writes=[WB[b]])
        for (c0, csz) in col_chunks(lo, hi):
            for (t0, tn) in tgroups:
                pb = pi % 4
                pi += 1
                pt = bank(g, pb)
                S.group("pe", [lambda e, b=b, kc=kc, c0=c0, csz=csz, t0=t0, tn=tn, pt=pt, lo=lo: e.matmul(
                    pt[0:csz, 0:tn], wb[b][:, kc, c0 - lo:c0 - lo + csz], hT[:, kc, t0:t0 + tn],
                    start=(kc == 0), stop=(kc == NCH - 1)) for kc in range(NCH)],
                    reads=[WB[b], hB], writes=[g.PB[pb]])
                s = si % 4
                si += 1
                eng = "act" if s % 2 == 0 else "dve"
                if eng == "act":
                    S.op("act", lambda e, s=s, csz=csz, tn=tn, pt=pt: e.activation(stg[s][0:csz, 0:tn], pt[0:csz, 0:tn], AF.Copy),
                         reads=[g.PB[pb]], writes=[SG[s]])
                else:
                    S.op("dve", lambda e, s=s, csz=csz, tn=tn, pt=pt: e.tensor_copy(stg[s][0:csz, 0:tn], pt[0:csz, 0:tn]),
                         reads=[g.PB[pb]], writes=[SG[s]])
                S.dma("sp", g.zT[c0:c0 + csz, tok_col0 + t0:tok_col0 + t0 + tn], stg[s][0:csz, 0:tn], reads=[SG[s]], writes=[Buf("zw")])
    A.release(m)


def phase_m1(g):
    S, A = g.S, g.A
    g.ZB = Buf("zT")
    m = A.mark()
    gv = A.alloc([NCH], F32)
    GV = Buf("gv")
    S.dma("sp", gv, g.norm_mix_g, writes=[GV])
    hT = A.alloc([NCH, TB], BF16)
    hB = Buf("hT")
    xin_t = g.xin.rearrange("(n p) d -> n p d", p=128)
    if g.dbg == "m1s":
        norm_transpose(g, [xin_t[7]], 1, gv, GV, hT, hB, 0)
        hf = A.alloc([32, 128], F32)
        HF = Buf("hf")
        S.op("dve", lambda e: e.tensor_copy(hf, hT[:, :, 0:128]), reads=[hB], writes=[HF])
        S.dma("sp", g.d_hT, hf, reads=[HF])
        gemm_zT(g, hT, hB, 128, TA, [(0, 512)], [(0, 128)])
        return
    norm_transpose(g, [xin_t[i] for i in range(7)], 7, gv, GV, hT, hB, 0)
    gemm_zT(g, hT, hB, TA, 0, [(2048, 2560), (2560, 3072), (3072, 3584), (3584, 4096),
                               (4096, 4608), (4608, 5120), (5120, 5632), (5632, 6144), (6144, 6336)],
            [(0, 512), (512, 384)])
    norm_transpose(g, [xin_t[7 + i] for i in range(10)], 10, gv, GV, hT, hB, 0)
    blocks = [(i * 512, (i + 1) * 512) for i in range(12)] + [(6144, 6592)] + [(6592 + i * 512, 6592 + (i + 1) * 512) for i in range(4)]
    gemm_zT(g, hT, hB, TB, TA, blocks, [(0, 512), (512, 512), (1024, 256)])
    A.release(m)


def make_consts():
    c = {}
    c["c_identf"] = np.eye(128, dtype=np.float32)
    c["c_identb"] = np.eye(128, dtype=np.float32)
    p = np.arange(128)[:, None]
    t = np.arange(64)[None, :]
    i = p % 64
    msu = (i < t).astype(np.float32)
    miu = (i <= t).astype(np.float32)
    msl = (i > t).astype(np.float32)
    blk = ((i // 8) == (t // 8)).astype(np.float32)
    c["c_masks"] = np.stack([msu, miu, -msu, -msl, msu * blk, miu * blk, -msu * blk, -msl * blk], axis=1).astype(np.float32)
    bo = np.zeros((128, 128), np.float32)
    bo[:64, :64] = 1
    bo[64:, 64:] = 1
    c["c_bones"] = bo
    sm = np.ones((128, 640), np.float32)
    sm[:, 0:512:64] = 0
    sm[:, 512::8] = 0
    c["c_scanmask"] = sm
    s8 = np.arange(8)[None, :, None]
    tt = np.arange(64)[None, None, :]
    c["c_seqmask"] = np.broadcast_to((tt // 8 == s8), (128, 8, 64)).astype(np.float32).copy()
    c["c_tokmask"] = ((i // 8) == np.arange(8)[None, :]).astype(np.float32)
    s16 = np.arange(16)[None, :, None]
    t128 = np.arange(128)[None, None, :]
    c["c_qmask"] = np.broadcast_to((t128 // 8 == s16), (128, 16, 128)).astype(np.float32).copy()
    return c


def make_in_map(inp, c, consts):
    b, half = c // 2, c % 2
    f = lambda a: np.ascontiguousarray(a, dtype=np.float32)
    m = dict(consts)
    xin = np.zeros((NT, D), np.float32)
    xp = inp["x_prompt"][b]
    if half == 1:
        xin[0:1024] = xp[0:1024]
    xin[1024:2048] = xp[half * 1024:(half + 1) * 1024]
    xin[2048:2176] = inp["x_sample"][16 * c:16 * c + 16].reshape(128, D)
    m["xin"] = xin
    m["mem"] = f(inp["mem_prompt"][b])
    m["st_wkv"] = f(inp["state_wkv"][0, 16 * c:16 * c + 16])
    m["st_shift"] = f(inp["state_shift"][0, 16 * c:16 * c + 16])
    m["st_pool"] = f(inp["state_pool"][0, 16 * c:16 * c + 16])
    m["ck"] = f(inp["cache_mem_k"][0, 16 * c:16 * c + 16].reshape(16, 256, 512))
    m["cv"] = f(inp["cache_mem_v"][0, 16 * c:16 * c + 16].reshape(16, 256, 512))
    for nm in ["w_in", "w_out", "w_pool", "w_decay_up", "w_a_up", "w_g_up", "w_xq", "w_mk", "w_mv", "w_xo", "w_up", "w_down",
               ]:
        m[nm] = f(inp[nm][0])
    pv = lambda a: np.ascontiguousarray(np.asarray(a, np.float32).reshape(-1, 128).T)
    for nm in ["norm_mix_g", "norm_xa_g", "norm_mem_g", "norm_ffn_g", "w0", "a0", "k_k", "k_a", "lnx_g", "lnx_b", "pool_scale", "r_k"]:
        m[nm] = pv(inp[nm][0])
    m["norm_final_g"] = pv(inp["norm_final_g"])
    m["norm_final_row"] = f(inp["norm_final_g"])
    mu = np.asarray(inp["mu_shift"][0], np.float32)
    mup = np.zeros((128, 52), np.float32)
    mup[:, 0:48] = mu[0:6144].reshape(48, 128).T
    mup[0:96, 48] = mu[6144:6240]
    mup[0:96, 49] = mu[6240:6336]
    mup[:, 50:52] = mu[6336:6592].reshape(2, 128).T
    m["mu_shift"] = mup
    pos0 = half * 1024
    ic = np.zeros((4, OWN), np.float32)
    for gi, w in enumerate((2, 4, 8, 16)):
        ic[gi, :1024] = 1.0 / np.minimum(w, pos0 + np.arange(1024) + 1)
        ic[gi, 1024:] = 1.0 / w
    m["c_invcnt"] = ic
    return m


def bc(ap, shape):
    return ap.to_broadcast(list(shape))


def phase_m2(g):
    S, A = g.S, g.A
    ps = g.ps
    g.OTB = Buf("oTd")
    CB = Buf("m2consts")
    ld = lambda shape, src, dt=F32, q="sp": (lambda t: (S.dma(q if dt == F32 else "pool", t, src, writes=[CB]), t)[1])(A.alloc(shape, dt))
    scanmask = ld([640], g.c_scanmask)
    seqmask = ld([8, 64], g.c_seqmask)
    tokmask = ld([8], g.c_tokmask)
    mu = ld([52], g.mu_shift)
    w0 = ld([16], g.w0)
    a0 = ld([16], g.a0)
    k_k = ld([16], g.k_k)
    k_a = ld([16], g.k_a)
    r_k = ld([16], g.r_k)
    lnx_g = ld([16], g.lnx_g)
    lnx_b = ld([16], g.lnx_b)
    omka = A.alloc([16], F32)
    S.op("dve", lambda e: e.tensor_scalar(omka, k_a, -1.0, 1.0, ALU.mult, ALU.add), reads=[CB], writes=[CB])
    wdu = A.alloc([RW], BF16)
    wau = A.alloc([RW], BF16)
    wgu = A.alloc([2, RW], BF16)
    S.dma("pool", wdu[0:96, :], g.w_decay_up, writes=[CB])
    S.dma("pool", wau[0:96, :], g.w_a_up, writes=[CB])
    S.dma("pool", wgu, g.w_g_up.rearrange("(c p) n -> p c n", p=128), writes=[CB])
    identf, identb, bones, masks = g.identf, g.identb, g.bones, g.masks
    CBs = [CB, g.CB]
    shst = A.alloc([2, 3, 128], F32)
    OSB = Buf("shst")
    cl = col_chunks(0, SHIFT_W)

    stT = A.alloc([52, 16], F32)
    STB = Buf("stT")
    mk0 = A.mark()
    strow = A.alloc([SHIFT_W], F32)
    SR = Buf("strow")
    S.dma("sp", strow[0:16, :], g.st_shift, writes=[SR])
    for half, (j0, j1) in enumerate([(0, 32), (32, 52)]):
        S.group("pe", [lambda e, j=j, half=half, j0=j0: e.transpose(ps[0:cl[j][1], half * 512 + (j - j0) * 16: half * 512 + (j - j0 + 1) * 16],
                                                                   strow[0:16, cl[j][0]:cl[j][0] + cl[j][1]], identf[0:16, 0:16])
                       for j in range(j0, j1)], reads=[SR, g.CB], writes=[g.PB[half]])
    S.op("dve", lambda e: e.tensor_copy(stT[:, 0:32, :], ps[:, 0:512].rearrange("p (a b) -> p a b", b=16)), reads=[g.PB[0]], writes=[STB])
    S.op("dve", lambda e: e.tensor_copy(stT[:, 32:48, :], ps[:, 512:768].rearrange("p (a b) -> p a b", b=16)), reads=[g.PB[1]], writes=[STB])
    S.op("dve", lambda e: e.tensor_copy(stT[0:96, 48:50, :], ps[0:96, 768:800].rearrange("p (a b) -> p a b", b=16)), reads=[g.PB[1]], writes=[STB])
    S.op("dve", lambda e: e.tensor_copy(stT[:, 50:52, :], ps[:, 800:832].rearrange("p (a b) -> p a b", b=16)), reads=[g.PB[1]], writes=[STB])
    A.release(mk0)
    if g.stop <= 1:
        return

    tw = A.alloc([NT], BF16)
    adm = A.alloc([NT], BF16)
    sg = A.alloc([2, NT], BF16)
    LRB = Buf("lowrank")
    mk1 = A.mark()
    LP = 1 + 2048 + 144
    for j, (dst, func) in enumerate([(tw, AF.Tanh), (adm, AF.Copy), (sg[:, 0, :], AF.Sigmoid), (sg[:, 1, :], AF.Sigmoid)]):
        c0, csz = cl[48 + j]
        lz = A.alloc([LP], F32)
        dd = A.alloc([NT], F32)
        LZ = Buf("lz")
        DD = Buf("dd")
        S.op("dve", lambda e, lz=lz: e.memset(lz[:, 0:1], 0.0), writes=[LZ])
        S.dma("sp", lz[0:csz, 1:2049], g.zT[c0:c0 + csz, 0:2048], reads=[g.ZB], writes=[LZ])
        lzs = lz[:, 2049:LP].rearrange("p (s t) -> p s t", t=9)
        S.dma("sp", lzs[0:csz, :, 1:9], g.zT[c0:c0 + csz, 2048:NT].rearrange("p (s t) -> p s t", t=8), reads=[g.ZB], writes=[LZ])
        S.op("dve", lambda e, lzs=lzs, j=j, csz=csz: e.tensor_copy(lzs[0:csz, :, 0], stT[0:csz, 48 + j, :]), reads=[STB], writes=[LZ])
        pb = 2 + j % 2
        S.group("pe", [lambda e, lzs=lzs, csz=csz, pb=pb: e.transpose(ps[0:16, pb * 512:pb * 512 + csz], lzs[0:csz, :, 8], identf[0:csz, 0:csz]),
                       lambda e, lz=lz, csz=csz, pb=pb: e.transpose(ps[0:1, pb * 512 + 128:pb * 512 + 128 + csz], lz[0:csz, 2048:2049], identf[0:csz, 0:csz])],
                reads=[LZ, g.CB], writes=[g.PB[pb]])
        S.op("act", lambda e, c0=c0, csz=csz, pb=pb: e.activation(shst[0:16, 0, 0, 0:csz], ps[0:16, pb * 512:pb * 512 + csz], AF.Copy), reads=[g.PB[pb]], writes=[OSB])
        S.op("act", lambda e, c0=c0, csz=csz, pb=pb: e.activation(shst[0:1, 1, 0, 0:csz], ps[0:1, pb * 512 + 128:pb * 512 + 128 + csz], AF.Copy), reads=[g.PB[pb]], writes=[OSB])
        S.dma("sp", g.o_shift_s[:, c0:c0 + csz], shst[0:16, 0, 0, 0:csz], reads=[OSB])
        S.dma("sp", g.o_shift_p.rearrange("(o n) -> o n", o=1)[:, c0:c0 + csz], shst[0:1, 1, 0, 0:csz], reads=[OSB])
        S.op("dve", lambda e, lz=lz, dd=dd, csz=csz: e.tensor_tensor(dd[0:csz, 0:2048], lz[0:csz, 0:2048], lz[0:csz, 1:2049], ALU.subtract), reads=[LZ], writes=[DD])
        S.op("dve", lambda e, lzs=lzs, dd=dd, csz=csz: e.tensor_tensor(dd[0:csz, 2048:NT].rearrange("p (s t) -> p s t", t=8), lzs[0:csz, :, 0:8], lzs[0:csz, :, 1:9], ALU.subtract), reads=[LZ], writes=[DD])
        S.op("dve", lambda e, lz=lz, dd=dd, csz=csz, j=j: e.scalar_tensor_tensor(dd[0:csz, 0:2048], dd[0:csz, 0:2048], mu[0:csz, 48 + j:49 + j], lz[0:csz, 1:2049], ALU.mult, ALU.add), reads=[LZ, DD, CB], writes=[DD])
        S.op("dve", lambda e, lzs=lzs, dd=dd, csz=csz, j=j: e.scalar_tensor_tensor(dd[0:csz, 2048:NT].rearrange("p (s t) -> p s t", t=8), dd[0:csz, 2048:NT].rearrange("p (s t) -> p s t", t=8), mu[0:csz, 48 + j:49 + j], lzs[0:csz, :, 1:9], ALU.mult, ALU.add), reads=[LZ, DD, CB], writes=[DD])
        S.op("act", lambda e, dst=dst, dd=dd, csz=csz, func=func: e.activation(dst[0:csz, :], dd[0:csz, :], func), reads=[DD], writes=[LRB])
        A.release(A.mark())
        A.off = mk1
    A.release(mk1)
    if g.stop <= 2:
        return

    N_OWN = OWN
    ybuf = A.alloc([N_OWN], F32)
    YB = Buf("y")
    gbuf = A.alloc([N_OWN], F32)
    GB = Buf("g")
    bonus = A.alloc([N_OWN], F32)
    BNB = Buf("bonus")
    Sf = A.alloc([64], F32)
    SFB = Buf("Sf")
    Sb = [A.alloc([128], BF16) for _ in range(2)]
    SBB = [Buf("Sb0"), Buf("Sb1")]
    S0T = A.alloc([16, 64], F32)
    S0B = Buf("S0T")
    Snew = S0T
    SNB = S0B
    Sb8 = A.alloc([8, 128], BF16)
    SB8 = Buf("Sb8")
    Zp4 = [A.alloc([4, 2, 128], BF16) for _ in range(2)]
    ZPB = [Buf("Zp0"), Buf("Zp1")]
    Wp4 = [A.alloc([4, 2, 128], BF16) for _ in range(2)]
    WPB = [Buf("Wp0"), Buf("Wp1")]
    Vp4 = [A.alloc([4, 2, 64], BF16) for _ in range(2)]
    VPB = [Buf("Vp0"), Buf("Vp1")]
    NaK = [A.alloc([8, 64], BF16) for _ in range(2)]
    M12 = [A.alloc([8, 64], BF16) for _ in range(2)]
    NKB = [Buf("NaK0"), Buf("NaK1")]
    Tinv = [A.alloc([8, 64], BF16) for _ in range(2)]
    TIB = [Buf("Ti0"), Buf("Ti1")]
    Xb = [A.alloc([8, 64], BF16) for _ in range(2)]
    XTb = [A.alloc([8, 64], BF16) for _ in range(2)]
    XB_ = [Buf("X0"), Buf("X1")]
    Pb = [A.alloc([8, 64], BF16) for _ in range(2)]
    PBF = [Buf("P0"), Buf("P1")]
    Bsb = A.alloc([128], BF16)
    BSB = Buf("Bsb")
    Kmask = A.alloc([8, 64], BF16)
    Rmask = A.alloc([8, 64], BF16)
    KMB = Buf("KRmask")
    Wm = A.alloc([2, 8, 64], BF16)
    WMB = Buf("Wm")
    for t, b in [(Sb[0], SBB[0]), (Sb[1], SBB[1]), (Sb8, SB8), (Zp4[0], ZPB[0]), (Zp4[1], ZPB[1]), (Wp4[0], WPB[0]), (Wp4[1], WPB[1]),
                 (Vp4[0], VPB[0]), (Vp4[1], VPB[1])]:
        S.op("dve", lambda e, t=t: e.memset(t, 0.0), writes=[b])
    KR = [A.alloc([8, 2, 64], BF16) for _ in range(2)]
    AKm = [[A.alloc([8, 2, 64], BF16) for _ in range(2)] for _ in range(2)]
    AKd = [A.alloc([8, 2, 64], BF16) for _ in range(2)]
    mx = [A.alloc([3, 64 + 512], F32) for _ in range(2)]
    Gc = [A.alloc([16], F32) for _ in range(2)]
    mvb = [A.alloc([64 + 512], BF16) for _ in range(2)]
    SIB = [Buf("scanin0"), Buf("scanin1")]
    for i in range(2):
        S.op("dve", lambda e, i=i: e.memset(mx[i], 0.0), writes=[SIB[i]])
        S.op("dve", lambda e, i=i: e.memset(mvb[i], 0.0), writes=[SIB[i]])
        for h in range(2):
            S.op("dve", lambda e, i=i, h=h: e.memset(AKm[i][h], 0.0), writes=[SIB[i]])
    zoff = A.off
    zb = A.alloc([3, 513], F32)
    dt_ = A.alloc([3, 512], F32)
    eoff = A.off
    A.off = zoff
    stage = A.alloc([16, 128], F32)
    A.off = eoff
    ZBB = Buf("zb+d+stage")
    DTB = ZBB
    STG = ZBB
    tmp = [A.alloc([512], F32) for _ in range(11)]
    TB_ = [Buf("tmp%d" % i) for i in range(11)]
    oTs = [A.alloc([OWN], BF16) for _ in range(2)]
    OSG = [Buf("oTs0"), Buf("oTs1")]
    sqb = A.alloc([512], BF16)
    SQB = Buf("sq")

    zT3 = g.zT[0:6144, :].rearrange("(j q) t -> q j t", q=2048)
    segs = [(0, 512, "pre"), (512, 512, "pre"), (1024, 512, "own"), (1536, 512, "own"), (2048, 128, "smp")]
    st = {"si": 0, "cur": 0, "hk": 0, "done": 0}

    def do_pair(p):
        prow = slice(p * 128, (p + 1) * 128)
        for sq_ in range(16):
            S.dma("sp", stage[0:64, sq_, :].rearrange("p (h k) -> p h k", h=2),
                  g.st_wkv[sq_, 2 * p:2 * p + 2, :, :].rearrange("h v k -> v h k"), writes=[STG])
        for q in range(2):
            S.group("pe", [lambda e, q=q, s=s: e.transpose(ps[:, 2560 + q * 512 + s * 64:2560 + q * 512 + (s + 1) * 64], stage[0:64, q * 8 + s, :], identf[0:64, 0:64])
                           for s in range(8)], reads=[STG, g.CB], writes=[g.PB[5 + q]])
            S.op("act", lambda e, q=q: e.activation(S0T[:, q * 8:(q + 1) * 8, :], ps[:, 2560 + q * 512:2560 + (q + 1) * 512].rearrange("p (s v) -> p s v", v=64), AF.Copy),
                 reads=[g.PB[5 + q]], writes=[S0B])
        S.op("dve", lambda e: e.memset(Sf, 0.0), writes=[SFB])
        S.op("dve", lambda e: e.memset(Sb[0][0:64, 0:64], 0.0), writes=[SBB[0]])
        S.op("dve", lambda e: e.memset(Sb[0][64:128, 64:128], 0.0), writes=[SBB[0]])
        st["cur"] = 0

        def do_seg(t0, N, kind):
            yield ("WAIT", st["hk"] - 2)
            b = st["si"] % 2
            st["si"] += 1
            ncn = N // 64
            own = kind != "pre"
            smp = kind == "smp"
            oo = t0 - 1024
            if not smp:
                if t0 == 0:
                    S.op("dve", lambda e: e.memset(zb[:, :, 0:1], 0.0), writes=[ZBB])
                    S.dma("sp", zb[:, :, 1:513], zT3[prow, :, 0:512], reads=[g.ZB], writes=[ZBB])
                else:
                    S.dma("sp", zb[:, :, 0:513], zT3[prow, :, t0 - 1:t0 + 512], reads=[g.ZB], writes=[ZBB])
                zprev = zb[:, :, 0:512]
                zcur = zb[:, :, 1:513]
                shp3 = lambda ap: ap
            else:
                zv = zb[:, :, 0:144].rearrange("p j (s t) -> p j s t", t=9)
                for j in range(3):
                    S.dma("sp", zv[:, j, :, 1:9], zT3[prow, j, 2048:NT].rearrange("p (s t) -> p s t", t=8), reads=[g.ZB], writes=[ZBB])
                    S.op("dve", lambda e, j=j, zv=zv: e.tensor_copy(zv[:, j, :, 0], stT[:, j * 16 + p, :]), reads=[STB], writes=[ZBB])
                zprev = zv[:, :, :, 0:8]
                zcur = zv[:, :, :, 1:9]
                shp3 = lambda ap: ap.rearrange("p j (s t) -> p j s t", t=8)
            if smp or t0 == 1536:
                pbk = 7
                if smp:
                    fns = [lambda e, j=j: e.transpose(ps[0:16, 3584 + j * 128:3584 + (j + 1) * 128], zv[:, j, :, 8], identf) for j in range(3)]
                    S.group("pe", fns, reads=[ZBB, g.CB], writes=[g.PB[7]])
                    for j in range(3):
                        S.op("act", lambda e, j=j: e.activation(shst[0:16, 0, j, :], ps[0:16, 3584 + j * 128:3584 + (j + 1) * 128], AF.Copy),
                             reads=[g.PB[7]], writes=[OSB])
                        S.dma("sp", g.o_shift_s[:, j * 2048 + p * 128:j * 2048 + (p + 1) * 128], shst[0:16, 0, j, :], reads=[OSB])
                else:
                    fns = [lambda e, j=j: e.transpose(ps[0:1, 3584 + j * 128:3584 + (j + 1) * 128], zb[:, j, 512:513], identf) for j in range(3)]
                    S.group("pe", fns, reads=[ZBB, g.CB], writes=[g.PB[7]])
                    for j in range(3):
                        S.op("act", lambda e, j=j: e.activation(shst[0:1, 1, j, :], ps[0:1, 3584 + j * 128:3584 + (j + 1) * 128], AF.Copy),
                             reads=[g.PB[7]], writes=[OSB])
                        S.dma("sp", g.o_shift_p.rearrange("(o n) -> o n", o=1)[:, j * 2048 + p * 128:j * 2048 + (p + 1) * 128], shst[0:1, 1, j, :], reads=[OSB])
            dv = shp3(dt_[:, :, 0:N])
            mv3 = shp3(mx[b][:, :, 64:64 + N])
            mu3 = mu[:, p:48:16]
            S.op("dve", lambda e, dv=dv, zprev=zprev, zcur=zcur: e.tensor_tensor(dv, zprev, zcur, ALU.subtract), reads=[ZBB], writes=[DTB])
            if smp:
                mub = mu3.unsqueeze(2).unsqueeze(3).to_broadcast([128, 3, 16, 8])
            else:
                mub = mu3.unsqueeze(2).to_broadcast([128, 3, N])
            S.op("dve", lambda e, dv=dv, mub=mub: e.tensor_tensor(dv, dv, mub, ALU.mult), reads=[DTB, CB], writes=[DTB])
            S.op("dve", lambda e, dv=dv, mv3=mv3, zcur=zcur: e.tensor_tensor(mv3, dv, zcur, ALU.add), reads=[DTB, ZBB], writes=[SIB[b]])
            S.op("act", lambda e: e.activation(mvb[b][:, 64:64 + N], mx[b][:, 2, 64:64 + N], AF.Copy), reads=[SIB[b]], writes=[SIB[b]])
            mr = mx[b][:, 0, 64:64 + N]
            mk_ = mx[b][:, 1, 64:64 + N]
            mv_ = mx[b][:, 2, 64:64 + N]
            T_ = lambda i: tmp[i][:, 0:N]
            yield 1
            pw = ps[:, 3584:3584 + N]
            S.op("pe", lambda e, pw=pw: e.matmul(pw, wdu[0:96, prow], tw[0:96, t0:t0 + N], start=True, stop=True), reads=[LRB, CB], writes=[g.PB[7]])
            S.op("act", lambda e, pw=pw: e.activation(T_(0), pw, AF.Sigmoid, bias=w0[:, p:p + 1]), reads=[g.PB[7], CB], writes=[TB_[0]])
            S.op("pe", lambda e, pw=pw: e.matmul(pw, wau[0:96, prow], adm[0:96, t0:t0 + N], start=True, stop=True), reads=[LRB, CB], writes=[g.PB[7]])
            S.op("act", lambda e, pw=pw: e.activation(T_(1), pw, AF.Sigmoid, bias=a0[:, p:p + 1]), reads=[g.PB[7], CB], writes=[TB_[1]])
            if own:
                S.group("pe", [lambda e, pw=pw, c=c: e.matmul(pw, wgu[:, c, prow], sg[:, c, t0:t0 + N], start=(c == 0), stop=(c == 1)) for c in range(2)],
                        reads=[LRB, CB], writes=[g.PB[7]])
                S.op("act", lambda e, pw=pw: e.activation(gbuf[:, oo:oo + N], pw, AF.Copy), reads=[g.PB[7]], writes=[GB])
            yield 1
            smk = scanmask[:, 512:640] if smp else scanmask[:, 0:N]
            S.op("dve", lambda e, smk=smk: e.tensor_tensor_scan(T_(2), smk, T_(0), 0.0, ALU.mult, ALU.add), reads=[TB_[0], CB], writes=[TB_[2]])
            cl_ = 8 if smp else 64
            nseg = N // cl_
            cs3 = T_(2).rearrange("p (c t) -> p c t", t=cl_)
            S.op("dve", lambda e, cs3=cs3, cl_=cl_, nseg=nseg: e.tensor_tensor(T_(3).rearrange("p (c t) -> p c t", t=cl_), cs3[:, :, cl_ - 1:cl_].to_broadcast([128, nseg, cl_]), cs3, ALU.subtract),
                 reads=[TB_[2]], writes=[TB_[3]])
            S.op("pool", lambda e: e.tensor_tensor(T_(4), T_(2), T_(0), ALU.subtract), reads=[TB_[2], TB_[0]], writes=[TB_[4]])
            S.op("act", lambda e: e.activation(T_(5), T_(2), AF.Exp, scale=-C0), reads=[TB_[2]], writes=[TB_[5]])
            S.op("act", lambda e: e.activation(T_(6), T_(2), AF.Exp, scale=C0), reads=[TB_[2]], writes=[TB_[6]])
            S.op("act", lambda e: e.activation(T_(4), T_(4), AF.Exp, scale=-C0), reads=[TB_[4]], writes=[TB_[4]])
            S.op("act", lambda e: e.activation(T_(3), T_(3), AF.Exp, scale=-C0), reads=[TB_[3]], writes=[TB_[3]])
            E5 = T_(5).rearrange("p (c t) -> p c t", t=cl_)
            S.op("pool", lambda e, E5=E5, b=b, nseg=nseg, cl_=cl_: e.tensor_copy(Gc[b][:, 0:nseg], E5[:, :, cl_ - 1]), reads=[TB_[5]], writes=[SIB[b]])
            yield 1
            S.op("pool", lambda e: e.tensor_scalar(T_(7), mk_, k_k[:, p:p + 1], None, ALU.mult), reads=[SIB[b], CB], writes=[TB_[7]])
            S.op("pool", lambda e: e.tensor_tensor(sqb[:, 0:N], T_(7), T_(7), ALU.mult), reads=[TB_[7]], writes=[SQB])
            S.op("pe", lambda e, pw=pw: e.matmul(pw, bones, sqb[:, 0:N], start=True, stop=True), reads=[SQB, g.CB], writes=[g.PB[7]])
            S.op("act", lambda e, pw=pw: e.activation(T_(8), pw, AF.Sqrt), reads=[g.PB[7]], writes=[TB_[8]])
            S.op("dve", lambda e: e.tensor_scalar(T_(8), T_(8), 1e-12, None, ALU.max), reads=[TB_[8]], writes=[TB_[8]])
            S.op("dve", lambda e: e.reciprocal(T_(8), T_(8)), reads=[TB_[8]], writes=[TB_[8]])
            S.op("pool", lambda e: e.tensor_tensor(T_(7), T_(7), T_(8), ALU.mult), reads=[TB_[7], TB_[8]], writes=[TB_[7]])
            S.op("pool", lambda e: e.tensor_scalar(T_(9), T_(1), k_a[:, p:p + 1], omka[:, p:p + 1], ALU.mult, ALU.add), reads=[TB_[1], CB], writes=[TB_[9]])
            S.op("pool", lambda e: e.tensor_tensor(T_(9), T_(9), mk_, ALU.mult), reads=[TB_[9], SIB[b]], writes=[TB_[9]])
            S.op("pool", lambda e: e.tensor_tensor(T_(10), T_(7), T_(1), ALU.mult), reads=[TB_[7], TB_[1]], writes=[TB_[10]])
            yield 1
            c64 = lambda ap: ap.rearrange("p (c t) -> p c t", t=64)
            S.op("pool", lambda e, b=b, ncn=ncn: e.tensor_tensor(KR[b][:, 0:ncn, 0, :], c64(T_(7)), c64(T_(4)), ALU.mult), reads=[TB_[7], TB_[4]], writes=[SIB[b]])
            S.op("pool", lambda e, b=b, ncn=ncn: e.tensor_tensor(KR[b][:, 0:ncn, 1, :], c64(mr), c64(T_(5)), ALU.mult), reads=[TB_[5]], writes=[SIB[b]])
            for h in range(2):
                hs = slice(h * 64, (h + 1) * 64)
                S.op("pool", lambda e, b=b, h=h, hs=hs, ncn=ncn: e.tensor_tensor(AKm[b][h][hs, 0:ncn, 0, :], c64(T_(10))[hs], c64(T_(6))[hs], ALU.mult),
                     reads=[TB_[10], TB_[6]], writes=[SIB[b]])
                S.op("pool", lambda e, b=b, h=h, hs=hs, ncn=ncn: e.tensor_tensor(AKm[b][h][hs, 0:ncn, 1, :], c64(T_(9))[hs], c64(T_(6))[hs], ALU.mult),
                     reads=[TB_[9], TB_[6]], writes=[SIB[b]])
            S.op("pool", lambda e, b=b, ncn=ncn: e.tensor_tensor(AKd[b][:, 0:ncn, 0, :], c64(T_(10)), c64(T_(3)), ALU.mult), reads=[TB_[10], TB_[3]], writes=[SIB[b]])
            S.op("pool", lambda e, b=b, ncn=ncn: e.tensor_tensor(AKd[b][:, 0:ncn, 1, :], c64(T_(9)), c64(T_(3)), ALU.mult), reads=[TB_[9], TB_[3]], writes=[SIB[b]])
            if own:
                S.op("dve", lambda e: e.scalar_tensor_tensor(sqb[:, 0:N], mr, r_k[:, p:p + 1], T_(9), ALU.mult, ALU.mult), reads=[SIB[b], TB_[9], CB], writes=[SQB])
                S.op("pe", lambda e, pw=pw: e.matmul(pw, bones, sqb[:, 0:N], start=True, stop=True), reads=[SQB, g.CB], writes=[g.PB[7]])
                S.op("dve", lambda e, pw=pw: e.tensor_tensor(bonus[:, oo:oo + N], pw, mv_, ALU.mult), reads=[g.PB[7], SIB[b]], writes=[BNB])
            yield 1
            mo = 4 if smp else 0
            nlev = 3 if smp else 5
            def do_half(hsx):
                cc0 = hsx * 4
                nch = min(4, ncn - cc0)
                nm = nch * 2
                yield ("WAIT", st["hk"] - 1)
                hb = st["hk"] % 2
                st["hk"] += 1
                last_half = hsx == (ncn + 3) // 4 - 1
                fns = []
                for c in range(nch):
                    for h in range(2):
                        mi = c * 2 + h
                        fns.append(lambda e, c=c, h=h, mi=mi: e.matmul(ps[:, mi * 128:(mi + 1) * 128], AKm[b][h][:, cc0 + c, :, :].rearrange("p a t -> p (a t)"),
                                                                       KR[b][:, cc0 + c, :, :].rearrange("p a t -> p (a t)"), start=True, stop=True))
                S.group("pe", fns, reads=[SIB[b]], writes=[g.PB[0], g.PB[1]])
                fns = []
                for c in range(nch):
                    for h in range(2):
                        mi = c * 2 + h
                        fns.append(lambda e, c=c, h=h, mi=mi: e.matmul(ps[0:64, 1024 + mi * 64:1024 + (mi + 1) * 64], KR[b][:, cc0 + c, 0, :], AKm[b][h][:, cc0 + c, 0, :], start=True, stop=True))
                S.group("pe", fns, reads=[SIB[b]], writes=[g.PB[2]])
                P1v = ps[:, 0:nm * 128].rearrange("p (m t) -> p m t", t=128)
                mb = lambda k, rows=slice(0, 128): masks[rows, mo + k:mo + k + 1, :].to_broadcast([rows.stop - rows.start, nm, 64])
                S.op("dve", lambda e, P1v=P1v, hb=hb, nm=nm: e.tensor_tensor(NaK[hb][:, 0:nm, :], P1v[:, :, 0:64], mb(0), ALU.mult), reads=[g.PB[0], g.PB[1], g.CB], writes=[NKB[hb]])
                S.op("dve", lambda e, P1v=P1v, hb=hb, nm=nm: e.tensor_tensor(M12[hb][:, 0:nm, :], P1v[:, :, 64:128], mb(1), ALU.mult), reads=[g.PB[0], g.PB[1], g.CB], writes=[NKB[hb]])
                S.op("dve", lambda e, P1v=P1v, nm=nm: e.tensor_tensor(Xb[0][0:64, 0:nm, :], P1v[0:64, :, 0:64], mb(2, slice(0, 64)), ALU.mult), reads=[g.PB[0], g.PB[1], g.CB], writes=[XB_[0]])
                S.op("dve", lambda e, nm=nm: e.tensor_tensor(XTb[0][0:64, 0:nm, :], ps[0:64, 1024:1024 + nm * 64].rearrange("p (m t) -> p m t", t=64), mb(3, slice(0, 64)), ALU.mult),
                     reads=[g.PB[2], g.CB], writes=[XB_[0]])
                S.op("dve", lambda e, nm=nm: e.tensor_tensor(Pb[0][0:64, 0:nm, :], Xb[0][0:64, 0:nm, :], identb[0:64, 0:64].unsqueeze(1).to_broadcast([64, nm, 64]), ALU.add),
                     reads=[XB_[0], g.CB], writes=[PBF[0]])
                yield 1
                xc, pc = 0, 0
                for lv in range(nlev):
                    last = lv == nlev - 1
                    xn = 1 - xc
                    fns = []
                    for mi in range(nm):
                        if not last:
                            fns.append(lambda e, mi=mi, xc=xc: e.matmul(ps[0:64, mi * 64:(mi + 1) * 64], XTb[xc][0:64, mi, :], Xb[xc][0:64, mi, :], start=True, stop=True))
                        fns.append(lambda e, mi=mi, xc=xc: e.matmul(ps[0:64, 512 + mi * 64:512 + (mi + 1) * 64], Xb[xc][0:64, mi, :], XTb[xc][0:64, mi, :], start=True, stop=True))
                    S.group("pe", fns, reads=[XB_[xc]], writes=[g.PB[0], g.PB[1]])
                    if not last:
                        S.op("act", lambda e, xn=xn, nm=nm: e.activation(Xb[xn][0:64, 0:nm, :], ps[0:64, 0:nm * 64].rearrange("p (m t) -> p m t", t=64), AF.Copy), reads=[g.PB[0]], writes=[XB_[xn]])
                    S.op("dve", lambda e, xn=xn, nm=nm: e.tensor_copy(XTb[xn][0:64, 0:nm, :], ps[0:64, 512:512 + nm * 64].rearrange("p (m t) -> p m t", t=64)), reads=[g.PB[1]], writes=[XB_[xn]])
                    pn = 1 - pc
                    S.group("pe", [lambda e, mi=mi, xn=xn, pc=pc: e.matmul(ps[0:64, 1024 + mi * 64:1024 + (mi + 1) * 64], XTb[xn][0:64, mi, :], Pb[pc][0:64, mi, :], start=True, stop=True)
                                   for mi in range(nm)], reads=[XB_[xn], PBF[pc]], writes=[g.PB[2]])
                    dst = Tinv[hb] if last else Pb[pn]
                    dstB = TIB[hb] if last else PBF[pn]
                    S.op("dve", lambda e, dst=dst, pc=pc, nm=nm: e.tensor_tensor(dst[0:64, 0:nm, :], ps[0:64, 1024:1024 + nm * 64].rearrange("p (m t) -> p m t", t=64), Pb[pc][0:64, 0:nm, :], ALU.add),
                         reads=[g.PB[2], PBF[pc]], writes=[dstB])
                    xc, pc = xn, pn
                    yield 1
                pz = bank(g, 4, BF16)
                S.group("pe", [lambda e, c=c, pz=pz: e.transpose(pz[:, c * 128:(c + 1) * 128], AKd[b][:, cc0 + c, :, :].rearrange("p a t -> p (a t)"), identb) for c in range(nch)],
                        reads=[SIB[b], g.CB], writes=[g.PB[4]])
                zdst = lambda t, rows: bass.AP(t.tensor, t[rows, 0, 0, 0:1].offset, [list(t[rows, 0, 0, 0:1].ap[0]), [256, nch], [192, 2], [1, 64]])
                S.op("act", lambda e, pz=pz, hb=hb, nch=nch: e.activation(zdst(Zp4[hb], slice(0, 128)), pz[:, 0:nch * 128].rearrange("p (c h k) -> p c h k", h=2, k=64), AF.Copy),
                     reads=[g.PB[4]], writes=[ZPB[hb]])
                pz5 = bank(g, 5, BF16)
                S.group("pe", [lambda e, c=c, pz5=pz5: e.transpose(pz5[:, c * 128:(c + 1) * 128], mvb[b][:, (cc0 + c) * 64:(cc0 + c) * 64 + 128], identb) for c in range(nch)],
                        reads=[SIB[b], g.CB], writes=[g.PB[5]])
                pv5 = pz5[64:128, 0:nch * 128].rearrange("p (c h k) -> p c h k", h=2, k=64)
                S.op("act", lambda e, pv5=pv5, hb=hb: e.activation(zdst(Wp4[hb], slice(64, 128)), pv5, AF.Copy), reads=[g.PB[5]], writes=[WPB[hb]])
                S.op("act", lambda e, pv5=pv5, hb=hb, nch=nch: e.activation(Vp4[hb][64:128, 0:nch, :, :], pv5, AF.Copy), reads=[g.PB[5]], writes=[VPB[hb]])
                yield 1
                def do_chunk(c):
                    cur = st["cur"]
                    cg = cc0 + c
                    nxt = 1 - cur
                    pB = ps[0:64, 1536:1664]
                    if not smp:
                        fns = [lambda e, cg=cg, cur=cur, pB=pB: e.matmul(pB, KR[b][:, cg, 0, :], Sb[cur], start=True, stop=False)]
                        rd = [SIB[b], SBB[cur], NKB[hb], VPB[hb]]
                    else:
                        S.op("dve", lambda e, cg=cg: e.tensor_copy(Sb8[0:64, :, 0:64], S0T[0:64, cg * 8:(cg + 1) * 8, :]), reads=[S0B], writes=[SB8])
                        S.op("dve", lambda e, cg=cg: e.tensor_copy(Sb8[64:128, :, 64:128], S0T[64:128, cg * 8:(cg + 1) * 8, :]), reads=[S0B], writes=[SB8])
                        S.op("dve", lambda e, cg=cg: e.tensor_tensor(Kmask, KR[b][:, cg, 0:1, :].to_broadcast([128, 8, 64]), seqmask, ALU.mult), reads=[SIB[b], CB], writes=[KMB])
                        S.op("dve", lambda e, cg=cg: e.tensor_tensor(Rmask, KR[b][:, cg, 1:2, :].to_broadcast([128, 8, 64]), seqmask, ALU.mult), reads=[SIB[b], CB], writes=[KMB])
                        fns = [lambda e, s=s, pB=pB: e.matmul(pB, Kmask[:, s, :], Sb8[:, s, :], start=(s == 0), stop=False) for s in range(8)]
                        rd = [KMB, SB8, NKB[hb], VPB[hb]]
                    for h in range(2):
                        fns.append(lambda e, h=h, c=c, hb=hb: e.matmul(ps[0:64, 1536 + h * 64:1536 + (h + 1) * 64], NaK[hb][:, c * 2 + h, :], Vp4[hb][:, c, h, :], start=False, stop=True))
                    S.group("pe", fns, reads=rd, writes=[g.PB[3]])
                    S.op("act", lambda e, pB=pB: e.activation(Bsb[0:64, :], pB, AF.Copy), reads=[g.PB[3]], writes=[BSB])
                    yield 1
                    S.group("pe", [lambda e, h=h, c=c, hb=hb: e.matmul(ps[0:64, 1664 + h * 64:1664 + (h + 1) * 64], Tinv[hb][0:64, c * 2 + h, :], Bsb[0:64, h * 64:(h + 1) * 64], start=True, stop=True)
                                   for h in range(2)], reads=[TIB[hb], BSB], writes=[g.PB[3]])
                    wrow = Wp4[hb][0:64, c, 0, 0:1]
                    wdst = bass.AP(wrow.tensor, wrow.offset, [list(wrow.ap[0]), [192, 2], [1, 64]])
                    yield 1
                    S.op("dve", lambda e, wdst=wdst: e.tensor_scalar(wdst, ps[0:64, 1664:1792].rearrange("p (h v) -> p h v", h=2), -1.0, None, ALU.mult), reads=[g.PB[3]], writes=[WPB[hb]])
                    if not smp:
                        S.group("pe", [lambda e, h=h, c=c, hb=hb: e.matmul(ps[:, 1792:1856], Zp4[hb][:, c, h, :], Wp4[hb][:, c, h, h * 64:(h + 1) * 64], start=(h == 0), stop=(h == 1)) for h in range(2)],
                                reads=[ZPB[hb], WPB[hb]], writes=[g.PB[3]])
                    else:
                        for h in range(2):
                            S.op("dve", lambda e, h=h, c=c, hb=hb: e.tensor_tensor(Wm[:, h, :, :], Wp4[hb][:, c, h, h * 64:(h + 1) * 64].unsqueeze(1).to_broadcast([128, 8, 64]),
                                                                                   tokmask.unsqueeze(2).to_broadcast([128, 8, 64]), ALU.mult), reads=[WPB[hb], CB], writes=[WMB])
                        S.group("pe", [lambda e, h=h, c=c, hb=hb: e.matmul(ps[:, 3584:4096], Zp4[hb][:, c, h, :], Wm[:, h, :, :].rearrange("p s v -> p (s v)"), start=(h == 0), stop=(h == 1)) for h in range(2)],
                                reads=[ZPB[hb], WMB], writes=[g.PB[7]])
                    yield 1
                    if own:
                        ycol = 3072 + cg * 64
                        if not smp:
                            fns = [lambda e, cg=cg, cur=cur, ycol=ycol: e.matmul(ps[:, ycol:ycol + 64], Sb[cur], KR[b][:, cg, 1, :], start=True, stop=False)]
                            rd = [SIB[b], SBB[cur], NKB[hb], WPB[hb]]
                        else:
                            fns = [lambda e, s=s, ycol=ycol: e.matmul(ps[:, ycol:ycol + 64], Sb8[:, s, :], Rmask[:, s, :], start=(s == 0), stop=False) for s in range(8)]
                            rd = [KMB, SB8, NKB[hb], WPB[hb]]
                        for h in range(2):
                            fns.append(lambda e, h=h, c=c, hb=hb, ycol=ycol: e.matmul(ps[:, ycol:ycol + 64], Wp4[hb][:, c, h, :], M12[hb][:, c * 2 + h, :], start=False, stop=(h == 1)))
                        S.group("pe", fns, reads=rd, writes=[g.PB[6]])
                    if not smp:
                        S.op("dve", lambda e, cg=cg, b=b: e.scalar_tensor_tensor(Sf, Sf, Gc[b][:, cg:cg + 1], ps[:, 1792:1856], ALU.mult, ALU.add), reads=[SFB, SIB[b], g.PB[3]], writes=[SFB])
                        S.op("act", lambda e, nxt=nxt: e.activation(Sb[nxt][0:64, 0:64], Sf[0:64, :], AF.Copy), reads=[SFB], writes=[SBB[nxt]])
                        S.op("act", lambda e, nxt=nxt: e.activation(Sb[nxt][64:128, 64:128], Sf[64:128, :], AF.Copy), reads=[SFB], writes=[SBB[nxt]])
                        st["cur"] = nxt
                    else:
                        S.op("dve", lambda e, cg=cg, b=b: e.tensor_tensor(Snew[:, cg * 8:(cg + 1) * 8, :], S0T[:, cg * 8:(cg + 1) * 8, :], Gc[b][:, cg * 8:(cg + 1) * 8].unsqueeze(2).to_broadcast([128, 8, 64]), ALU.mult),
                             reads=[S0B, SIB[b]], writes=[SNB])
                        S.op("dve", lambda e, cg=cg: e.tensor_tensor(Snew[:, cg * 8:(cg + 1) * 8, :], Snew[:, cg * 8:(cg + 1) * 8, :], ps[:, 3584:4096].rearrange("p (s v) -> p s v", v=64), ALU.add),
                             reads=[SNB, g.PB[7]], writes=[SNB])
                def seq_half():
                    for c_ in range(nch):
                        yield from do_chunk(c_)
                        yield 1
                    if own and last_half:
                        S.op("act", lambda e, oo=oo, N=N: e.activation(ybuf[:, oo:oo + N], ps[:, 3072:3072 + N], AF.Copy), reads=[g.PB[6]], writes=[YB])
                    st["done"] += 1
                seq_q.append(seq_half())
            for hsx_ in range((ncn + 3) // 4):
                yield from do_half(hsx_)

        seq_q = []

        def stream_b():
            for (t0_, N_, kind_) in segs:
                yield from do_seg(t0_, N_, kind_)

        B = stream_b()
        b_done = False
        b_wait = None
        a_cur = None
        while True:
            prog = False
            if a_cur is None and seq_q:
                a_cur = seq_q.pop(0)
            if a_cur is not None:
                for _ in range(3):
                    try:
                        next(a_cur)
                    except StopIteration:
                        a_cur = None
                        break
                prog = True
            if not b_done:
                if b_wait is not None and st["done"] >= b_wait:
                    b_wait = None
                if b_wait is None:
                    try:
                        r_ = next(B)
                        if isinstance(r_, tuple):
                            b_wait = r_[1]
                    except StopIteration:
                        b_done = True
                    prog = True
            if b_done and a_cur is None and not seq_q:
                break
            assert prog, "scheduler deadlock"
        S.op("pe", lambda e: e.transpose(ps[0:64, 2560:2688], Sf, identf), reads=[SFB, g.CB], writes=[g.PB[5]])
        S.op("act", lambda e: e.activation(stage[0:64, 0, :], ps[0:64, 2560:2688], AF.Copy), reads=[g.PB[5]], writes=[STG])
        S.dma("sp", g.o_wkv_p[2 * p:2 * p + 2, :, :].rearrange("h v k -> v h k"), stage[0:64, 0, :].rearrange("p (h k) -> p h k", h=2), reads=[STG])
        for q4 in range(4):
            pbq = 5 + q4 % 2
            S.group("pe", [lambda e, q4=q4, s=s, pbq=pbq: e.transpose(ps[0:64, pbq * 512 + s * 128:pbq * 512 + (s + 1) * 128], Snew[:, q4 * 4 + s, :], identf) for s in range(4)],
                    reads=[SNB, g.CB], writes=[g.PB[pbq]])
            S.op("act", lambda e, q4=q4, pbq=pbq: e.activation(stage[0:64, q4 * 4:(q4 + 1) * 4, :], ps[0:64, pbq * 512:(pbq + 1) * 512].rearrange("p (s k) -> p s k", k=128), AF.Copy),
                 reads=[g.PB[pbq]], writes=[STG])
            for s4 in range(4):
                S.dma("sp", g.o_wkv_s[q4 * 4 + s4, 2 * p:2 * p + 2, :, :].rearrange("h v k -> v h k"),
                      stage[0:64, q4 * 4 + s4, :].rearrange("p (h k) -> p h k", h=2), reads=[STG])
        for (o0, n) in [(0, 512), (512, 512), (1024, 128)]:
            pw = ps[:, 3584:3584 + n]
            yv = ybuf[:, o0:o0 + n]
            S.op("dve", lambda e, yv=yv, n=n: e.tensor_copy(sqb[:, 0:n], yv), reads=[YB], writes=[SQB])
            S.op("pe", lambda e, pw=pw, n=n: e.matmul(pw, bones, sqb[:, 0:n], start=True, stop=True), reads=[SQB, g.CB], writes=[g.PB[7]])
            S.op("dve", lambda e, pw=pw, yv=yv, n=n: e.scalar_tensor_tensor(tmp[0][:, 0:n], pw, -1.0 / 64, yv, ALU.mult, ALU.add), reads=[g.PB[7], YB], writes=[TB_[0]])
            S.op("act", lambda e, n=n: e.activation(sqb[:, 0:n], tmp[0][:, 0:n], AF.Square), reads=[TB_[0]], writes=[SQB])
            S.op("pe", lambda e, pw=pw, n=n: e.matmul(pw, bones, sqb[:, 0:n], start=True, stop=True), reads=[SQB, g.CB], writes=[g.PB[7]])
            S.op("act", lambda e, pw=pw, n=n: e.activation(tmp[1][:, 0:n], pw, AF.Sqrt, scale=1.0 / 64, bias=64e-5), reads=[g.PB[7]], writes=[TB_[1]])
            S.op("dve", lambda e, n=n: e.reciprocal(tmp[1][:, 0:n], tmp[1][:, 0:n]), reads=[TB_[1]], writes=[TB_[1]])
            S.op("dve", lambda e, n=n: e.tensor_tensor(tmp[0][:, 0:n], tmp[0][:, 0:n], tmp[1][:, 0:n], ALU.mult), reads=[TB_[0], TB_[1]], writes=[TB_[0]])
            S.op("dve", lambda e, n=n: e.tensor_scalar(tmp[0][:, 0:n], tmp[0][:, 0:n], lnx_g[:, p:p + 1], lnx_b[:, p:p + 1], ALU.mult, ALU.add), reads=[TB_[0], CB], writes=[TB_[0]])
            S.op("dve", lambda e, n=n, o0=o0: e.tensor_tensor(tmp[0][:, 0:n], tmp[0][:, 0:n], bonus[:, o0:o0 + n], ALU.add), reads=[TB_[0], BNB], writes=[TB_[0]])
            S.op("dve", lambda e, n=n, o0=o0: e.tensor_tensor(oTs[p % 2][:, o0:o0 + n], tmp[0][:, 0:n], gbuf[:, o0:o0 + n], ALU.mult), reads=[TB_[0], GB], writes=[OSG[p % 2]])
        S.dma("sp", g.oTd[p * 128:(p + 1) * 128, :], oTs[p % 2], reads=[OSG[p % 2]], writes=[g.OTB])

    for p_ in range(16 if g.stop > 4 else 1):
        do_pair(p_)


def phase_m3(g):
    S, A = g.S, g.A
    ps = g.ps
    identf = g.identf
    mk = A.mark()
    CB = Buf("m3c")
    pscale = A.alloc([16], F32)
    S.dma("sp", pscale, g.pool_scale, writes=[CB])
    spT = A.alloc([16, 240], F32)
    SPB = Buf("spT")
    strow = A.alloc([2, RW], F32)
    SRB = Buf("strow")
    spf = g.st_pool.rearrange("s i c -> (s i) c")
    S.dma("sp", strow[:, 0, :], spf[0:128, :], writes=[SRB])
    S.dma("sp", strow[0:112, 1, :], spf[128:240, :], writes=[SRB])
    for j in range(16):
        pb = j % 4
        S.group("pe", [lambda e, j=j, pb=pb: e.transpose(ps[:, pb * 512:pb * 512 + 128], strow[:, 0, j * 128:(j + 1) * 128], identf),
                       lambda e, j=j, pb=pb: e.transpose(ps[:, pb * 512 + 128:pb * 512 + 240], strow[0:112, 1, j * 128:(j + 1) * 128], identf[0:112, 0:112])],
                reads=[SRB, g.CB], writes=[g.PB[pb]])
        S.op("act", lambda e, j=j, pb=pb: e.activation(spT[:, j, :], ps[:, pb * 512:pb * 512 + 240], AF.Copy), reads=[g.PB[pb]], writes=[SPB])
    S.dma("sp", g.o_pool_s[:, 0:7, :], g.st_pool[:, 8:15, :])
    LA = 15 + 1024
    arr = [A.alloc([LA], F32) for _ in range(3)]
    ar2 = [A.alloc([16, 23], F32) for _ in range(3)]
    ARB = [Buf("arr%d" % i) for i in range(3)]
    invc = A.alloc([OWN], F32)
    IVB = Buf("invc")
    tmpf = A.alloc([OWN], F32)
    TMB = Buf("tmpf")
    dT4 = A.alloc([4, OWN], BF16)
    DTB = Buf("dT4")
    wp = A.alloc([4, 512], BF16)
    WPB = Buf("wp")
    ppst = A.alloc([RW], F32)
    psst = A.alloc([RW], F32)
    PST = Buf("poolstage")
    ost = [A.alloc([OWN], BF16) for _ in range(2)]
    OST = [Buf("ost0"), Buf("ost1")]
    oi = 0
    for j in range(16):
        gi = j // 4
        r0 = 6592 + j * 128
        a0_, a2 = arr[0], ar2[0]
        S.dma("sp", a0_, g.zT[r0:r0 + 128, 1009:2048], reads=[g.ZB], writes=[ARB[0]])
        S.dma("sp", a2[:, :, 15:23], g.zT[r0:r0 + 128, 2048:NT].rearrange("p (s t) -> p s t", t=8), reads=[g.ZB], writes=[ARB[0]])
        S.op("dve", lambda e, a2=a2, j=j: e.tensor_copy(a2[:, :, 0:15], spT[:, j, :].rearrange("p (s i) -> p s i", i=15)), reads=[SPB], writes=[ARB[0]])
        if gi != (j - 1) // 4 or j == 0:
            S.dma("sp", invc, g.c_invcnt[gi:gi + 1, :].partition_broadcast(128) if False else g.c_invcnt[gi, :].partition_broadcast(128), writes=[IVB])
        pb = 4 + j % 2
        S.op("dve", lambda e, a2=a2: e.tensor_copy(tmpf[:, 0:128].rearrange("p (s t) -> p s t", t=8), a2[:, :, 15:23]), reads=[ARB[0]], writes=[TMB])
        S.group("pe", [lambda e, a0_=a0_, pb=pb: e.transpose(ps[0:15, pb * 512:pb * 512 + 128], a0_[:, 1024:1039], identf),
                       lambda e, pb=pb: e.transpose(ps[:, pb * 512 + 128:pb * 512 + 256], tmpf[:, 0:128], identf)],
                reads=[ARB[0], TMB, g.CB], writes=[g.PB[pb]])
        S.op("act", lambda e, j=j, pb=pb: e.activation(ppst[0:15, j * 128:(j + 1) * 128], ps[0:15, pb * 512:pb * 512 + 128], AF.Copy), reads=[g.PB[pb]], writes=[PST])
        S.op("act", lambda e, j=j, pb=pb: e.activation(psst[:, j * 128:(j + 1) * 128], ps[:, pb * 512 + 128:pb * 512 + 256], AF.Copy), reads=[g.PB[pb]], writes=[PST])
        cur = 0
        for k in range(gi + 1):
            sh = 1 << k
            nxt = 1 + (k % 2)
            S.op("dve", lambda e, cur=cur, nxt=nxt, sh=sh: e.tensor_tensor(arr[nxt][:, sh:LA], arr[cur][:, sh:LA], arr[cur][:, 0:LA - sh], ALU.add), reads=[ARB[cur]], writes=[ARB[nxt]])
            S.op("dve", lambda e, cur=cur, nxt=nxt, sh=sh: e.tensor_tensor(ar2[nxt][:, :, sh:23], ar2[cur][:, :, sh:23], ar2[cur][:, :, 0:23 - sh], ALU.add), reads=[ARB[cur]], writes=[ARB[nxt]])
            cur = nxt
        S.op("dve", lambda e, cur=cur: e.tensor_tensor(tmpf[:, 0:1024], arr[cur][:, 15:LA], invc[:, 0:1024], ALU.mult), reads=[ARB[cur], IVB], writes=[TMB])
        S.op("dve", lambda e, cur=cur: e.tensor_tensor(tmpf[:, 1024:OWN].rearrange("p (s t) -> p s t", t=8), ar2[cur][:, :, 15:23], invc[:, 1024:OWN].rearrange("p (s t) -> p s t", t=8), ALU.mult),
             reads=[ARB[cur], IVB], writes=[TMB])
        S.op("dve", lambda e, j=j, a0_=a0_: e.tensor_tensor(dT4[:, j % 4, 0:1024], tmpf[:, 0:1024], a0_[:, 15:LA], ALU.subtract), reads=[TMB, ARB[0]], writes=[DTB])
        S.op("dve", lambda e, j=j, a2=a2: e.tensor_tensor(dT4[:, j % 4, 1024:OWN].rearrange("p (s t) -> p s t", t=8), tmpf[:, 1024:OWN].rearrange("p (s t) -> p s t", t=8), a2[:, :, 15:23], ALU.subtract),
             reads=[TMB, ARB[0]], writes=[DTB])
        if j % 4 == 3:
            S.dma("pool", wp, g.w_pool[gi].rearrange("(c p) e -> p c e", p=128), writes=[WPB])
            for e_ in range(4):
                ob = oi % 2
                oi += 1
                for ti, (t0, tn) in enumerate([(0, 512), (512, 512), (1024, 128)]):
                    pb = ti % 4
                    S.group("pe", [lambda e, cc=cc, e_=e_, t0=t0, tn=tn, pb=pb: e.matmul(ps[:, pb * 512:pb * 512 + tn], wp[:, cc, e_ * 128:(e_ + 1) * 128], dT4[:, cc, t0:t0 + tn],
                                                                                         start=(cc == 0), stop=(cc == 3)) for cc in range(4)],
                            reads=[WPB, DTB], writes=[g.PB[pb]])
                    S.op("act", lambda e, e_=e_, t0=t0, tn=tn, pb=pb, ob=ob, gi=gi: e.activation(ost[ob][:, t0:t0 + tn], ps[:, pb * 512:pb * 512 + tn], AF.Copy, scale=pscale[:, gi * 4 + e_:gi * 4 + e_ + 1]),
                         reads=[g.PB[pb], CB], writes=[OST[ob]])
                rr = 2048 + (gi * 4 + e_) * 128
                S.dma("sp", g.oTd[rr:rr + 128, :], ost[ob], reads=[OST[ob]], writes=[g.OTB])
    S.dma("sp", g.o_pool_p, ppst[0:15, :], reads=[PST])
    for s in range(16):
        S.dma("sp", g.o_pool_s[s, 7:15, :], psst[s * 8:(s + 1) * 8, :], reads=[PST])
    A.release(mk)


def gemm_resid(g, actT, AB, nk, w_ap, x_src_fn, x_dst_fn, XR, XW, col_block=512):
    S, A = g.S, g.A
    ps = g.ps
    mk = A.mark()
    wb = [A.alloc([nk, col_block], BF16) for _ in range(2)]
    WB = [Buf("wo0"), Buf("wo1")]
    xt = [A.alloc([col_block], F32) for _ in range(4)]
    XT = [Buf("xt%d" % i) for i in range(4)]
    w_v = w_ap.rearrange("(kc p) c -> p kc c", p=128)
    ncb = D // col_block
    xi = 0
    for cb in range(ncb):
        b = cb % 2
        cs = slice(cb * col_block, (cb + 1) * col_block)
        step = max(1, nk // 4)
        for q0 in range(0, nk, step):
            S.dma("pool", wb[b][:, q0:q0 + step, :], w_v[:, q0:q0 + step, cs], writes=[WB[b]])
        for i in range(9):
            x = xi % 4
            xi += 1
            pb = xi % 4
            S.dma("sp", xt[x], x_src_fn(i, cs), reads=[XR], writes=[XT[x]])
            S.group("pe", [lambda e, kc=kc, i=i, b=b, pb=pb: e.matmul(ps[:, pb * 512:pb * 512 + col_block], actT[:, kc, i * 128:(i + 1) * 128], wb[b][:, kc, :],
                                                                      start=(kc == 0), stop=(kc == nk - 1)) for kc in range(nk)],
                    reads=[AB, WB[b]], writes=[g.PB[pb]])
            S.op("dve", lambda e, x=x, pb=pb: e.tensor_tensor(xt[x], xt[x], ps[:, pb * 512:pb * 512 + col_block], ALU.add), reads=[XT[x], g.PB[pb]], writes=[XT[x]])
            S.dma("sp", x_dst_fn(i, cs), xt[x], reads=[XT[x]], writes=[Buf("xw")])
    A.release(mk)


def phase_m4(g):
    S, A = g.S, g.A
    mk = A.mark()
    oT = A.alloc([NCH, OWN], BF16)
    OB = Buf("oT")
    ov = g.oTd.rearrange("(kc p) t -> p kc t", p=128)
    for q in range(4):
        S.dma("sp", oT[:, q * 8:(q + 1) * 8, :], ov[:, q * 8:(q + 1) * 8, :], reads=[g.OTB], writes=[OB])
    g.XSB = Buf("xs")
    XIN = Buf("xin")
    gemm_resid(g, oT, OB, NCH, g.w_out, lambda i, cs: g.xin[1024 + i * 128:1024 + (i + 1) * 128, cs],
               lambda i, cs: g.xs[i * 128:(i + 1) * 128, cs], XIN, g.XSB)
    A.release(mk)


def proj_T(g, hT, HB, ntok, w_ap, col_lo, ncols, evac, blk=256):
    S, A = g.S, g.A
    ps = g.ps
    mk = A.mark()
    wb = [A.alloc([NCH, blk], BF16) for _ in range(2)]
    WB = [Buf("pw0"), Buf("pw1")]
    w_v = w_ap.rearrange("(kc p) c -> p kc c", p=128)
    tg = []
    t = 0
    while t < ntok:
        n = min(512, ntok - t)
        tg.append((t, n))
        t += n
    pi = 0
    for bi in range(ncols // blk):
        b = bi % 2
        lo = col_lo + bi * blk
        for q in range(4):
            S.dma("pool", wb[b][:, q * 8:(q + 1) * 8, :], w_v[:, q * 8:(q + 1) * 8, lo:lo + blk], writes=[WB[b]])
        for cj in range(blk // 128):
            j = (bi * blk) // 128 + cj
            for (t0, tn) in tg:
                pb = pi % 4
                pi += 1
                S.group("pe", [lambda e, kc=kc, b=b, cj=cj, t0=t0, tn=tn, pb=pb: e.matmul(ps[:, pb * 512:pb * 512 + tn], wb[b][:, kc, cj * 128:(cj + 1) * 128], hT[:, kc, t0:t0 + tn],
                                                                                     start=(kc == 0), stop=(kc == NCH - 1)) for kc in range(NCH)],
                        reads=[WB[b], HB], writes=[g.PB[pb]])
                evac(j, t0, tn, ps[:, pb * 512:pb * 512 + tn], g.PB[pb])
    A.release(mk)


def phase_x(g):
    S, A = g.S, g.A
    ps = g.ps
    identb = g.identb
    mk = A.mark()
    gv = A.alloc([NCH], F32)
    gm = A.alloc([NCH], F32)
    GV = Buf("gvx")
    S.dma("sp", gv, g.norm_xa_g, writes=[GV])
    S.dma("sp", gm, g.norm_mem_g, writes=[GV])
    qmask = A.alloc([16, 128], BF16)
    S.dma("pool", qmask, g.c_qmask, writes=[GV])
    qT = A.alloc([4, OWN], BF16)
    QB = Buf("qT")
    mkT = A.alloc([4, 256], BF16)
    KB = Buf("mkT")
    mvb = A.alloc([2, 512], BF16)
    VB = Buf("mvb")
    mkh = A.mark()
    hT = A.alloc([NCH, OWN], BF16)
    HB = Buf("hT2")
    xs_t = g.xs.rearrange("(n p) d -> n p d", p=128)
    norm_transpose(g, [xs_t[i] for i in range(9)], 9, gv, GV, hT, HB, 0)
    sc_ = 128.0 ** -0.5

    def ev_q(j, t0, tn, pap, PBf):
        S.op("act", lambda e: e.activation(qT[:, j, t0:t0 + tn], pap, AF.Copy, scale=sc_), reads=[PBf], writes=[QB])
    proj_T(g, hT, HB, OWN, g.w_xq, 0, 512, ev_q)
    A.release(mkh)
    if g.stop <= 7.1:
        return
    mT = A.alloc([NCH, 256], BF16)
    MB = Buf("mT")
    mem_t = g.mem.rearrange("(n p) d -> n p d", p=128)
    norm_transpose(g, [mem_t[i] for i in range(2)], 2, gm, GV, mT, MB, 0)
    if g.stop <= 7.11:
        return

    def ev_k(j, t0, tn, pap, PBf):
        S.op("act", lambda e: e.activation(mkT[:, j, t0:t0 + tn], pap, AF.Copy), reads=[PBf], writes=[KB])
    proj_T(g, mT, MB, 256, g.w_mk, 0, 512, ev_k)
    if g.stop <= 7.12:
        return
    mk2 = A.mark()
    wb = A.alloc([NCH, 512], BF16)
    WBF = Buf("wkv")
    stf = [A.alloc([512], F32) for _ in range(2)]
    SF = [Buf("stf0"), Buf("stf1")]
    for wi, (w_ap, o_ap) in enumerate([(g.w_mk, g.o_mk), (g.w_mv, g.o_mv)]):
        w_v = w_ap.rearrange("(kc p) c -> p kc c", p=128)
        for q in range(4):
            S.dma("pool", wb[:, q * 8:(q + 1) * 8, :], w_v[:, q * 8:(q + 1) * 8, :], writes=[WBF])
        for mt in range(2):
            pb = mt
            S.group("pe", [lambda e, kc=kc, mt=mt, pb=pb: e.matmul(ps[:, pb * 512:(pb + 1) * 512], mT[:, kc, mt * 128:(mt + 1) * 128], wb[:, kc, :], start=(kc == 0), stop=(kc == NCH - 1))
                           for kc in range(NCH)], reads=[MB, WBF], writes=[g.PB[pb]])
            S.op("act", lambda e, mt=mt, pb=pb: e.activation(stf[mt], ps[:, pb * 512:(pb + 1) * 512], AF.Copy), reads=[g.PB[pb]], writes=[SF[mt]])
            if wi == 1:
                S.op("dve", lambda e, mt=mt: e.tensor_copy(mvb[:, mt, :], stf[mt]), reads=[SF[mt]], writes=[VB])
            S.dma("sp", o_ap[mt * 128:(mt + 1) * 128, :], stf[mt], reads=[SF[mt]])
        if g.stop <= 7.13:
            return
    A.release(mkh)
    if g.stop <= 7.2:
        return
    kT = A.alloc([16, 4, 256], BF16)
    KTB = Buf("kT")
    vb = A.alloc([16, 2, 512], BF16)
    VSB = Buf("vb")
    mk3 = A.mark()
    kb = A.alloc([16, 2, 512], BF16)
    KBB = Buf("kb")
    for s in range(16):
        S.dma("pool", kb[:, s, :, :], g.ck[s].rearrange("(c p) e -> p c e", p=128), writes=[KBB])
        S.dma("pool", vb[:, s, :, :], g.cv[s].rearrange("(c p) e -> p c e", p=128), writes=[VSB])
    for s in range(16):
        pb = 4 + s % 2
        pz = bank(g, pb, BF16)
        S.group("pe", [lambda e, s=s, hd=hd, mc=mc, pz=pz: e.transpose(pz[:, (hd * 2 + mc) * 128:(hd * 2 + mc + 1) * 128], kb[:, s, mc, hd * 128:(hd + 1) * 128], identb)
                       for hd in range(4) for mc in range(2)], reads=[KBB, g.CB], writes=[g.PB[pb]])
        S.op("act" if s % 2 == 0 else "dve",
             (lambda e, s=s, pz=pz: e.activation(kT[:, s, :, :], pz.rearrange("p (h m) -> p h m", h=4), AF.Copy)) if s % 2 == 0 else
             (lambda e, s=s, pz=pz: e.tensor_copy(kT[:, s, :, :], pz.rearrange("p (h m) -> p h m", h=4))), reads=[g.PB[pb]], writes=[KTB])
    A.release(mk3)
    if g.stop <= 7.3:
        return
    aoT = A.alloc([4, OWN], BF16)
    AOB = Buf("aoT")
    qm = A.alloc([4, 16, 128], BF16)
    QMB = Buf("qm")
    Pf = A.alloc([4, 256], F32)
    PFB = Buf("Pf")
    Pb_ = A.alloc([4, 256], BF16)
    PBB = Buf("Pb")
    PT = A.alloc([4, 2, 128], BF16)
    PTB = Buf("PT")
    stt = A.alloc([16], F32)
    STB = Buf("stt")
    for i in range(9):
        ts = slice(i * 128, (i + 1) * 128)
        smp = i == 8
        if g.stop <= 7.4 and smp:
            return
        if smp:
            S.op("dve", lambda e: e.tensor_tensor(qm, qT[:, :, 1024:1152].unsqueeze(2).to_broadcast([128, 4, 16, 128]), qmask.unsqueeze(1).to_broadcast([128, 4, 16, 128]), ALU.mult),
                 reads=[QB, GV], writes=[QMB])
        for hd in range(4):
            pb = hd // 2
            o = pb * 512 + (hd % 2) * 256
            if not smp:
                S.op("pe", lambda e, hd=hd, ts=ts, o=o: e.matmul(ps[:, o:o + 256], qT[:, hd, ts], mkT[:, hd, :], start=True, stop=True), reads=[QB, KB], writes=[g.PB[pb]])
            else:
                S.group("pe", [lambda e, hd=hd, s=s, o=o: e.matmul(ps[:, o:o + 256], qm[:, hd, s, :], kT[:, s, hd, :], start=(s == 0), stop=(s == 15)) for s in range(16)],
                        reads=[QMB, KTB], writes=[g.PB[pb]])
        scv = ps[:, 0:1024].rearrange("p (h m) -> p h m", h=4)
        S.op("dve", lambda e, scv=scv: e.tensor_reduce(stt[:, 0:4], scv, AX.X, ALU.max), reads=[g.PB[0], g.PB[1]], writes=[STB])
        S.op("dve", lambda e: e.tensor_scalar(stt[:, 4:8], stt[:, 0:4], -1.0, None, ALU.mult), reads=[STB], writes=[STB])
        for hd in range(4):
            S.op("act", lambda e, hd=hd, scv=scv: e.activation(Pf[:, hd, :], scv[:, hd, :], AF.Exp, bias=stt[:, 4 + hd:5 + hd], accum_out=stt[:, 8 + hd:9 + hd]),
                 reads=[g.PB[0], g.PB[1], STB], writes=[PFB, STB])
        S.op("dve", lambda e: e.reciprocal(stt[:, 12:16], stt[:, 8:12]), reads=[STB], writes=[STB])
        S.op("dve", lambda e: e.tensor_tensor(Pb_, Pf, stt[:, 12:16].unsqueeze(2).to_broadcast([128, 4, 256]), ALU.mult), reads=[PFB, STB], writes=[PBB])
        pz = bank(g, 2, BF16)
        S.group("pe", [lambda e, hd=hd, mc=mc, pz=pz: e.transpose(pz[:, (hd * 2 + mc) * 128:(hd * 2 + mc + 1) * 128], Pb_[:, hd, mc * 128:(mc + 1) * 128], identb)
                       for hd in range(4) for mc in range(2)], reads=[PBB, g.CB], writes=[g.PB[2]])
        S.op("act", lambda e, pz=pz: e.activation(PT, pz.rearrange("p (h c t) -> p h c t", h=4, c=2), AF.Copy), reads=[g.PB[2]], writes=[PTB])
        po = ps[:, 1536:2048]
        if not smp:
            fns = [lambda e, hd=hd, mc=mc: e.matmul(ps[:, 1536 + hd * 128:1536 + (hd + 1) * 128], mvb[:, mc, hd * 128:(hd + 1) * 128], PT[:, hd, mc, :], start=(mc == 0), stop=(mc == 1))
                   for hd in range(4) for mc in range(2)]
            S.group("pe", fns, reads=[VB, PTB], writes=[g.PB[3]])
        else:
            fns = [lambda e, hd=hd, mc=mc, s=s: e.matmul(ps[:, 1536 + hd * 128 + s * 8:1536 + hd * 128 + (s + 1) * 8], vb[:, s, mc, hd * 128:(hd + 1) * 128], PT[:, hd, mc, s * 8:(s + 1) * 8],
                                                         start=(mc == 0), stop=(mc == 1)) for hd in range(4) for s in range(16) for mc in range(2)]
            S.group("pe", fns, reads=[VSB, PTB], writes=[g.PB[3]])
        S.op("act", lambda e, ts=ts, po=po: e.activation(aoT[:, :, ts], po.rearrange("p (h t) -> p h t", h=4), AF.Copy), reads=[g.PB[3]], writes=[AOB])
    if g.stop <= 7.5:
        return
    g.XS2B = Buf("xs2")
    gemm_resid(g, aoT, AOB, 4, g.w_xo, lambda i, cs: g.xs[i * 128:(i + 1) * 128, cs], lambda i, cs: g.xs2[i * 128:(i + 1) * 128, cs], g.XSB, g.XS2B)
    A.release(mk)


def phase_f(g):
    S, A = g.S, g.A
    ps = g.ps
    mk = A.mark()
    gv = A.alloc([NCH], F32)
    GV = Buf("gvf")
    S.dma("sp", gv, g.norm_ffn_g, writes=[GV])
    hT = A.alloc([NCH, OWN], BF16)
    HB = Buf("hT3")
    x2_t = g.xs2.rearrange("(n p) d -> n p d", p=128)
    norm_transpose(g, [x2_t[i] for i in range(9)], 9, gv, GV, hT, HB, 0)
    hid = A.alloc([NCH, OWN], BF16)
    HDB = Buf("hid")
    rl = [A.alloc([512], F32) for _ in range(3)]
    RLB = [Buf("rl%d" % i) for i in range(3)]
    cnt = [0]
    bufs = [(g.xs2, g.XS2B), (g.xs, g.XSB)]
    for q in range(4):
        def ev_h(j, t0, tn, pap, PBf):
            r = cnt[0] % 3
            cnt[0] += 1
            S.op("act", lambda e: e.activation(rl[r][:, 0:tn], pap, AF.Relu), reads=[PBf], writes=[RLB[r]])
            S.op("dve", lambda e: e.tensor_tensor(hid[:, j, t0:t0 + tn], rl[r][:, 0:tn], rl[r][:, 0:tn], ALU.mult), reads=[RLB[r]], writes=[HDB])
        proj_T(g, hT, HB, OWN, g.w_up, q * D, D, ev_h)
        (src, SB_), (dst, DB_) = bufs[q % 2], bufs[(q + 1) % 2]
        gemm_resid(g, hid, HDB, NCH, g.w_down[q * D:(q + 1) * D, :], lambda i, cs, src=src: src[i * 128:(i + 1) * 128, cs],
                   lambda i, cs, dst=dst: dst[i * 128:(i + 1) * 128, cs], SB_, DB_, col_block=256)
    A.release(mk)
    mk = A.mark()
    gb = A.alloc([D], F32)
    GB = Buf("gfinal")
    S.dma("sp", gb, g.norm_final_row.partition_broadcast(128), writes=[GB])
    xt = [A.alloc([D], F32) for _ in range(2)]
    XB = [Buf("fx0"), Buf("fx1")]
    junk = A.alloc([D], BF16)
    JB = Buf("fjunk")
    st = A.alloc([9, 4], F32)
    STB = Buf("fst")
    x_t = g.xs2.rearrange("(n p) d -> n p d", p=128)
    y_t = g.y_out.rearrange("(n p) d -> n p d", p=128)
    for i in range(9):
        b = i % 2
        S.dma("sp", xt[b], x_t[i], reads=[g.XS2B], writes=[XB[b]])
        S.op("act", lambda e, b=b, i=i: e.activation(junk, xt[b], AF.Square, accum_out=st[:, i, 0:1]), reads=[XB[b]], writes=[JB, STB])
        S.op("act", lambda e, i=i: e.activation(st[:, i, 1:2], st[:, i, 0:1], AF.Sqrt, scale=1.0 / D, bias=1e-6), reads=[STB], writes=[STB])
        S.op("dve", lambda e, i=i: e.reciprocal(st[:, i, 2:3], st[:, i, 1:2]), reads=[STB], writes=[STB])
        S.op("dve", lambda e, b=b, i=i: e.scalar_tensor_tensor(xt[b], xt[b], st[:, i, 2:3], gb, ALU.mult, ALU.mult), reads=[XB[b], STB, GB], writes=[XB[b]])
        S.dma("sp", y_t[i], xt[b], reads=[XB[b]])
    A.release(mk)


_CACHE = {}


def kernel(**inputs):
    inp = {k: np.asarray(v) for k, v in inputs.items()}
    if "prog" not in _CACHE:
        _CACHE["prog"] = build_program()
    nc, g = _CACHE["prog"]
    consts = make_consts()
    in_maps = []
    for c in range(8):
        m = make_in_map(inp, c, consts)
        in_maps.append({k: m[k] for k in g.used_inputs})
    res = run_bass_kernel_spmd(nc, in_maps, core_ids=list(range(8)))
    R = res.results
    f32 = np.float32
    y_prompt = np.zeros((4, 2048, D), f32)
    y_sample = np.zeros((128, 8, D), f32)
    wkv_p = np.zeros((1, 4, 32, 64, 64), f32)
    sh_p = np.zeros((1, 4, SHIFT_W), f32)
    pl_p = np.zeros((1, 4, 15, RW), f32)
    mk_p = np.zeros((1, 4, 256, 4, 128), f32)
    mv_p = np.zeros((1, 4, 256, 4, 128), f32)
    wkv_s = np.zeros((1, 128, 32, 64, 64), f32)
    sh_s = np.zeros((1, 128, SHIFT_W), f32)
    pl_s = np.zeros((1, 128, 15, RW), f32)
    for c in range(8):
        b, half = c // 2, c % 2
        r = R[c]
        y = np.asarray(r["y_out"], f32)
        y_prompt[b, half * 1024:(half + 1) * 1024] = y[:1024]
        y_sample[16 * c:16 * c + 16] = y[1024:].reshape(16, 8, D)
        if half == 1:
            wkv_p[0, b] = r["o_wkv_p"]
            sh_p[0, b] = r["o_shift_p"]
            pl_p[0, b] = r["o_pool_p"]
        else:
            mk_p[0, b] = np.asarray(r["o_mk"]).reshape(256, 4, 128)
            mv_p[0, b] = np.asarray(r["o_mv"]).reshape(256, 4, 128)
        wkv_s[0, 16 * c:16 * c + 16] = r["o_wkv_s"]
        sh_s[0, 16 * c:16 * c + 16] = r["o_shift_s"]
        pl_s[0, 16 * c:16 * c + 16] = r["o_pool_s"]
    return (y_prompt, y_sample, wkv_p, sh_p, pl_p, mk_p, mv_p, wkv_s, sh_s, pl_s)
```

```python
import contextlib
import math
import numpy as np
import ml_dtypes
import concourse.bass as bass
import concourse.mybir as mybir
from concourse.bass_utils import run_bass_kernel_spmd

F32 = mybir.dt.float32
BF16 = mybir.dt.bfloat16
AF = mybir.ActivationFunctionType
ALU = mybir.AluOpType
AX = mybir.AxisListType

ENGS = ["pe", "act", "dve", "pool", "sp"]
D = 4096
NCH = 32
TA = 896
TB = 1280
NT = TA + TB
OWN = 1152
RW = 2048
SHIFT_W = 6592
IN_W = 8640
C0 = math.exp(-0.5)


class Buf:
    __slots__ = ("name", "w", "r")

    def __init__(self, name=""):
        self.name = name
        self.w = None
        self.r = {}


class Sched:
    def __init__(self, nc, es, ndma=24):
        self.nc = nc
        self.ops = {e: [] for e in ENGS}
        self.cnt = {e: 0 for e in ENGS}
        self.waited = {e: {} for e in ENGS}
        self.esem = {e: es.enter_context(nc.semaphore("s_" + e)) for e in ENGS}
        self.dsem = {}
        self.dnext = {}
        self.ndma = ndma
        for q in ["sp", "pool"]:
            self.dsem[q] = [es.enter_context(nc.semaphore("d_%s_%d" % (q, i))) for i in range(ndma)]
            self.dnext[q] = 0

    def sem_of(self, key):
        if isinstance(key, tuple):
            return self.dsem[key[0]][key[1]]
        return self.esem[key]

    def _deps(self, eng, reads, writes):
        need = {}

        def add(tok, same_ok):
            if tok is None:
                return
            k, v = tok
            if k == eng and (not same_ok or eng == "pe"):
                return
            if need.get(k, 0) < v:
                need[k] = v
        for b in reads:
            add(b.w, True)
        for b in writes:
            add(b.w, True)
            for k, v in b.r.items():
                add((k, v), False)
        for k, v in need.items():
            if self.waited[eng].get(k, 0) < v:
                self.waited[eng][k] = v
                self.ops[eng].append(("wait", k, v))

    def _mark(self, tok, reads, writes):
        k, v = tok
        for b in reads:
            if b.r.get(k, 0) < v:
                b.r[k] = v
        for b in writes:
            b.w = tok
            b.r = {}

    def op(self, eng, fn, reads=(), writes=()):
        return self.group(eng, [fn], reads, writes)

    def group(self, eng, fns, reads=(), writes=()):
        self._deps(eng, reads, writes)
        self.cnt[eng] += 1
        tok = (eng, self.cnt[eng])
        n = len(fns)
        for i, fn in enumerate(fns):
            self.ops[eng].append(("op", fn, i == n - 1))
        self._mark(tok, reads, writes)
        return tok

    def dma(self, q, out_ap, in_ap, reads=(), writes=()):
        self._deps(q, reads, writes)
        i = self.dnext[q]
        self.dnext[q] += 1
        slot = i % self.ndma
        k = i // self.ndma
        key = (q, slot)
        if k > 0 and self.waited[q].get(key, 0) < 16 * k:
            self.waited[q][key] = 16 * k
            self.ops[q].append(("wait", key, 16 * k))
        self.ops[q].append(("dma", out_ap, in_ap, key))
        tok = (key, 16 * (k + 1))
        self._mark(tok, reads, writes)
        return tok

    def _all_last(self):
        last = {}
        for q in ["sp", "pool"]:
            n = self.dnext[q]
            for slot in range(min(n, self.ndma)):
                uses = (n - 1 - slot) // self.ndma + 1
                last[(q, slot)] = 16 * uses
        for e in ["pe", "act", "dve", "pool"]:
            if self.cnt[e] > 0:
                last[e] = self.cnt[e]
        return last

    def barrier(self, engs=ENGS):
        last = self._all_last()
        for e in engs:
            for k, v in last.items():
                if k == e:
                    continue
                if self.waited[e].get(k, 0) < v:
                    self.waited[e][k] = v
                    self.ops[e].append(("wait", k, v))

    def emit(self):
        nc = self.nc
        self.barrier(["sp"])

        def run(name, e):
            sem = self.esem[name]
            for it in self.ops[name]:
                if it[0] == "wait":
                    e.wait_ge(self.sem_of(it[1]), it[2])
                elif it[0] == "op":
                    ins = it[1](e)
                    if it[2]:
                        ins.then_inc(sem, 1)
                else:
                    e.dma_start(out=it[1], in_=it[2]).then_inc(self.sem_of(it[3]), 16)

        with nc.Block() as block:
            @block.sync
            def _(e):
                run("sp", e)

            @block.tensor
            def _(e):
                run("pe", e)

            @block.scalar
            def _(e):
                run("act", e)

            @block.vector
            def _(e):
                run("dve", e)

            @block.gpsimd
            def _(e):
                run("pool", e)


class Arena:
    def __init__(self, nc, es, kbytes):
        self.t = es.enter_context(nc.sbuf_tensor("arena", [128, kbytes * 256], F32))
        self.cap = kbytes * 1024
        self.off = 0

    def alloc(self, shape, dt):
        esz = 2 if dt == BF16 else 4
        n = 1
        for s in shape:
            n *= s
        nb = (n * esz + 63) // 64 * 64
        assert self.off + nb <= self.cap, "SBUF arena overflow %d + %d" % (self.off, nb)
        ap = self.t[:, self.off // 4:(self.off + nb) // 4]
        self.off += nb
        if dt == BF16:
            ap = ap.bitcast(BF16)
        ap = ap[:, 0:n]
        if len(shape) == 2:
            ap = ap.rearrange("p (a b) -> p a b", a=shape[0])
        elif len(shape) == 3:
            ap = ap.rearrange("p (a b c) -> p a b c", a=shape[0], b=shape[1])
        elif len(shape) == 4:
            ap = ap.rearrange("p (a b c d) -> p a b c d", a=shape[0], b=shape[1], c=shape[2])
        return ap

    def mark(self):
        return self.off

    def release(self, m):
        self.S.barrier()
        self.off = m


IN_SHAPES = {
    "xin": [NT, D], "mem": [256, D], "st_wkv": [16, 32, 64, 64], "st_shift": [16, SHIFT_W], "st_pool": [16, 15, RW],
    "ck": [16, 256, 512], "cv": [16, 256, 512], "w_in": [D, IN_W], "w_out": [D, D], "w_pool": [4, 512, 512],
    "w_decay_up": [96, RW], "w_a_up": [96, RW], "w_g_up": [256, RW], "w_xq": [D, 512], "w_mk": [D, 512], "w_mv": [D, 512],
    "w_xo": [512, D], "w_up": [D, 4 * D], "w_down": [4 * D, D],
    "norm_mix_g": [128, 32], "norm_xa_g": [128, 32], "norm_mem_g": [128, 32], "norm_ffn_g": [128, 32], "norm_final_g": [128, 32],
    "mu_shift": [128, 52], "w0": [128, 16], "a0": [128, 16], "k_k": [128, 16], "k_a": [128, 16], "r_k": [128, 16],
    "lnx_g": [128, 16], "lnx_b": [128, 16], "pool_scale": [128, 16],
    "c_identf": [128, 128], "c_identb": [128, 128], "c_masks": [128, 8, 64], "c_bones": [128, 128],
    "norm_final_row": [D], "c_scanmask": [128, 640], "c_invcnt": [4, OWN], "c_seqmask": [128, 8, 64], "c_tokmask": [128, 8], "c_qmask": [128, 16, 128],
}


class K:
    def __getattr__(self, name):
        if name in IN_SHAPES:
            ap = self.nc.dram_tensor(name, list(IN_SHAPES[name]), F32, kind="ExternalInput").ap()
            self.used_inputs.append(name)
            setattr(self, name, ap)
            return ap
        raise AttributeError(name)


def build_program(dbg=None, stop=99):
    nc = bass.Bass("TRN2", target_bir_lowering=False)
    g = K()
    g.nc = nc
    g.used_inputs = []
    g.stop = stop
    I = lambda name, shape, dt=F32: nc.dram_tensor(name, list(shape), dt, kind="ExternalInput").ap()
    O = lambda name, shape, dt=F32: nc.dram_tensor(name, list(shape), dt, kind="ExternalOutput").ap()
    T = lambda name, shape, dt=F32: nc.dram_tensor(name, list(shape), dt).ap()
    g.y_out = O("y_out", [OWN, D])
    g.o_wkv_p = O("o_wkv_p", [32, 64, 64])
    g.o_shift_p = O("o_shift_p", [SHIFT_W])
    g.o_pool_p = O("o_pool_p", [15, RW])
    g.o_mk = O("o_mk", [256, 512])
    g.o_mv = O("o_mv", [256, 512])
    g.o_wkv_s = O("o_wkv_s", [16, 32, 64, 64])
    g.o_shift_s = O("o_shift_s", [16, SHIFT_W])
    g.o_pool_s = O("o_pool_s", [16, 15, RW])
    g.zT = T("zT", [IN_W, NT])
    g.oTd = T("oTd", [D, OWN], BF16)
    g.xs2 = T("xs2", [OWN, D])
    g.xs = T("xs", [OWN, D])
    g.dbg = dbg
    if dbg in ("m1", "m1s"):
        g.d_zT = O("d_zT", [IN_W, NT])
        g.d_hT = O("d_hT", [128, 32, 128])

    with contextlib.ExitStack() as es:
        S = Sched(nc, es)
        g.S = S
        g.A = Arena(nc, es, 204)
        g.A.S = S
        g.ps = es.enter_context(nc.psum_tensor("ps", [128, 4096], F32))
        g.PB = [Buf("psum%d" % i) for i in range(8)]
        setup_consts(g)
        g.A0 = g.A.off
        phase_m1(g)
        if dbg in ("m1", "m1s"):
            S.barrier()
            S.dma("sp", g.d_zT, g.zT)
        else:
            phase_m2(g)
            S.barrier()
            g.A.off = g.A0
            if g.stop > 5:
                phase_m3(g)
            if g.stop > 6:
                phase_m4(g)
            if g.stop > 7:
                phase_x(g)
            if g.stop > 8:
                phase_f(g)
            if dbg == "mix":
                S.barrier()
                S.dma("sp", g.y_out, g.xs if g.stop <= 7 else g.xs2)
        S.emit()
    return nc, g


def bank(g, i, dt=F32):
    ap = g.ps[:, i * 512:(i + 1) * 512]
    if dt == BF16:
        ap = ap.bitcast(BF16)
    return ap


def pvec(ap1d, n_chunks):
    return ap1d.rearrange("(c p) -> p c", p=128)


def setup_consts(g):
    S, A = g.S, g.A
    g.CB = Buf("consts")
    g.identf = A.alloc([128], F32)
    g.identb = A.alloc([128], BF16)
    g.bones = A.alloc([128], BF16)
    g.masks = A.alloc([8, 64], F32)
    S.dma("sp", g.identf, g.c_identf, writes=[g.CB])
    S.dma("pool", g.identb, g.c_identb, writes=[g.CB])
    S.dma("pool", g.bones, g.c_bones, writes=[g.CB])
    S.dma("sp", g.masks, g.c_masks, writes=[g.CB])


def norm_transpose(g, src_rows, ntiles, gvec, gB, hT, hB, tok0=0):
    S, A = g.S, g.A
    m = A.mark()
    xt = [A.alloc([D], F32) for _ in range(2)]
    XB = [Buf("xt0"), Buf("xt1")]
    junk = A.alloc([D], BF16)
    JB = Buf("junk")
    xsb = [A.alloc([D], BF16) for _ in range(2)]
    XS = [Buf("xs0"), Buf("xs1")]
    st = A.alloc([ntiles, 4], F32)
    SB = [Buf("st%d" % i) for i in range(ntiles)]
    for i in range(ntiles):
        b = i % 2
        S.dma("sp", xt[b], src_rows[i], writes=[XB[b]])
        S.op("act", lambda e, b=b, i=i: e.activation(junk, xt[b], AF.Square, accum_out=st[:, i, 0:1]),
             reads=[XB[b]], writes=[JB, SB[i]])
        S.op("act", lambda e, i=i: e.activation(st[:, i, 1:2], st[:, i, 0:1], AF.Sqrt, scale=1.0 / D, bias=1e-6),
             reads=[SB[i]], writes=[SB[i]])
        S.op("dve", lambda e, i=i: e.reciprocal(st[:, i, 2:3], st[:, i, 1:2]), reads=[SB[i]], writes=[SB[i]])
        S.op("act", lambda e, b=b, i=i: e.activation(xsb[b], xt[b], AF.Copy, scale=st[:, i, 2:3]),
             reads=[XB[b], SB[i]], writes=[XS[b]])
        for q in range(4):
            pb = 4 + (i * 4 + q) % 4
            pt = bank(g, pb, BF16)
            S.group("pe", [lambda e, b=b, q=q, j=j, pt=pt: e.transpose(pt[:, j * 128:(j + 1) * 128], xsb[b][:, (q * 8 + j) * 128:(q * 8 + j + 1) * 128], g.identb)
                           for j in range(8)], reads=[XS[b], g.CB], writes=[g.PB[pb]])
            S.op("dve", lambda e, q=q, i=i, pt=pt: e.tensor_tensor(
                hT[:, q * 8:(q + 1) * 8, tok0 + i * 128:tok0 + (i + 1) * 128],
                pt.rearrange("p (a b) -> p a b", a=8),
                gvec[:, q * 8:(q + 1) * 8].unsqueeze(2).to_broadcast([128, 8, 128]), ALU.mult),
                reads=[g.PB[pb], gB], writes=[hB])
    A.release(m)


def col_chunks(lo, hi):
    out = []
    c = lo
    while c < hi:
        if c == 6144 or c == 6240:
            sz = 96
        else:
            sz = 128
        out.append((c, sz))
        c += sz
    return out


def gemm_zT(g, hT, hB, ntok, tok_col0, col_blocks, tgroups):
    S, A = g.S, g.A
    m = A.mark()
    wb = [A.alloc([NCH, 512], BF16) for _ in range(2)]
    WB = [Buf("wb0"), Buf("wb1")]
    stg = [A.alloc([512], F32) for _ in range(4)]
    SG = [Buf("stg%d" % i) for i in range(4)]
    w_v = g.w_in.rearrange("(kc p) c -> p kc c", p=128)
    si = 0
    pi = 0
    for bi, (lo, hi) in enumerate(col_blocks):
        b = bi % 2
        nb = hi - lo
        for q4 in range(4):
            S.dma("pool", wb[b][:, q4 * 8:(q4 + 1) * 8, 0:nb], w_v[:, q4 * 8:(q4 + 1) * 8, lo:hi], writes=[WB[b]])
        for (c0, csz) in col_chunks(lo, hi):
            for (t0, tn) in tgroups:
                pb = pi % 4
                pi += 1
                pt = bank(g, pb)
                S.group("pe", [lambda e, b=b, kc=kc, c0=c0, csz=csz, t0=t0, tn=tn, pt=pt, lo=lo: e.matmul(
                    pt[0:csz, 0:tn], wb[b][:, kc, c0 - lo:c0 - lo + csz], hT[:, kc, t0:t0 + tn],
                    start=(kc == 0), stop=(kc == NCH - 1)) for kc in range(NCH)],
                    reads=[WB[b], hB], writes=[g.PB[pb]])
                s = si % 4
                si += 1
                eng = "act" if s % 2 == 0 else "dve"
                if eng == "act":
                    S.op("act", lambda e, s=s, csz=csz, tn=tn, pt=pt: e.activation(stg[s][0:csz, 0:tn], pt[0:csz, 0:tn], AF.Copy),
                         reads=[g.PB[pb]], writes=[SG[s]])
                else:
                    S.op("dve", lambda e, s=s, csz=csz, tn=tn, pt=pt: e.tensor_copy(stg[s][0:csz, 0:tn], pt[0:csz, 0:tn]),
                         reads=[g.PB[pb]], writes=[SG[s]])
                S.dma("sp", g.zT[c0:c0 + csz, tok_col0 + t0:tok_col0 + t0 + tn], stg[s][0:csz, 0:tn], reads=[SG[s]], writes=[Buf("zw")])
    A.release(m)


def phase_m1(g):
    S, A = g.S, g.A
    g.ZB = Buf("zT")
    m = A.mark()
    gv = A.alloc([NCH], F32)
    GV = Buf("gv")
    S.dma("sp", gv, g.norm_mix_g, writes=[GV])
    hT = A.alloc([NCH, TB], BF16)
    hB = Buf("hT")
    xin_t = g.xin.rearrange("(n p) d -> n p d", p=128)
    if g.dbg == "m1s":
        norm_transpose(g, [xin_t[7]], 1, gv, GV, hT, hB, 0)
        hf = A.alloc([32, 128], F32)
        HF = Buf("hf")
        S.op("dve", lambda e: e.tensor_copy(hf, hT[:, :, 0:128]), reads=[hB], writes=[HF])
        S.dma("sp", g.d_hT, hf, reads=[HF])
        gemm_zT(g, hT, hB, 128, TA, [(0, 512)], [(0, 128)])
        return
    norm_transpose(g, [xin_t[i] for i in range(7)], 7, gv, GV, hT, hB, 0)
    gemm_zT(g, hT, hB, TA, 0, [(2048, 2560), (2560, 3072), (3072, 3584), (3584, 4096),
                               (4096, 4608), (4608, 5120), (5120, 5632), (5632, 6144), (6144, 6336)],
            [(0, 512), (512, 384)])
    norm_transpose(g, [xin_t[7 + i] for i in range(10)], 10, gv, GV, hT, hB, 0)
    blocks = [(i * 512, (i + 1) * 512) for i in range(12)] + [(6144, 6592)] + [(6592 + i * 512, 6592 + (i + 1) * 512) for i in range(4)]
    gemm_zT(g, hT, hB, TB, TA, blocks, [(0, 512), (512, 512), (1024, 256)])
    A.release(m)


def make_consts():
    c = {}
    c["c_identf"] = np.eye(128, dtype=np.float32)
    c["c_identb"] = np.eye(128, dtype=np.float32)
    p = np.arange(128)[:, None]
    t = np.arange(64)[None, :]
    i = p % 64
    msu = (i < t).astype(np.float32)
    miu = (i <= t).astype(np.float32)
    msl = (i > t).astype(np.float32)
    blk = ((i // 8) == (t // 8)).astype(np.float32)
    c["c_masks"] = np.stack([msu, miu, -msu, -msl, msu * blk, miu * blk, -msu * blk, -msl * blk], axis=1).astype(np.float32)
    bo = np.zeros((128, 128), np.float32)
    bo[:64, :64] = 1
    bo[64:, 64:] = 1
    c["c_bones"] = bo
    sm = np.ones((128, 640), np.float32)
    sm[:, 0:512:64] = 0
    sm[:, 512::8] = 0
    c["c_scanmask"] = sm
    s8 = np.arange(8)[None, :, None]
    tt = np.arange(64)[None, None, :]
    c["c_seqmask"] = np.broadcast_to((tt // 8 == s8), (128, 8, 64)).astype(np.float32).copy()
    c["c_tokmask"] = ((i // 8) == np.arange(8)[None, :]).astype(np.float32)
    s16 = np.arange(16)[None, :, None]
    t128 = np.arange(128)[None, None, :]
    c["c_qmask"] = np.broadcast_to((t128 // 8 == s16), (128, 16, 128)).astype(np.float32).copy()
    return c


def make_in_map(inp, c, consts):
    b, half = c // 2, c % 2
    f = lambda a: np.ascontiguousarray(a, dtype=np.float32)
    m = dict(consts)
    xin = np.zeros((NT, D), np.float32)
    xp = inp["x_prompt"][b]
    if half == 1:
        xin[0:1024] = xp[0:1024]
    xin[1024:2048] = xp[half * 1024:(half + 1) * 1024]
    xin[2048:2176] = inp["x_sample"][16 * c:16 * c + 16].reshape(128, D)
    m["xin"] = xin
    m["mem"] = f(inp["mem_prompt"][b])
    m["st_wkv"] = f(inp["state_wkv"][0, 16 * c:16 * c + 16])
    m["st_shift"] = f(inp["state_shift"][0, 16 * c:16 * c + 16])
    m["st_pool"] = f(inp["state_pool"][0, 16 * c:16 * c + 16])
    m["ck"] = f(inp["cache_mem_k"][0, 16 * c:16 * c + 16].reshape(16, 256, 512))
    m["cv"] = f(inp["cache_mem_v"][0, 16 * c:16 * c + 16].reshape(16, 256, 512))
    for nm in ["w_in", "w_out", "w_pool", "w_decay_up", "w_a_up", "w_g_up", "w_xq", "w_mk", "w_mv", "w_xo", "w_up", "w_down",
               ]:
        m[nm] = f(inp[nm][0])
    pv = lambda a: np.ascontiguousarray(np.asarray(a, np.float32).reshape(-1, 128).T)
    for nm in ["norm_mix_g", "norm_xa_g", "norm_mem_g", "norm_ffn_g", "w0", "a0", "k_k", "k_a", "lnx_g", "lnx_b", "pool_scale", "r_k"]:
        m[nm] = pv(inp[nm][0])
    m["norm_final_g"] = pv(inp["norm_final_g"])
    m["norm_final_row"] = f(inp["norm_final_g"])
    mu = np.asarray(inp["mu_shift"][0], np.float32)
    mup = np.zeros((128, 52), np.float32)
    mup[:, 0:48] = mu[0:6144].reshape(48, 128).T
    mup[0:96, 48] = mu[6144:6240]
    mup[0:96, 49] = mu[6240:6336]
    mup[:, 50:52] = mu[6336:6592].reshape(2, 128).T
    m["mu_shift"] = mup
    pos0 = half * 1024
    ic = np.zeros((4, OWN), np.float32)
    for gi, w in enumerate((2, 4, 8, 16)):
        ic[gi, :1024] = 1.0 / np.minimum(w, pos0 + np.arange(1024) + 1)
        ic[gi, 1024:] = 1.0 / w
    m["c_invcnt"] = ic
    return m


def bc(ap, shape):
    return ap.to_broadcast(list(shape))


def phase_m2(g):
    S, A = g.S, g.A
    ps = g.ps
    g.OTB = Buf("oTd")
    CB = Buf("m2consts")
    ld = lambda shape, src, dt=F32, q="sp": (lambda t: (S.dma(q if dt == F32 else "pool", t, src, writes=[CB]), t)[1])(A.alloc(shape, dt))
    scanmask = ld([640], g.c_scanmask)
    seqmask = ld([8, 64], g.c_seqmask)
    tokmask = ld([8], g.c_tokmask)
    mu = ld([52], g.mu_shift)
    w0 = ld([16], g.w0)
    a0 = ld([16], g.a0)
    k_k = ld([16], g.k_k)
    k_a = ld([16], g.k_a)
    r_k = ld([16], g.r_k)
    lnx_g = ld([16], g.lnx_g)
    lnx_b = ld([16], g.lnx_b)
    omka = A.alloc([16], F32)
    S.op("dve", lambda e: e.tensor_scalar(omka, k_a, -1.0, 1.0, ALU.mult, ALU.add), reads=[CB], writes=[CB])
    wdu = A.alloc([RW], BF16)
    wau = A.alloc([RW], BF16)
    wgu = A.alloc([2, RW], BF16)
    S.dma("pool", wdu[0:96, :], g.w_decay_up, writes=[CB])
    S.dma("pool", wau[0:96, :], g.w_a_up, writes=[CB])
    S.dma("pool", wgu, g.w_g_up.rearrange("(c p) n -> p c n", p=128), writes=[CB])
    identf, identb, bones, masks = g.identf, g.identb, g.bones, g.masks
    CBs = [CB, g.CB]
    shst = A.alloc([2, 3, 128], F32)
    OSB = Buf("shst")
    cl = col_chunks(0, SHIFT_W)

    stT = A.alloc([52, 16], F32)
    STB = Buf("stT")
    mk0 = A.mark()
    strow = A.alloc([SHIFT_W], F32)
    SR = Buf("strow")
    S.dma("sp", strow[0:16, :], g.st_shift, writes=[SR])
    for half, (j0, j1) in enumerate([(0, 32), (32, 52)]):
        S.group("pe", [lambda e, j=j, half=half, j0=j0: e.transpose(ps[0:cl[j][1], half * 512 + (j - j0) * 16: half * 512 + (j - j0 + 1) * 16],
                                                                   strow[0:16, cl[j][0]:cl[j][0] + cl[j][1]], identf[0:16, 0:16])
                       for j in range(j0, j1)], reads=[SR, g.CB], writes=[g.PB[half]])
    S.op("dve", lambda e: e.tensor_copy(stT[:, 0:32, :], ps[:, 0:512].rearrange("p (a b) -> p a b", b=16)), reads=[g.PB[0]], writes=[STB])
    S.op("dve", lambda e: e.tensor_copy(stT[:, 32:48, :], ps[:, 512:768].rearrange("p (a b) -> p a b", b=16)), reads=[g.PB[1]], writes=[STB])
    S.op("dve", lambda e: e.tensor_copy(stT[0:96, 48:50, :], ps[0:96, 768:800].rearrange("p (a b) -> p a b", b=16)), reads=[g.PB[1]], writes=[STB])
    S.op("dve", lambda e: e.tensor_copy(stT[:, 50:52, :], ps[:, 800:832].rearrange("p (a b) -> p a b", b=16)), reads=[g.PB[1]], writes=[STB])
    A.release(mk0)
    if g.stop <= 1:
        return

    tw = A.alloc([NT], BF16)
    adm = A.alloc([NT], BF16)
    sg = A.alloc([2, NT], BF16)
    LRB = Buf("lowrank")
    mk1 = A.mark()
    LP = 1 + 2048 + 144
    for j, (dst, func) in enumerate([(tw, AF.Tanh), (adm, AF.Copy), (sg[:, 0, :], AF.Sigmoid), (sg[:, 1, :], AF.Sigmoid)]):
        c0, csz = cl[48 + j]
        lz = A.alloc([LP], F32)
        dd = A.alloc([NT], F32)
        LZ = Buf("lz")
        DD = Buf("dd")
        S.op("dve", lambda e, lz=lz: e.memset(lz[:, 0:1], 0.0), writes=[LZ])
        S.dma("sp", lz[0:csz, 1:2049], g.zT[c0:c0 + csz, 0:2048], reads=[g.ZB], writes=[LZ])
        lzs = lz[:, 2049:LP].rearrange("p (s t) -> p s t", t=9)
        S.dma("sp", lzs[0:csz, :, 1:9], g.zT[c0:c0 + csz, 2048:NT].rearrange("p (s t) -> p s t", t=8), reads=[g.ZB], writes=[LZ])
        S.op("dve", lambda e, lzs=lzs, j=j, csz=csz: e.tensor_copy(lzs[0:csz, :, 0], stT[0:csz, 48 + j, :]), reads=[STB], writes=[LZ])
        pb = 2 + j % 2
        S.group("pe", [lambda e, lzs=lzs, csz=csz, pb=pb: e.transpose(ps[0:16, pb * 512:pb * 512 + csz], lzs[0:csz, :, 8], identf[0:csz, 0:csz]),
                       lambda e, lz=lz, csz=csz, pb=pb: e.transpose(ps[0:1, pb * 512 + 128:pb * 512 + 128 + csz], lz[0:csz, 2048:2049], identf[0:csz, 0:csz])],
                reads=[LZ, g.CB], writes=[g.PB[pb]])
        S.op("act", lambda e, c0=c0, csz=csz, pb=pb: e.activation(shst[0:16, 0, 0, 0:csz], ps[0:16, pb * 512:pb * 512 + csz], AF.Copy), reads=[g.PB[pb]], writes=[OSB])
        S.op("act", lambda e, c0=c0, csz=csz, pb=pb: e.activation(shst[0:1, 1, 0, 0:csz], ps[0:1, pb * 512 + 128:pb * 512 + 128 + csz], AF.Copy), reads=[g.PB[pb]], writes=[OSB])
        S.dma("sp", g.o_shift_s[:, c0:c0 + csz], shst[0:16, 0, 0, 0:csz], reads=[OSB])
        S.dma("sp", g.o_shift_p.rearrange("(o n) -> o n", o=1)[:, c0:c0 + csz], shst[0:1, 1, 0, 0:csz], reads=[OSB])
        S.op("dve", lambda e, lz=lz, dd=dd, csz=csz: e.tensor_tensor(dd[0:csz, 0:2048], lz[0:csz, 0:2048], lz[0:csz, 1:2049], ALU.subtract), reads=[LZ], writes=[DD])
        S.op("dve", lambda e, lzs=lzs, dd=dd, csz=csz: e.tensor_tensor(dd[0:csz, 2048:NT].rearrange("p (s t) -> p s t", t=8), lzs[0:csz, :, 0:8], lzs[0:csz, :, 1:9], ALU.subtract), reads=[LZ], writes=[DD])
        S.op("dve", lambda e, lz=lz, dd=dd, csz=csz, j=j: e.scalar_tensor_tensor(dd[0:csz, 0:2048], dd[0:csz, 0:2048], mu[0:csz, 48 + j:49 + j], lz[0:csz, 1:2049], ALU.mult, ALU.add), reads=[LZ, DD, CB], writes=[DD])
        S.op("dve", lambda e, lzs=lzs, dd=dd, csz=csz, j=j: e.scalar_tensor_tensor(dd[0:csz, 2048:NT].rearrange("p (s t) -> p s t", t=8), dd[0:csz, 2048:NT].rearrange("p (s t) -> p s t", t=8), mu[0:csz, 48 + j:49 + j], lzs[0:csz, :, 1:9], ALU.mult, ALU.add), reads=[LZ, DD, CB], writes=[DD])
        S.op("act", lambda e, dst=dst, dd=dd, csz=csz, func=func: e.activation(dst[0:csz, :], dd[0:csz, :], func), reads=[DD], writes=[LRB])
        A.release(A.mark())
        A.off = mk1
    A.release(mk1)
    if g.stop <= 2:
        return

    N_OWN = OWN
    ybuf = A.alloc([N_OWN], F32)
    YB = Buf("y")
    gbuf = A.alloc([N_OWN], F32)
    GB = Buf("g")
    bonus = A.alloc([N_OWN], F32)
    BNB = Buf("bonus")
    Sf = A.alloc([64], F32)
    SFB = Buf("Sf")
    Sb = [A.alloc([128], BF16) for _ in range(2)]
    SBB = [Buf("Sb0"), Buf("Sb1")]
    S0T = A.alloc([16, 64], F32)
    S0B = Buf("S0T")
    Snew = S0T
    SNB = S0B
    Sb8 = A.alloc([8, 128], BF16)
    SB8 = Buf("Sb8")
    Zp4 = [A.alloc([4, 2, 128], BF16) for _ in range(2)]
    ZPB = [Buf("Zp0"), Buf("Zp1")]
    Wp4 = [A.alloc([4, 2, 128], BF16) for _ in range(2)]
    WPB = [Buf("Wp0"), Buf("Wp1")]
    Vp4 = [A.alloc([4, 2, 64], BF16) for _ in range(2)]
    VPB = [Buf("Vp0"), Buf("Vp1")]
    NaK = [A.alloc([8, 64], BF16) for _ in range(2)]
    M12 = [A.alloc([8, 64], BF16) for _ in range(2)]
    NKB = [Buf("NaK0"), Buf("NaK1")]
    Tinv = [A.alloc([8, 64], BF16) for _ in range(2)]
    TIB = [Buf("Ti0"), Buf("Ti1")]
    Xb = [A.alloc([8, 64], BF16) for _ in range(2)]
    XTb = [A.alloc([8, 64], BF16) for _ in range(2)]
    XB_ = [Buf("X0"), Buf("X1")]
    Pb = [A.alloc([8, 64], BF16) for _ in range(2)]
    PBF = [Buf("P0"), Buf("P1")]
    Bsb = A.alloc([128], BF16)
    BSB = Buf("Bsb")
    Kmask = A.alloc([8, 64], BF16)
    Rmask = A.alloc([8, 64], BF16)
    KMB = Buf("KRmask")
    Wm = A.alloc([2, 8, 64], BF16)
    WMB = Buf("Wm")
    for t, b in [(Sb[0], SBB[0]), (Sb[1], SBB[1]), (Sb8, SB8), (Zp4[0], ZPB[0]), (Zp4[1], ZPB[1]), (Wp4[0], WPB[0]), (Wp4[1], WPB[1]),
                 (Vp4[0], VPB[0]), (Vp4[1], VPB[1])]:
        S.op("dve", lambda e, t=t: e.memset(t, 0.0), writes=[b])
    KR = [A.alloc([8, 2, 64], BF16) for _ in range(2)]
    AKm = [[A.alloc([8, 2, 64], BF16) for _ in range(2)] for _ in range(2)]
    AKd = [A.alloc([8, 2, 64], BF16) for _ in range(2)]
    mx = [A.alloc([3, 64 + 512], F32) for _ in range(2)]
    Gc = [A.alloc([16], F32) for _ in range(2)]
    mvb = [A.alloc([64 + 512], BF16) for _ in range(2)]
    SIB = [Buf("scanin0"), Buf("scanin1")]
    for i in range(2):
        S.op("dve", lambda e, i=i: e.memset(mx[i], 0.0), writes=[SIB[i]])
        S.op("dve", lambda e, i=i: e.memset(mvb[i], 0.0), writes=[SIB[i]])
        for h in range(2):
            S.op("dve", lambda e, i=i, h=h: e.memset(AKm[i][h], 0.0), writes=[SIB[i]])
    zoff = A.off
    zb = A.alloc([3, 513], F32)
    dt_ = A.alloc([3, 512], F32)
    eoff = A.off
    A.off = zoff
    stage = A.alloc([16, 128], F32)
    A.off = eoff
    ZBB = Buf("zb+d+stage")
    DTB = ZBB
    STG = ZBB
    tmp = [A.alloc([512], F32) for _ in range(11)]
    TB_ = [Buf("tmp%d" % i) for i in range(11)]
    oTs = [A.alloc([OWN], BF16) for _ in range(2)]
    OSG = [Buf("oTs0"), Buf("oTs1")]
    sqb = A.alloc([512], BF16)
    SQB = Buf("sq")

    zT3 = g.zT[0:6144, :].rearrange("(j q) t -> q j t", q=2048)
    segs = [(0, 512, "pre"), (512, 512, "pre"), (1024, 512, "own"), (1536, 512, "own"), (2048, 128, "smp")]
    st = {"si": 0, "cur": 0, "hk": 0, "done": 0}

    def do_pair(p):
        prow = slice(p * 128, (p + 1) * 128)
        for sq_ in range(16):
            S.dma("sp", stage[0:64, sq_, :].rearrange("p (h k) -> p h k", h=2),
                  g.st_wkv[sq_, 2 * p:2 * p + 2, :, :].rearrange("h v k -> v h k"), writes=[STG])
        for q in range(2):
            S.group("pe", [lambda e, q=q, s=s: e.transpose(ps[:, 2560 + q * 512 + s * 64:2560 + q * 512 + (s + 1) * 64], stage[0:64, q * 8 + s, :], identf[0:64, 0:64])
                           for s in range(8)], reads=[STG, g.CB], writes=[g.PB[5 + q]])
            S.op("act", lambda e, q=q: e.activation(S0T[:, q * 8:(q + 1) * 8, :], ps[:, 2560 + q * 512:2560 + (q + 1) * 512].rearrange("p (s v) -> p s v", v=64), AF.Copy),
                 reads=[g.PB[5 + q]], writes=[S0B])
        S.op("dve", lambda e: e.memset(Sf, 0.0), writes=[SFB])
        S.op("dve", lambda e: e.memset(Sb[0][0:64, 0:64], 0.0), writes=[SBB[0]])
        S.op("dve", lambda e: e.memset(Sb[0][64:128, 64:128], 0.0), writes=[SBB[0]])
        st["cur"] = 0

        def do_seg(t0, N, kind):
            yield ("WAIT", st["hk"] - 2)
            b = st["si"] % 2
            st["si"] += 1
            ncn = N // 64
            own = kind != "pre"
            smp = kind == "smp"
            oo = t0 - 1024
            if not smp:
                if t0 == 0:
                    S.op("dve", lambda e: e.memset(zb[:, :, 0:1], 0.0), writes=[ZBB])
                    S.dma("sp", zb[:, :, 1:513], zT3[prow, :, 0:512], reads=[g.ZB], writes=[ZBB])
                else:
                    S.dma("sp", zb[:, :, 0:513], zT3[prow, :, t0 - 1:t0 + 512], reads=[g.ZB], writes=[ZBB])
                zprev = zb[:, :, 0:512]
                zcur = zb[:, :, 1:513]
                shp3 = lambda ap: ap
            else:
                zv = zb[:, :, 0:144].rearrange("p j (s t) -> p j s t", t=9)
                for j in range(3):
                    S.dma("sp", zv[:, j, :, 1:9], zT3[prow, j, 2048:NT].rearrange("p (s t) -> p s t", t=8), reads=[g.ZB], writes=[ZBB])
                    S.op("dve", lambda e, j=j, zv=zv: e.tensor_copy(zv[:, j, :, 0], stT[:, j * 16 + p, :]), reads=[STB], writes=[ZBB])
                zprev = zv[:, :, :, 0:8]
                zcur = zv[:, :, :, 1:9]
                shp3 = lambda ap: ap.rearrange("p j (s t) -> p j s t", t=8)
            if smp or t0 == 1536:
                pbk = 7
                if smp:
                    fns = [lambda e, j=j: e.transpose(ps[0:16, 3584 + j * 128:3584 + (j + 1) * 128], zv[:, j, :, 8], identf) for j in range(3)]
                    S.group("pe", fns, reads=[ZBB, g.CB], writes=[g.PB[7]])
                    for j in range(3):
                        S.op("act", lambda e, j=j: e.activation(shst[0:16, 0, j, :], ps[0:16, 3584 + j * 128:3584 + (j + 1) * 128], AF.Copy),
                             reads=[g.PB[7]], writes=[OSB])
                        S.dma("sp", g.o_shift_s[:, j * 2048 + p * 128:j * 2048 + (p + 1) * 128], shst[0:16, 0, j, :], reads=[OSB])
                else:
                    fns = [lambda e, j=j: e.transpose(ps[0:1, 3584 + j * 128:3584 + (j + 1) * 128], zb[:, j, 512:513], identf) for j in range(3)]
                    S.group("pe", fns, reads=[ZBB, g.CB], writes=[g.PB[7]])
                    for j in range(3):
                        S.op("act", lambda e, j=j: e.activation(shst[0:1, 1, j, :], ps[0:1, 3584 + j * 128:3584 + (j + 1) * 128], AF.Copy),
                             reads=[g.PB[7]], writes=[OSB])
                        S.dma("sp", g.o_shift_p.rearrange("(o n) -> o n", o=1)[:, j * 2048 + p * 128:j * 2048 + (p + 1) * 128], shst[0:1, 1, j, :], reads=[OSB])
            dv = shp3(dt_[:, :, 0:N])
            mv3 = shp3(mx[b][:, :, 64:64 + N])
            mu3 = mu[:, p:48:16]
            S.op("dve", lambda e, dv=dv, zprev=zprev, zcur=zcur: e.tensor_tensor(dv, zprev, zcur, ALU.subtract), reads=[ZBB], writes=[DTB])
            if smp:
                mub = mu3.unsqueeze(2).unsqueeze(3).to_broadcast([128, 3, 16, 8])
            else:
                mub = mu3.unsqueeze(2).to_broadcast([128, 3, N])
            S.op("dve", lambda e, dv=dv, mub=mub: e.tensor_tensor(dv, dv, mub, ALU.mult), reads=[DTB, CB], writes=[DTB])
            S.op("dve", lambda e, dv=dv, mv3=mv3, zcur=zcur: e.tensor_tensor(mv3, dv, zcur, ALU.add), reads=[DTB, ZBB], writes=[SIB[b]])
            S.op("act", lambda e: e.activation(mvb[b][:, 64:64 + N], mx[b][:, 2, 64:64 + N], AF.Copy), reads=[SIB[b]], writes=[SIB[b]])
            mr = mx[b][:, 0, 64:64 + N]
            mk_ = mx[b][:, 1, 64:64 + N]
            mv_ = mx[b][:, 2, 64:64 + N]
            T_ = lambda i: tmp[i][:, 0:N]
            yield 1
            pw = ps[:, 3584:3584 + N]
            S.op("pe", lambda e, pw=pw: e.matmul(pw, wdu[0:96, prow], tw[0:96, t0:t0 + N], start=True, stop=True), reads=[LRB, CB], writes=[g.PB[7]])
            S.op("act", lambda e, pw=pw: e.activation(T_(0), pw, AF.Sigmoid, bias=w0[:, p:p + 1]), reads=[g.PB[7], CB], writes=[TB_[0]])
            S.op("pe", lambda e, pw=pw: e.matmul(pw, wau[0:96, prow], adm[0:96, t0:t0 + N], start=True, stop=True), reads=[LRB, CB], writes=[g.PB[7]])
            S.op("act", lambda e, pw=pw: e.activation(T_(1), pw, AF.Sigmoid, bias=a0[:, p:p + 1]), reads=[g.PB[7], CB], writes=[TB_[1]])
            if own:
                S.group("pe", [lambda e, pw=pw, c=c: e.matmul(pw, wgu[:, c, prow], sg[:, c, t0:t0 + N], start=(c == 0), stop=(c == 1)) for c in range(2)],
                        reads=[LRB, CB], writes=[g.PB[7]])
                S.op("act", lambda e, pw=pw: e.activation(gbuf[:, oo:oo + N], pw, AF.Copy), reads=[g.PB[7]], writes=[GB])
            yield 1
            smk = scanmask[:, 512:640] if smp else scanmask[:, 0:N]
            S.op("dve", lambda e, smk=smk: e.tensor_tensor_scan(T_(2), smk, T_(0), 0.0, ALU.mult, ALU.add), reads=[TB_[0], CB], writes=[TB_[2]])
            cl_ = 8 if smp else 64
            nseg = N // cl_
            cs3 = T_(2).rearrange("p (c t) -> p c t", t=cl_)
            S.op("dve", lambda e, cs3=cs3, cl_=cl_, nseg=nseg: e.tensor_tensor(T_(3).rearrange("p (c t) -> p c t", t=cl_), cs3[:, :, cl_ - 1:cl_].to_broadcast([128, nseg, cl_]), cs3, ALU.subtract),
                 reads=[TB_[2]], writes=[TB_[3]])
            S.op("dve", lambda e: e.tensor_tensor(T_(4), T_(2), T_(0), ALU.subtract), reads=[TB_[2], TB_[0]], writes=[TB_[4]])
            yield 1
            S.op("act", lambda e: e.activation(T_(5), T_(2), AF.Exp, scale=-C0), reads=[TB_[2]], writes=[TB_[5]])
            S.op("act", lambda e: e.activation(T_(6), T_(2), AF.Exp, scale=C0), reads=[TB_[2]], writes=[TB_[6]])
            yield 1
            S.op("act", lambda e: e.activation(T_(4), T_(4), AF.Exp, scale=-C0), reads=[TB_[4]], writes=[TB_[4]])
            S.op("act", lambda e: e.activation(T_(3), T_(3), AF.Exp, scale=-C0), reads=[TB_[3]], writes=[TB_[3]])
            E5 = T_(5).rearrange("p (c t) -> p c t", t=cl_)
            S.op("dve", lambda e, E5=E5, b=b, nseg=nseg, cl_=cl_: e.tensor_copy(Gc[b][:, 0:nseg], E5[:, :, cl_ - 1]), reads=[TB_[5]], writes=[SIB[b]])
            yield 1
            S.op("dve", lambda e: e.tensor_scalar(T_(7), mk_, k_k[:, p:p + 1], None, ALU.mult), reads=[SIB[b], CB], writes=[TB_[7]])
            S.op("dve", lambda e: e.tensor_tensor(sqb[:, 0:N], T_(7), T_(7), ALU.mult), reads=[TB_[7]], writes=[SQB])
            S.op("pe", lambda e, pw=pw: e.matmul(pw, bones, sqb[:, 0:N], start=True, stop=True), reads=[SQB, g.CB], writes=[g.PB[7]])
            S.op("act", lambda e, pw=pw: e.activation(T_(8), pw, AF.Sqrt), reads=[g.PB[7]], writes=[TB_[8]])
            yield 1
            S.op("dve", lambda e: e.tensor_scalar(T_(8), T_(8), 1e-12, None, ALU.max), reads=[TB_[8]], writes=[TB_[8]])
            S.op("dve", lambda e: e.reciprocal(T_(8), T_(8)), reads=[TB_[8]], writes=[TB_[8]])
            S.op("dve", lambda e: e.tensor_tensor(T_(7), T_(7), T_(8), ALU.mult), reads=[TB_[7], TB_[8]], writes=[TB_[7]])
            yield 1
            S.op("dve", lambda e: e.tensor_scalar(T_(9), T_(1), k_a[:, p:p + 1], omka[:, p:p + 1], ALU.mult, ALU.add), reads=[TB_[1], CB], writes=[TB_[9]])
            S.op("dve", lambda e: e.tensor_tensor(T_(9), T_(9), mk_, ALU.mult), reads=[TB_[9], SIB[b]], writes=[TB_[9]])
            S.op("dve", lambda e: e.tensor_tensor(T_(10), T_(7), T_(1), ALU.mult), reads=[TB_[7], TB_[1]], writes=[TB_[10]])
            yield 1
            c64 = lambda ap: ap.rearrange("p (c t) -> p c t", t=64)
            S.op("dve", lambda e, b=b, ncn=ncn: e.tensor_tensor(KR[b][:, 0:ncn, 0, :], c64(T_(7)), c64(T_(4)), ALU.mult), reads=[TB_[7], TB_[4]], writes=[SIB[b]])
            S.op("dve", lambda e, b=b, ncn=ncn: e.tensor_tensor(KR[b][:, 0:ncn, 1, :], c64(mr), c64(T_(5)), ALU.mult), reads=[TB_[5]], writes=[SIB[b]])
            yield 1
            for h in range(2):
                hs = slice(h * 64, (h + 1) * 64)
                S.op("dve", lambda e, b=b, h=h, hs=hs, ncn=ncn: e.tensor_tensor(AKm[b][h][hs, 0:ncn, 0, :], c64(T_(10))[hs], c64(T_(6))[hs], ALU.mult),
                     reads=[TB_[10], TB_[6]], writes=[SIB[b]])
                S.op("dve", lambda e, b=b, h=h, hs=hs, ncn=ncn: e.tensor_tensor(AKm[b][h][hs, 0:ncn, 1, :], c64(T_(9))[hs], c64(T_(6))[hs], ALU.mult),
                     reads=[TB_[9], TB_[6]], writes=[SIB[b]])
            S.op("dve", lambda e, b=b, ncn=ncn: e.tensor_tensor(AKd[b][:, 0:ncn, 0, :], c64(T_(10)), c64(T_(3)), ALU.mult), reads=[TB_[10], TB_[3]], writes=[SIB[b]])
            S.op("dve", lambda e, b=b, ncn=ncn: e.tensor_tensor(AKd[b][:, 0:ncn, 1, :], c64(T_(9)), c64(T_(3)), ALU.mult), reads=[TB_[9], TB_[3]], writes=[SIB[b]])
            if own:
                S.op("dve", lambda e: e.scalar_tensor_tensor(sqb[:, 0:N], mr, r_k[:, p:p + 1], T_(9), ALU.mult, ALU.mult), reads=[SIB[b], TB_[9], CB], writes=[SQB])
                S.op("pe", lambda e, pw=pw: e.matmul(pw, bones, sqb[:, 0:N], start=True, stop=True), reads=[SQB, g.CB], writes=[g.PB[7]])
                S.op("dve", lambda e, pw=pw: e.tensor_tensor(bonus[:, oo:oo + N], pw, mv_, ALU.mult), reads=[g.PB[7], SIB[b]], writes=[BNB])
            yield 1
            mo = 4 if smp else 0
            nlev = 3 if smp else 5
            def do_half(hsx):
                cc0 = hsx * 4
                nch = min(4, ncn - cc0)
                nm = nch * 2
                yield ("WAIT", st["hk"] - 1)
                hb = st["hk"] % 2
                st["hk"] += 1
                last_half = hsx == (ncn + 3) // 4 - 1
                fns = []
                for c in range(nch):
                    for h in range(2):
                        mi = c * 2 + h
                        fns.append(lambda e, c=c, h=h, mi=mi: e.matmul(ps[:, mi * 128:(mi + 1) * 128], AKm[b][h][:, cc0 + c, :, :].rearrange("p a t -> p (a t)"),
                                                                       KR[b][:, cc0 + c, :, :].rearrange("p a t -> p (a t)"), start=True, stop=True))
                S.group("pe", fns, reads=[SIB[b]], writes=[g.PB[0], g.PB[1]])
                yield 1
                fns = []
                for c in range(nch):
                    for h in range(2):
                        mi = c * 2 + h
                        fns.append(lambda e, c=c, h=h, mi=mi: e.matmul(ps[0:64, 1024 + mi * 64:1024 + (mi + 1) * 64], KR[b][:, cc0 + c, 0, :], AKm[b][h][:, cc0 + c, 0, :], start=True, stop=True))
                S.group("pe", fns, reads=[SIB[b]], writes=[g.PB[2]])
                yield 1
                P1v = ps[:, 0:nm * 128].rearrange("p (m t) -> p m t", t=128)
                mb = lambda k, rows=slice(0, 128): masks[rows, mo + k:mo + k + 1, :].to_broadcast([rows.stop - rows.start, nm, 64])
                S.op("dve", lambda e, P1v=P1v, hb=hb, nm=nm: e.tensor_tensor(NaK[hb][:, 0:nm, :], P1v[:, :, 0:64], mb(0), ALU.mult), reads=[g.PB[0], g.PB[1], g.CB], writes=[NKB[hb]])
                S.op("dve", lambda e, P1v=P1v, hb=hb, nm=nm: e.tensor_tensor(M12[hb][:, 0:nm, :], P1v[:, :, 64:128], mb(1), ALU.mult), reads=[g.PB[0], g.PB[1], g.CB], writes=[NKB[hb]])
                S.op("dve", lambda e, P1v=P1v, nm=nm: e.tensor_tensor(Xb[0][0:64, 0:nm, :], P1v[0:64, :, 0:64], mb(2, slice(0, 64)), ALU.mult), reads=[g.PB[0], g.PB[1], g.CB], writes=[XB_[0]])
                S.op("dve", lambda e, nm=nm: e.tensor_tensor(XTb[0][0:64, 0:nm, :], ps[0:64, 1024:1024 + nm * 64].rearrange("p (m t) -> p m t", t=64), mb(3, slice(0, 64)), ALU.mult),
                     reads=[g.PB[2], g.CB], writes=[XB_[0]])
                S.op("dve", lambda e, nm=nm: e.tensor_tensor(Pb[0][0:64, 0:nm, :], Xb[0][0:64, 0:nm, :], identb[0:64, 0:64].unsqueeze(1).to_broadcast([64, nm, 64]), ALU.add),
                     reads=[XB_[0], g.CB], writes=[PBF[0]])
                yield 1
                xc, pc = 0, 0
                for lv in range(nlev):
                    last = lv == nlev - 1
                    xn = 1 - xc
                    fns = []
                    for mi in range(nm):
                        if not last:
                            fns.append(lambda e, mi=mi, xc=xc: e.matmul(ps[0:64, mi * 64:(mi + 1) * 64], XTb[xc][0:64, mi, :], Xb[xc][0:64, mi, :], start=True, stop=True))
                        fns.append(lambda e, mi=mi, xc=xc: e.matmul(ps[0:64, 512 + mi * 64:512 + (mi + 1) * 64], Xb[xc][0:64, mi, :], XTb[xc][0:64, mi, :], start=True, stop=True))
                    S.group("pe", fns, reads=[XB_[xc]], writes=[g.PB[0], g.PB[1]])
                    yield 1
                    if not last:
                        S.op("act", lambda e, xn=xn, nm=nm: e.activation(Xb[xn][0:64, 0:nm, :], ps[0:64, 0:nm * 64].rearrange("p (m t) -> p m t", t=64), AF.Copy), reads=[g.PB[0]], writes=[XB_[xn]])
                    S.op("dve", lambda e, xn=xn, nm=nm: e.tensor_copy(XTb[xn][0:64, 0:nm, :], ps[0:64, 512:512 + nm * 64].rearrange("p (m t) -> p m t", t=64)), reads=[g.PB[1]], writes=[XB_[xn]])
                    yield 1
                    pn = 1 - pc
                    S.group("pe", [lambda e, mi=mi, xn=xn, pc=pc: e.matmul(ps[0:64, 1024 + mi * 64:1024 + (mi + 1) * 64], XTb[xn][0:64, mi, :], Pb[pc][0:64, mi, :], start=True, stop=True)
                                   for mi in range(nm)], reads=[XB_[xn], PBF[pc]], writes=[g.PB[2]])
                    yield 1
                    dst = Tinv[hb] if last else Pb[pn]
                    dstB = TIB[hb] if last else PBF[pn]
                    S.op("dve", lambda e, dst=dst, pc=pc, nm=nm: e.tensor_tensor(dst[0:64, 0:nm, :], ps[0:64, 1024:1024 + nm * 64].rearrange("p (m t) -> p m t", t=64), Pb[pc][0:64, 0:nm, :], ALU.add),
                         reads=[g.PB[2], PBF[pc]], writes=[dstB])
                    xc, pc = xn, pn
                    yield 1
                pz = bank(g, 4, BF16)
                S.group("pe", [lambda e, c=c, pz=pz: e.transpose(pz[:, c * 128:(c + 1) * 128], AKd[b][:, cc0 + c, :, :].rearrange("p a t -> p (a t)"), identb) for c in range(nch)],
                        reads=[SIB[b], g.CB], writes=[g.PB[4]])
                zdst = lambda t, rows: bass.AP(t.tensor, t[rows, 0, 0, 0:1].offset, [list(t[rows, 0, 0, 0:1].ap[0]), [256, nch], [192, 2], [1, 64]])
                S.op("act", lambda e, pz=pz, hb=hb, nch=nch: e.activation(zdst(Zp4[hb], slice(0, 128)), pz[:, 0:nch * 128].rearrange("p (c h k) -> p c h k", h=2, k=64), AF.Copy),
                     reads=[g.PB[4]], writes=[ZPB[hb]])
                pz5 = bank(g, 5, BF16)
                S.group("pe", [lambda e, c=c, pz5=pz5: e.transpose(pz5[:, c * 128:(c + 1) * 128], mvb[b][:, (cc0 + c) * 64:(cc0 + c) * 64 + 128], identb) for c in range(nch)],
                        reads=[SIB[b], g.CB], writes=[g.PB[5]])
                pv5 = pz5[64:128, 0:nch * 128].rearrange("p (c h k) -> p c h k", h=2, k=64)
                S.op("act", lambda e, pv5=pv5, hb=hb: e.activation(zdst(Wp4[hb], slice(64, 128)), pv5, AF.Copy), reads=[g.PB[5]], writes=[WPB[hb]])
                S.op("act", lambda e, pv5=pv5, hb=hb, nch=nch: e.activation(Vp4[hb][64:128, 0:nch, :, :], pv5, AF.Copy), reads=[g.PB[5]], writes=[VPB[hb]])
                yield 1
                def do_chunk(c):
                    cur = st["cur"]
                    cg = cc0 + c
                    nxt = 1 - cur
                    pB = ps[0:64, 1536:1664]
                    if not smp:
                        fns = [lambda e, cg=cg, cur=cur, pB=pB: e.matmul(pB, KR[b][:, cg, 0, :], Sb[cur], start=True, stop=False)]
                        rd = [SIB[b], SBB[cur], NKB[hb], VPB[hb]]
                    else:
                        S.op("dve", lambda e, cg=cg: e.tensor_copy(Sb8[0:64, :, 0:64], S0T[0:64, cg * 8:(cg + 1) * 8, :]), reads=[S0B], writes=[SB8])
                        S.op("dve", lambda e, cg=cg: e.tensor_copy(Sb8[64:128, :, 64:128], S0T[64:128, cg * 8:(cg + 1) * 8, :]), reads=[S0B], writes=[SB8])
                        S.op("dve", lambda e, cg=cg: e.tensor_tensor(Kmask, KR[b][:, cg, 0:1, :].to_broadcast([128, 8, 64]), seqmask, ALU.mult), reads=[SIB[b], CB], writes=[KMB])
                        S.op("dve", lambda e, cg=cg: e.tensor_tensor(Rmask, KR[b][:, cg, 1:2, :].to_broadcast([128, 8, 64]), seqmask, ALU.mult), reads=[SIB[b], CB], writes=[KMB])
                        fns = [lambda e, s=s, pB=pB: e.matmul(pB, Kmask[:, s, :], Sb8[:, s, :], start=(s == 0), stop=False) for s in range(8)]
                        rd = [KMB, SB8, NKB[hb], VPB[hb]]
                    for h in range(2):
                        fns.append(lambda e, h=h, c=c, hb=hb: e.matmul(ps[0:64, 1536 + h * 64:1536 + (h + 1) * 64], NaK[hb][:, c * 2 + h, :], Vp4[hb][:, c, h, :], start=False, stop=True))
                    S.group("pe", fns, reads=rd, writes=[g.PB[3]])
                    S.op("act", lambda e, pB=pB: e.activation(Bsb[0:64, :], pB, AF.Copy), reads=[g.PB[3]], writes=[BSB])
                    yield 1
                    S.group("pe", [lambda e, h=h, c=c, hb=hb: e.matmul(ps[0:64, 1664 + h * 64:1664 + (h + 1) * 64], Tinv[hb][0:64, c * 2 + h, :], Bsb[0:64, h * 64:(h + 1) * 64], start=True, stop=True)
                                   for h in range(2)], reads=[TIB[hb], BSB], writes=[g.PB[3]])
                    wrow = Wp4[hb][0:64, c, 0, 0:1]
                    wdst = bass.AP(wrow.tensor, wrow.offset, [list(wrow.ap[0]), [192, 2], [1, 64]])
                    yield 1
                    S.op("dve", lambda e, wdst=wdst: e.tensor_scalar(wdst, ps[0:64, 1664:1792].rearrange("p (h v) -> p h v", h=2), -1.0, None, ALU.mult), reads=[g.PB[3]], writes=[WPB[hb]])
                    if not smp:
                        S.group("pe", [lambda e, h=h, c=c, hb=hb: e.matmul(ps[:, 1792:1856], Zp4[hb][:, c, h, :], Wp4[hb][:, c, h, h * 64:(h + 1) * 64], start=(h == 0), stop=(h == 1)) for h in range(2)],
                                reads=[ZPB[hb], WPB[hb]], writes=[g.PB[3]])
                    else:
                        for h in range(2):
                            S.op("dve", lambda e, h=h, c=c, hb=hb: e.tensor_tensor(Wm[:, h, :, :], Wp4[hb][:, c, h, h * 64:(h + 1) * 64].unsqueeze(1).to_broadcast([128, 8, 64]),
                                                                                   tokmask.unsqueeze(2).to_broadcast([128, 8, 64]), ALU.mult), reads=[WPB[hb], CB], writes=[WMB])
                        S.group("pe", [lambda e, h=h, c=c, hb=hb: e.matmul(ps[:, 3584:4096], Zp4[hb][:, c, h, :], Wm[:, h, :, :].rearrange("p s v -> p (s v)"), start=(h == 0), stop=(h == 1)) for h in range(2)],
                                reads=[ZPB[hb], WMB], writes=[g.PB[7]])
                    yield 1
                    if own:
                        ycol = 3072 + cg * 64
                        if not smp:
                            fns = [lambda e, cg=cg, cur=cur, ycol=ycol: e.matmul(ps[:, ycol:ycol + 64], Sb[cur], KR[b][:, cg, 1, :], start=True, stop=False)]
                            rd = [SIB[b], SBB[cur], NKB[hb], WPB[hb]]
                        else:
                            fns = [lambda e, s=s, ycol=ycol: e.matmul(ps[:, ycol:ycol + 64], Sb8[:, s, :], Rmask[:, s, :], start=(s == 0), stop=False) for s in range(8)]
                            rd = [KMB, SB8, NKB[hb], WPB[hb]]
                        for h in range(2):
                            fns.append(lambda e, h=h, c=c, hb=hb, ycol=ycol: e.matmul(ps[:, ycol:ycol + 64], Wp4[hb][:, c, h, :], M12[hb][:, c * 2 + h, :], start=False, stop=(h == 1)))
                        S.group("pe", fns, reads=rd, writes=[g.PB[6]])
                    if not smp:
                        S.op("dve", lambda e, cg=cg, b=b: e.scalar_tensor_tensor(Sf, Sf, Gc[b][:, cg:cg + 1], ps[:, 1792:1856], ALU.mult, ALU.add), reads=[SFB, SIB[b], g.PB[3]], writes=[SFB])
                        S.op("act", lambda e, nxt=nxt: e.activation(Sb[nxt][0:64, 0:64], Sf[0:64, :], AF.Copy), reads=[SFB], writes=[SBB[nxt]])
                        S.op("act", lambda e, nxt=nxt: e.activation(Sb[nxt][64:128, 64:128], Sf[64:128, :], AF.Copy), reads=[SFB], writes=[SBB[nxt]])
                        st["cur"] = nxt
                    else:
                        S.op("dve", lambda e, cg=cg, b=b: e.tensor_tensor(Snew[:, cg * 8:(cg + 1) * 8, :], S0T[:, cg * 8:(cg + 1) * 8, :], Gc[b][:, cg * 8:(cg + 1) * 8].unsqueeze(2).to_broadcast([128, 8, 64]), ALU.mult),
                             reads=[S0B, SIB[b]], writes=[SNB])
                        S.op("dve", lambda e, cg=cg: e.tensor_tensor(Snew[:, cg * 8:(cg + 1) * 8, :], Snew[:, cg * 8:(cg + 1) * 8, :], ps[:, 3584:4096].rearrange("p (s v) -> p s v", v=64), ALU.add),
                             reads=[SNB, g.PB[7]], writes=[SNB])
                def seq_half():
                    for c_ in range(nch):
                        yield from do_chunk(c_)
                        yield 1
                    if own and last_half:
                        S.op("act", lambda e, oo=oo, N=N: e.activation(ybuf[:, oo:oo + N], ps[:, 3072:3072 + N], AF.Copy), reads=[g.PB[6]], writes=[YB])
                    st["done"] += 1
                seq_q.append(seq_half())
            for hsx_ in range((ncn + 3) // 4):
                yield from do_half(hsx_)

        seq_q = []

        def stream_b():
            for (t0_, N_, kind_) in segs:
                yield from do_seg(t0_, N_, kind_)

        B = stream_b()
        b_done = False
        b_wait = None
        a_cur = None
        while True:
            prog = False
            if a_cur is None and seq_q:
                a_cur = seq_q.pop(0)
            if a_cur is not None:
                for _ in range(2):
                    try:
                        next(a_cur)
                    except StopIteration:
                        a_cur = None
                        break
                prog = True
            if not b_done:
                if b_wait is not None and st["done"] >= b_wait:
                    b_wait = None
                if b_wait is None:
                    try:
                        r_ = next(B)
                        if isinstance(r_, tuple):
                            b_wait = r_[1]
                    except StopIteration:
                        b_done = True
                    prog = True
            if b_done and a_cur is None and not seq_q:
                break
            assert prog, "scheduler deadlock"
        S.op("pe", lambda e: e.transpose(ps[0:64, 2560:2688], Sf, identf), reads=[SFB, g.CB], writes=[g.PB[5]])
        S.op("act", lambda e: e.activation(stage[0:64, 0, :], ps[0:64, 2560:2688], AF.Copy), reads=[g.PB[5]], writes=[STG])
        S.dma("sp", g.o_wkv_p[2 * p:2 * p + 2, :, :].rearrange("h v k -> v h k"), stage[0:64, 0, :].rearrange("p (h k) -> p h k", h=2), reads=[STG])
        for q4 in range(4):
            pbq = 5 + q4 % 2
            S.group("pe", [lambda e, q4=q4, s=s, pbq=pbq: e.transpose(ps[0:64, pbq * 512 + s * 128:pbq * 512 + (s + 1) * 128], Snew[:, q4 * 4 + s, :], identf) for s in range(4)],
                    reads=[SNB, g.CB], writes=[g.PB[pbq]])
            S.op("act", lambda e, q4=q4, pbq=pbq: e.activation(stage[0:64, q4 * 4:(q4 + 1) * 4, :], ps[0:64, pbq * 512:(pbq + 1) * 512].rearrange("p (s k) -> p s k", k=128), AF.Copy),
                 reads=[g.PB[pbq]], writes=[STG])
            for s4 in range(4):
                S.dma("sp", g.o_wkv_s[q4 * 4 + s4, 2 * p:2 * p + 2, :, :].rearrange("h v k -> v h k"),
                      stage[0:64, q4 * 4 + s4, :].rearrange("p (h k) -> p h k", h=2), reads=[STG])
        for (o0, n) in [(0, 512), (512, 512), (1024, 128)]:
            pw = ps[:, 3584:3584 + n]
            yv = ybuf[:, o0:o0 + n]
            S.op("dve", lambda e, yv=yv, n=n: e.tensor_copy(sqb[:, 0:n], yv), reads=[YB], writes=[SQB])
            S.op("pe", lambda e, pw=pw, n=n: e.matmul(pw, bones, sqb[:, 0:n], start=True, stop=True), reads=[SQB, g.CB], writes=[g.PB[7]])
            S.op("dve", lambda e, pw=pw, yv=yv, n=n: e.scalar_tensor_tensor(tmp[0][:, 0:n], pw, -1.0 / 64, yv, ALU.mult, ALU.add), reads=[g.PB[7], YB], writes=[TB_[0]])
            S.op("act", lambda e, n=n: e.activation(sqb[:, 0:n], tmp[0][:, 0:n], AF.Square), reads=[TB_[0]], writes=[SQB])
            S.op("pe", lambda e, pw=pw, n=n: e.matmul(pw, bones, sqb[:, 0:n], start=True, stop=True), reads=[SQB, g.CB], writes=[g.PB[7]])
            S.op("act", lambda e, pw=pw, n=n: e.activation(tmp[1][:, 0:n], pw, AF.Sqrt, scale=1.0 / 64, bias=64e-5), reads=[g.PB[7]], writes=[TB_[1]])
            S.op("dve", lambda e, n=n: e.reciprocal(tmp[1][:, 0:n], tmp[1][:, 0:n]), reads=[TB_[1]], writes=[TB_[1]])
            S.op("dve", lambda e, n=n: e.tensor_tensor(tmp[0][:, 0:n], tmp[0][:, 0:n], tmp[1][:, 0:n], ALU.mult), reads=[TB_[0], TB_[1]], writes=[TB_[0]])
            S.op("dve", lambda e, n=n: e.tensor_scalar(tmp[0][:, 0:n], tmp[0][:, 0:n], lnx_g[:, p:p + 1], lnx_b[:, p:p + 1], ALU.mult, ALU.add), reads=[TB_[0], CB], writes=[TB_[0]])
            S.op("dve", lambda e, n=n, o0=o0: e.tensor_tensor(tmp[0][:, 0:n], tmp[0][:, 0:n], bonus[:, o0:o0 + n], ALU.add), reads=[TB_[0], BNB], writes=[TB_[0]])
            S.op("dve", lambda e, n=n, o0=o0: e.tensor_tensor(oTs[p % 2][:, o0:o0 + n], tmp[0][:, 0:n], gbuf[:, o0:o0 + n], ALU.mult), reads=[TB_[0], GB], writes=[OSG[p % 2]])
        S.dma("sp", g.oTd[p * 128:(p + 1) * 128, :], oTs[p % 2], reads=[OSG[p % 2]], writes=[g.OTB])

    for p_ in range(16 if g.stop > 4 else 1):
        do_pair(p_)


def phase_m3(g):
    S, A = g.S, g.A
    ps = g.ps
    identf = g.identf
    mk = A.mark()
    CB = Buf("m3c")
    pscale = A.alloc([16], F32)
    S.dma("sp", pscale, g.pool_scale, writes=[CB])
    spT = A.alloc([16, 240], F32)
    SPB = Buf("spT")
    strow = A.alloc([2, RW], F32)
    SRB = Buf("strow")
    spf = g.st_pool.rearrange("s i c -> (s i) c")
    S.dma("sp", strow[:, 0, :], spf[0:128, :], writes=[SRB])
    S.dma("sp", strow[0:112, 1, :], spf[128:240, :], writes=[SRB])
    for j in range(16):
        pb = j % 4
        S.group("pe", [lambda e, j=j, pb=pb: e.transpose(ps[:, pb * 512:pb * 512 + 128], strow[:, 0, j * 128:(j + 1) * 128], identf),
                       lambda e, j=j, pb=pb: e.transpose(ps[:, pb * 512 + 128:pb * 512 + 240], strow[0:112, 1, j * 128:(j + 1) * 128], identf[0:112, 0:112])],
                reads=[SRB, g.CB], writes=[g.PB[pb]])
        S.op("act", lambda e, j=j, pb=pb: e.activation(spT[:, j, :], ps[:, pb * 512:pb * 512 + 240], AF.Copy), reads=[g.PB[pb]], writes=[SPB])
    S.dma("sp", g.o_pool_s[:, 0:7, :], g.st_pool[:, 8:15, :])
    LA = 15 + 1024
    arr = [A.alloc([LA], F32) for _ in range(3)]
    ar2 = [A.alloc([16, 23], F32) for _ in range(3)]
    ARB = [Buf("arr%d" % i) for i in range(3)]
    invc = A.alloc([OWN], F32)
    IVB = Buf("invc")
    tmpf = A.alloc([OWN], F32)
    TMB = Buf("tmpf")
    dT4 = A.alloc([4, OWN], BF16)
    DTB = Buf("dT4")
    wp = A.alloc([4, 512], BF16)
    WPB = Buf("wp")
    ppst = A.alloc([RW], F32)
    psst = A.alloc([RW], F32)
    PST = Buf("poolstage")
    ost = [A.alloc([OWN], BF16) for _ in range(2)]
    OST = [Buf("ost0"), Buf("ost1")]
    oi = 0
    for j in range(16):
        gi = j // 4
        r0 = 6592 + j * 128
        a0_, a2 = arr[0], ar2[0]
        S.dma("sp", a0_, g.zT[r0:r0 + 128, 1009:2048], reads=[g.ZB], writes=[ARB[0]])
        S.dma("sp", a2[:, :, 15:23], g.zT[r0:r0 + 128, 2048:NT].rearrange("p (s t) -> p s t", t=8), reads=[g.ZB], writes=[ARB[0]])
        S.op("dve", lambda e, a2=a2, j=j: e.tensor_copy(a2[:, :, 0:15], spT[:, j, :].rearrange("p (s i) -> p s i", i=15)), reads=[SPB], writes=[ARB[0]])
        if gi != (j - 1) // 4 or j == 0:
            S.dma("sp", invc, g.c_invcnt[gi:gi + 1, :].partition_broadcast(128) if False else g.c_invcnt[gi, :].partition_broadcast(128), writes=[IVB])
        pb = 4 + j % 2
        S.op("dve", lambda e, a2=a2: e.tensor_copy(tmpf[:, 0:128].rearrange("p (s t) -> p s t", t=8), a2[:, :, 15:23]), reads=[ARB[0]], writes=[TMB])
        S.group("pe", [lambda e, a0_=a0_, pb=pb: e.transpose(ps[0:15, pb * 512:pb * 512 + 128], a0_[:, 1024:1039], identf),
                       lambda e, pb=pb: e.transpose(ps[:, pb * 512 + 128:pb * 512 + 256], tmpf[:, 0:128], identf)],
                reads=[ARB[0], TMB, g.CB], writes=[g.PB[pb]])
        S.op("act", lambda e, j=j, pb=pb: e.activation(ppst[0:15, j * 128:(j + 1) * 128], ps[0:15, pb * 512:pb * 512 + 128], AF.Copy), reads=[g.PB[pb]], writes=[PST])
        S.op("act", lambda e, j=j, pb=pb: e.activation(psst[:, j * 128:(j + 1) * 128], ps[:, pb * 512 + 128:pb * 512 + 256], AF.Copy), reads=[g.PB[pb]], writes=[PST])
        cur = 0
        for k in range(gi + 1):
            sh = 1 << k
            nxt = 1 + (k % 2)
            S.op("dve", lambda e, cur=cur, nxt=nxt, sh=sh: e.tensor_tensor(arr[nxt][:, sh:LA], arr[cur][:, sh:LA], arr[cur][:, 0:LA - sh], ALU.add), reads=[ARB[cur]], writes=[ARB[nxt]])
            S.op("dve", lambda e, cur=cur, nxt=nxt, sh=sh: e.tensor_tensor(ar2[nxt][:, :, sh:23], ar2[cur][:, :, sh:23], ar2[cur][:, :, 0:23 - sh], ALU.add), reads=[ARB[cur]], writes=[ARB[nxt]])
            cur = nxt
        S.op("dve", lambda e, cur=cur: e.tensor_tensor(tmpf[:, 0:1024], arr[cur][:, 15:LA], invc[:, 0:1024], ALU.mult), reads=[ARB[cur], IVB], writes=[TMB])
        S.op("dve", lambda e, cur=cur: e.tensor_tensor(tmpf[:, 1024:OWN].rearrange("p (s t) -> p s t", t=8), ar2[cur][:, :, 15:23], invc[:, 1024:OWN].rearrange("p (s t) -> p s t", t=8), ALU.mult),
             reads=[ARB[cur], IVB], writes=[TMB])
        S.op("dve", lambda e, j=j, a0_=a0_: e.tensor_tensor(dT4[:, j % 4, 0:1024], tmpf[:, 0:1024], a0_[:, 15:LA], ALU.subtract), reads=[TMB, ARB[0]], writes=[DTB])
        S.op("dve", lambda e, j=j, a2=a2: e.tensor_tensor(dT4[:, j % 4, 1024:OWN].rearrange("p (s t) -> p s t", t=8), tmpf[:, 1024:OWN].rearrange("p (s t) -> p s t", t=8), a2[:, :, 15:23], ALU.subtract),
             reads=[TMB, ARB[0]], writes=[DTB])
        if j % 4 == 3:
            S.dma("pool", wp, g.w_pool[gi].rearrange("(c p) e -> p c e", p=128), writes=[WPB])
            for e_ in range(4):
                ob = oi % 2
                oi += 1
                for ti, (t0, tn) in enumerate([(0, 512), (512, 512), (1024, 128)]):
                    pb = ti % 4
                    S.group("pe", [lambda e, cc=cc, e_=e_, t0=t0, tn=tn, pb=pb: e.matmul(ps[:, pb * 512:pb * 512 + tn], wp[:, cc, e_ * 128:(e_ + 1) * 128], dT4[:, cc, t0:t0 + tn],
                                                                                         start=(cc == 0), stop=(cc == 3)) for cc in range(4)],
                            reads=[WPB, DTB], writes=[g.PB[pb]])
                    S.op("act", lambda e, e_=e_, t0=t0, tn=tn, pb=pb, ob=ob, gi=gi: e.activation(ost[ob][:, t0:t0 + tn], ps[:, pb * 512:pb * 512 + tn], AF.Copy, scale=pscale[:, gi * 4 + e_:gi * 4 + e_ + 1]),
                         reads=[g.PB[pb], CB], writes=[OST[ob]])
                rr = 2048 + (gi * 4 + e_) * 128
                S.dma("sp", g.oTd[rr:rr + 128, :], ost[ob], reads=[OST[ob]], writes=[g.OTB])
    S.dma("sp", g.o_pool_p, ppst[0:15, :], reads=[PST])
    for s in range(16):
        S.dma("sp", g.o_pool_s[s, 7:15, :], psst[s * 8:(s + 1) * 8, :], reads=[PST])
    A.release(mk)


def gemm_resid(g, actT, AB, nk, w_ap, x_src_fn, x_dst_fn, XR, XW, col_block=512):
    S, A = g.S, g.A
    ps = g.ps
    mk = A.mark()
    wb = [A.alloc([nk, col_block], BF16) for _ in range(2)]
    WB = [Buf("wo0"), Buf("wo1")]
    xt = [A.alloc([col_block], F32) for _ in range(4)]
    XT = [Buf("xt%d" % i) for i in range(4)]
    w_v = w_ap.rearrange("(kc p) c -> p kc c", p=128)
    ncb = D // col_block
    xi = 0
    for cb in range(ncb):
        b = cb % 2
        cs = slice(cb * col_block, (cb + 1) * col_block)
        step = max(1, nk // 4)
        for q0 in range(0, nk, step):
            S.dma("pool", wb[b][:, q0:q0 + step, :], w_v[:, q0:q0 + step, cs], writes=[WB[b]])
        for i in range(9):
            x = xi % 4
            xi += 1
            pb = xi % 4
            S.dma("sp", xt[x], x_src_fn(i, cs), reads=[XR], writes=[XT[x]])
            S.group("pe", [lambda e, kc=kc, i=i, b=b, pb=pb: e.matmul(ps[:, pb * 512:pb * 512 + col_block], actT[:, kc, i * 128:(i + 1) * 128], wb[b][:, kc, :],
                                                                      start=(kc == 0), stop=(kc == nk - 1)) for kc in range(nk)],
                    reads=[AB, WB[b]], writes=[g.PB[pb]])
            S.op("dve", lambda e, x=x, pb=pb: e.tensor_tensor(xt[x], xt[x], ps[:, pb * 512:pb * 512 + col_block], ALU.add), reads=[XT[x], g.PB[pb]], writes=[XT[x]])
            S.dma("sp", x_dst_fn(i, cs), xt[x], reads=[XT[x]], writes=[Buf("xw")])
    A.release(mk)


def phase_m4(g):
    S, A = g.S, g.A
    mk = A.mark()
    oT = A.alloc([NCH, OWN], BF16)
    OB = Buf("oT")
    ov = g.oTd.rearrange("(kc p) t -> p kc t", p=128)
    for q in range(4):
        S.dma("sp", oT[:, q * 8:(q + 1) * 8, :], ov[:, q * 8:(q + 1) * 8, :], reads=[g.OTB], writes=[OB])
    g.XSB = Buf("xs")
    XIN = Buf("xin")
    gemm_resid(g, oT, OB, NCH, g.w_out, lambda i, cs: g.xin[1024 + i * 128:1024 + (i + 1) * 128, cs],
               lambda i, cs: g.xs[i * 128:(i + 1) * 128, cs], XIN, g.XSB)
    A.release(mk)


def proj_T(g, hT, HB, ntok, w_ap, col_lo, ncols, evac, blk=256):
    S, A = g.S, g.A
    ps = g.ps
    mk = A.mark()
    wb = [A.alloc([NCH, blk], BF16) for _ in range(2)]
    WB = [Buf("pw0"), Buf("pw1")]
    w_v = w_ap.rearrange("(kc p) c -> p kc c", p=128)
    tg = []
    t = 0
    while t < ntok:
        n = min(512, ntok - t)
        tg.append((t, n))
        t += n
    pi = 0
    for bi in range(ncols // blk):
        b = bi % 2
        lo = col_lo + bi * blk
        for q in range(4):
            S.dma("pool", wb[b][:, q * 8:(q + 1) * 8, :], w_v[:, q * 8:(q + 1) * 8, lo:lo + blk], writes=[WB[b]])
        for cj in range(blk // 128):
            j = (bi * blk) // 128 + cj
            for (t0, tn) in tg:
                pb = pi % 4
                pi += 1
                S.group("pe", [lambda e, kc=kc, b=b, cj=cj, t0=t0, tn=tn, pb=pb: e.matmul(ps[:, pb * 512:pb * 512 + tn], wb[b][:, kc, cj * 128:(cj + 1) * 128], hT[:, kc, t0:t0 + tn],
                                                                                     start=(kc == 0), stop=(kc == NCH - 1)) for kc in range(NCH)],
                        reads=[WB[b], HB], writes=[g.PB[pb]])
                evac(j, t0, tn, ps[:, pb * 512:pb * 512 + tn], g.PB[pb])
    A.release(mk)


def phase_x(g):
    S, A = g.S, g.A
    ps = g.ps
    identb = g.identb
    mk = A.mark()
    gv = A.alloc([NCH], F32)
    gm = A.alloc([NCH], F32)
    GV = Buf("gvx")
    S.dma("sp", gv, g.norm_xa_g, writes=[GV])
    S.dma("sp", gm, g.norm_mem_g, writes=[GV])
    qmask = A.alloc([16, 128], BF16)
    S.dma("pool", qmask, g.c_qmask, writes=[GV])
    qT = A.alloc([4, OWN], BF16)
    QB = Buf("qT")
    mkT = A.alloc([4, 256], BF16)
    KB = Buf("mkT")
    mvb = A.alloc([2, 512], BF16)
    VB = Buf("mvb")
    mkh = A.mark()
    hT = A.alloc([NCH, OWN], BF16)
    HB = Buf("hT2")
    xs_t = g.xs.rearrange("(n p) d -> n p d", p=128)
    norm_transpose(g, [xs_t[i] for i in range(9)], 9, gv, GV, hT, HB, 0)
    sc_ = 128.0 ** -0.5

    def ev_q(j, t0, tn, pap, PBf):
        S.op("act", lambda e: e.activation(qT[:, j, t0:t0 + tn], pap, AF.Copy, scale=sc_), reads=[PBf], writes=[QB])
    proj_T(g, hT, HB, OWN, g.w_xq, 0, 512, ev_q)
    A.release(mkh)
    if g.stop <= 7.1:
        return
    mT = A.alloc([NCH, 256], BF16)
    MB = Buf("mT")
    mem_t = g.mem.rearrange("(n p) d -> n p d", p=128)
    norm_transpose(g, [mem_t[i] for i in range(2)], 2, gm, GV, mT, MB, 0)
    if g.stop <= 7.11:
        return

    def ev_k(j, t0, tn, pap, PBf):
        S.op("act", lambda e: e.activation(mkT[:, j, t0:t0 + tn], pap, AF.Copy), reads=[PBf], writes=[KB])
    proj_T(g, mT, MB, 256, g.w_mk, 0, 512, ev_k)
    if g.stop <= 7.12:
        return
    mk2 = A.mark()
    wb = A.alloc([NCH, 512], BF16)
    WBF = Buf("wkv")
    stf = [A.alloc([512], F32) for _ in range(2)]
    SF = [Buf("stf0"), Buf("stf1")]
    for wi, (w_ap, o_ap) in enumerate([(g.w_mk, g.o_mk), (g.w_mv, g.o_mv)]):
        w_v = w_ap.rearrange("(kc p) c -> p kc c", p=128)
        for q in range(4):
            S.dma("pool", wb[:, q * 8:(q + 1) * 8, :], w_v[:, q * 8:(q + 1) * 8, :], writes=[WBF])
        for mt in range(2):
            pb = mt
            S.group("pe", [lambda e, kc=kc, mt=mt, pb=pb: e.matmul(ps[:, pb * 512:(pb + 1) * 512], mT[:, kc, mt * 128:(mt + 1) * 128], wb[:, kc, :], start=(kc == 0), stop=(kc == NCH - 1))
                           for kc in range(NCH)], reads=[MB, WBF], writes=[g.PB[pb]])
            S.op("act", lambda e, mt=mt, pb=pb: e.activation(stf[mt], ps[:, pb * 512:(pb + 1) * 512], AF.Copy), reads=[g.PB[pb]], writes=[SF[mt]])
            if wi == 1:
                S.op("dve", lambda e, mt=mt: e.tensor_copy(mvb[:, mt, :], stf[mt]), reads=[SF[mt]], writes=[VB])
            S.dma("sp", o_ap[mt * 128:(mt + 1) * 128, :], stf[mt], reads=[SF[mt]])
        if g.stop <= 7.13:
            return
    A.release(mkh)
    if g.stop <= 7.2:
        return
    kT = A.alloc([16, 4, 256], BF16)
    KTB = Buf("kT")
    vb = A.alloc([16, 2, 512], BF16)
    VSB = Buf("vb")
    mk3 = A.mark()
    kb = A.alloc([16, 2, 512], BF16)
    KBB = Buf("kb")
    for s in range(16):
        S.dma("pool", kb[:, s, :, :], g.ck[s].rearrange("(c p) e -> p c e", p=128), writes=[KBB])
        S.dma("pool", vb[:, s, :, :], g.cv[s].rearrange("(c p) e -> p c e", p=128), writes=[VSB])
    for s in range(16):
        pb = 4 + s % 2
        pz = bank(g, pb, BF16)
        S.group("pe", [lambda e, s=s, hd=hd, mc=mc, pz=pz: e.transpose(pz[:, (hd * 2 + mc) * 128:(hd * 2 + mc + 1) * 128], kb[:, s, mc, hd * 128:(hd + 1) * 128], identb)
                       for hd in range(4) for mc in range(2)], reads=[KBB, g.CB], writes=[g.PB[pb]])
        S.op("act" if s % 2 == 0 else "dve",
             (lambda e, s=s, pz=pz: e.activation(kT[:, s, :, :], pz.rearrange("p (h m) -> p h m", h=4), AF.Copy)) if s % 2 == 0 else
             (lambda e, s=s, pz=pz: e.tensor_copy(kT[:, s, :, :], pz.rearrange("p (h m) -> p h m", h=4))), reads=[g.PB[pb]], writes=[KTB])
    A.release(mk3)
    if g.stop <= 7.3:
        return
    aoT = A.alloc([4, OWN], BF16)
    AOB = Buf("aoT")
    qm = A.alloc([4, 16, 128], BF16)
    QMB = Buf("qm")
    Pf = A.alloc([4, 256], F32)
    PFB = Buf("Pf")
    Pb_ = A.alloc([4, 256], BF16)
    PBB = Buf("Pb")
    PT = A.alloc([4, 2, 128], BF16)
    PTB = Buf("PT")
    stt = A.alloc([16], F32)
    STB = Buf("stt")
    for i in range(9):
        ts = slice(i * 128, (i + 1) * 128)
        smp = i == 8
        if g.stop <= 7.4 and smp:
            return
        if smp:
            S.op("dve", lambda e: e.tensor_tensor(qm, qT[:, :, 1024:1152].unsqueeze(2).to_broadcast([128, 4, 16, 128]), qmask.unsqueeze(1).to_broadcast([128, 4, 16, 128]), ALU.mult),
                 reads=[QB, GV], writes=[QMB])
        for hd in range(4):
            pb = hd // 2
            o = pb * 512 + (hd % 2) * 256
            if not smp:
                S.op("pe", lambda e, hd=hd, ts=ts, o=o: e.matmul(ps[:, o:o + 256], qT[:, hd, ts], mkT[:, hd, :], start=True, stop=True), reads=[QB, KB], writes=[g.PB[pb]])
            else:
                S.group("pe", [lambda e, hd=hd, s=s, o=o: e.matmul(ps[:, o:o + 256], qm[:, hd, s, :], kT[:, s, hd, :], start=(s == 0), stop=(s == 15)) for s in range(16)],
                        reads=[QMB, KTB], writes=[g.PB[pb]])
        scv = ps[:, 0:1024].rearrange("p (h m) -> p h m", h=4)
        S.op("dve", lambda e, scv=scv: e.tensor_reduce(stt[:, 0:4], scv, AX.X, ALU.max), reads=[g.PB[0], g.PB[1]], writes=[STB])
        S.op("dve", lambda e: e.tensor_scalar(stt[:, 4:8], stt[:, 0:4], -1.0, None, ALU.mult), reads=[STB], writes=[STB])
        for hd in range(4):
            S.op("act", lambda e, hd=hd, scv=scv: e.activation(Pf[:, hd, :], scv[:, hd, :], AF.Exp, bias=stt[:, 4 + hd:5 + hd], accum_out=stt[:, 8 + hd:9 + hd]),
                 reads=[g.PB[0], g.PB[1], STB], writes=[PFB, STB])
        S.op("dve", lambda e: e.reciprocal(stt[:, 12:16], stt[:, 8:12]), reads=[STB], writes=[STB])
        S.op("dve", lambda e: e.tensor_tensor(Pb_, Pf, stt[:, 12:16].unsqueeze(2).to_broadcast([128, 4, 256]), ALU.mult), reads=[PFB, STB], writes=[PBB])
        pz = bank(g, 2, BF16)
        S.group("pe", [lambda e, hd=hd, mc=mc, pz=pz: e.transpose(pz[:, (hd * 2 + mc) * 128:(hd * 2 + mc + 1) * 128], Pb_[:, hd, mc * 128:(mc + 1) * 128], identb)
                       for hd in range(4) for mc in range(2)], reads=[PBB, g.CB], writes=[g.PB[2]])
        S.op("act", lambda e, pz=pz: e.activation(PT, pz.rearrange("p (h c t) -> p h c t", h=4, c=2), AF.Copy), reads=[g.PB[2]], writes=[PTB])
        po = ps[:, 1536:2048]
        if not smp:
            fns = [lambda e, hd=hd, mc=mc: e.matmul(ps[:, 1536 + hd * 128:1536 + (hd + 1) * 128], mvb[:, mc, hd * 128:(hd + 1) * 128], PT[:, hd, mc, :], start=(mc == 0), stop=(mc == 1))
                   for hd in range(4) for mc in range(2)]
            S.group("pe", fns, reads=[VB, PTB], writes=[g.PB[3]])
        else:
            fns = [lambda e, hd=hd, mc=mc, s=s: e.matmul(ps[:, 1536 + hd * 128 + s * 8:1536 + hd * 128 + (s + 1) * 8], vb[:, s, mc, hd * 128:(hd + 1) * 128], PT[:, hd, mc, s * 8:(s + 1) * 8],
                                                         start=(mc == 0), stop=(mc == 1)) for hd in range(4) for s in range(16) for mc in range(2)]
            S.group("pe", fns, reads=[VSB, PTB], writes=[g.PB[3]])
        S.op("act", lambda e, ts=ts, po=po: e.activation(aoT[:, :, ts], po.rearrange("p (h t) -> p h t", h=4), AF.Copy), reads=[g.PB[3]], writes=[AOB])
    if g.stop <= 7.5:
        return
    g.XS2B = Buf("xs2")
    gemm_resid(g, aoT, AOB, 4, g.w_xo, lambda i, cs: g.xs[i * 128:(i + 1) * 128, cs], lambda i, cs: g.xs2[i * 128:(i + 1) * 128, cs], g.XSB, g.XS2B)
    A.release(mk)


def phase_f(g):
    S, A = g.S, g.A
    ps = g.ps
    mk = A.mark()
    gv = A.alloc([NCH], F32)
    GV = Buf("gvf")
    S.dma("sp", gv, g.norm_ffn_g, writes=[GV])
    hT = A.alloc([NCH, OWN], BF16)
    HB = Buf("hT3")
    x2_t = g.xs2.rearrange("(n p) d -> n p d", p=128)
    norm_transpose(g, [x2_t[i] for i in range(9)], 9, gv, GV, hT, HB, 0)
    hid = A.alloc([NCH, OWN], BF16)
    HDB = Buf("hid")
    rl = [A.alloc([512], F32) for _ in range(3)]
    RLB = [Buf("rl%d" % i) for i in range(3)]
    cnt = [0]
    bufs = [(g.xs2, g.XS2B), (g.xs, g.XSB)]
    for q in range(4):
        def ev_h(j, t0, tn, pap, PBf):
            r = cnt[0] % 3
            cnt[0] += 1
            S.op("act", lambda e: e.activation(rl[r][:, 0:tn], pap, AF.Relu), reads=[PBf], writes=[RLB[r]])
            S.op("dve", lambda e: e.tensor_tensor(hid[:, j, t0:t0 + tn], rl[r][:, 0:tn], rl[r][:, 0:tn], ALU.mult), reads=[RLB[r]], writes=[HDB])
        proj_T(g, hT, HB, OWN, g.w_up, q * D, D, ev_h)
        (src, SB_), (dst, DB_) = bufs[q % 2], bufs[(q + 1) % 2]
        gemm_resid(g, hid, HDB, NCH, g.w_down[q * D:(q + 1) * D, :], lambda i, cs, src=src: src[i * 128:(i + 1) * 128, cs],
                   lambda i, cs, dst=dst: dst[i * 128:(i + 1) * 128, cs], SB_, DB_, col_block=256)
    A.release(mk)
    mk = A.mark()
    gb = A.alloc([D], F32)
    GB = Buf("gfinal")
    S.dma("sp", gb, g.norm_final_row.partition_broadcast(128), writes=[GB])
    xt = [A.alloc([D], F32) for _ in range(2)]
    XB = [Buf("fx0"), Buf("fx1")]
    junk = A.alloc([D], BF16)
    JB = Buf("fjunk")
    st = A.alloc([9, 4], F32)
    STB = Buf("fst")
    x_t = g.xs2.rearrange("(n p) d -> n p d", p=128)
    y_t = g.y_out.rearrange("(n p) d -> n p d", p=128)
    for i in range(9):
        b = i % 2
        S.dma("sp", xt[b], x_t[i], reads=[g.XS2B], writes=[XB[b]])
        S.op("act", lambda e, b=b, i=i: e.activation(junk, xt[b], AF.Square, accum_out=st[:, i, 0:1]), reads=[XB[b]], writes=[JB, STB])
        S.op("act", lambda e, i=i: e.activation(st[:, i, 1:2], st[:, i, 0:1], AF.Sqrt, scale=1.0 / D, bias=1e-6), reads=[STB], writes=[STB])
        S.op("dve", lambda e, i=i: e.reciprocal(st[:, i, 2:3], st[:, i, 1:2]), reads=[STB], writes=[STB])
        S.op("dve", lambda e, b=b, i=i: e.scalar_tensor_tensor(xt[b], xt[b], st[:, i, 2:3], gb, ALU.mult, ALU.mult), reads=[XB[b], STB, GB], writes=[XB[b]])
        S.dma("sp", y_t[i], xt[b], reads=[XB[b]])
    A.release(mk)


_CACHE = {}


def kernel(**inputs):
    inp = {k: np.asarray(v) for k, v in inputs.items()}
    if "prog" not in _CACHE:
        _CACHE["prog"] = build_program()
    nc, g = _CACHE["prog"]
    consts = make_consts()
    in_maps = []
    for c in range(8):
        m = make_in_map(inp, c, consts)
        in_maps.append({k: m[k] for k in g.used_inputs})
    res = run_bass_kernel_spmd(nc, in_maps, core_ids=list(range(8)))
    R = res.results
    f32 = np.float32
    y_prompt = np.zeros((4, 2048, D), f32)
    y_sample = np.zeros((128, 8, D), f32)
    wkv_p = np.zeros((1, 4, 32, 64, 64), f32)
    sh_p = np.zeros((1, 4, SHIFT_W), f32)
    pl_p = np.zeros((1, 4, 15, RW), f32)
    mk_p = np.zeros((1, 4, 256, 4, 128), f32)
    mv_p = np.zeros((1, 4, 256, 4, 128), f32)
    wkv_s = np.zeros((1, 128, 32, 64, 64), f32)
    sh_s = np.zeros((1, 128, SHIFT_W), f32)
    pl_s = np.zeros((1, 128, 15, RW), f32)
    for c in range(8):
        b, half = c // 2, c % 2
        r = R[c]
        y = np.asarray(r["y_out"], f32)
        y_prompt[b, half * 1024:(half + 1) * 1024] = y[:1024]
        y_sample[16 * c:16 * c + 16] = y[1024:].reshape(16, 8, D)
        if half == 1:
            wkv_p[0, b] = r["o_wkv_p"]
            sh_p[0, b] = r["o_shift_p"]
            pl_p[0, b] = r["o_pool_p"]
        else:
            mk_p[0, b] = np.asarray(r["o_mk"]).reshape(256, 4, 128)
            mv_p[0, b] = np.asarray(r["o_mv"]).reshape(256, 4, 128)
        wkv_s[0, 16 * c:16 * c + 16] = r["o_wkv_s"]
        sh_s[0, 16 * c:16 * c + 16] = r["o_shift_s"]
        pl_s[0, 16 * c:16 * c + 16] = r["o_pool_s"]
    return (y_prompt, y_sample, wkv_p, sh_p, pl_p, mk_p, mv_p, wkv_s, sh_s, pl_s)
```

```python
import contextlib
import math
import numpy as np
import ml_dtypes
import concourse.bass as bass
import concourse.mybir as mybir
from concourse.bass_utils import run_bass_kernel_spmd

F32 = mybir.dt.float32
BF16 = mybir.dt.bfloat16
AF = mybir.ActivationFunctionType
ALU = mybir.AluOpType
AX = mybir.AxisListType

ENGS = ["pe", "act", "dve", "pool", "sp"]
D = 4096
NCH = 32
TA = 896
TB = 1280
NT = TA + TB
OWN = 1152
RW = 2048
SHIFT_W = 6592
IN_W = 8640
C0 = math.exp(-0.5)


class Buf:
    __slots__ = ("name", "w", "r")

    def __init__(self, name=""):
        self.name = name
        self.w = None
        self.r = {}


class Sched:
    def __init__(self, nc, es, ndma=24):
        self.nc = nc
        self.ops = {e: [] for e in ENGS}
        self.cnt = {e: 0 for e in ENGS}
        self.waited = {e: {} for e in ENGS}
        self.esem = {e: es.enter_context(nc.semaphore("s_" + e)) for e in ENGS}
        self.dsem = {}
        self.dnext = {}
        self.ndma = ndma
        for q in ["sp", "pool"]:
            self.dsem[q] = [es.enter_context(nc.semaphore("d_%s_%d" % (q, i))) for i in range(ndma)]
            self.dnext[q] = 0

    def sem_of(self, key):
        if isinstance(key, tuple):
            return self.dsem[key[0]][key[1]]
        return self.esem[key]

    def _deps(self, eng, reads, writes):
        need = {}

        def add(tok, same_ok):
            if tok is None:
                return
            k, v = tok
            if k == eng and (not same_ok or eng == "pe"):
                return
            if need.get(k, 0) < v:
                need[k] = v
        for b in reads:
            add(b.w, True)
        for b in writes:
            add(b.w, True)
            for k, v in b.r.items():
                add((k, v), False)
        for k, v in need.items():
            if self.waited[eng].get(k, 0) < v:
                self.waited[eng][k] = v
                self.ops[eng].append(("wait", k, v))

    def _mark(self, tok, reads, writes):
        k, v = tok
        for b in reads:
            if b.r.get(k, 0) < v:
                b.r[k] = v
        for b in writes:
            b.w = tok
            b.r = {}

    def op(self, eng, fn, reads=(), writes=()):
        return self.group(eng, [fn], reads, writes)

    def group(self, eng, fns, reads=(), writes=()):
        self._deps(eng, reads, writes)
        self.cnt[eng] += 1
        tok = (eng, self.cnt[eng])
        n = len(fns)
        for i, fn in enumerate(fns):
            self.ops[eng].append(("op", fn, i == n - 1))
        self._mark(tok, reads, writes)
        return tok

    def dma(self, q, out_ap, in_ap, reads=(), writes=()):
        self._deps(q, reads, writes)
        i = self.dnext[q]
        self.dnext[q] += 1
        slot = i % self.ndma
        k = i // self.ndma
        key = (q, slot)
        if k > 0 and self.waited[q].get(key, 0) < 16 * k:
            self.waited[q][key] = 16 * k
            self.ops[q].append(("wait", key, 16 * k))
        self.ops[q].append(("dma", out_ap, in_ap, key))
        tok = (key, 16 * (k + 1))
        self._mark(tok, reads, writes)
        return tok

    def _all_last(self):
        last = {}
        for q in ["sp", "pool"]:
            n = self.dnext[q]
            for slot in range(min(n, self.ndma)):
                uses = (n - 1 - slot) // self.ndma + 1
                last[(q, slot)] = 16 * uses
        for e in ["pe", "act", "dve", "pool"]:
            if self.cnt[e] > 0:
                last[e] = self.cnt[e]
        return last

    def barrier(self, engs=ENGS):
        last = self._all_last()
        for e in engs:
            for k, v in last.items():
                if k == e:
                    continue
                if self.waited[e].get(k, 0) < v:
                    self.waited[e][k] = v
                    self.ops[e].append(("wait", k, v))

    def emit(self):
        nc = self.nc
        self.barrier(["sp"])

        def run(name, e):
            sem = self.esem[name]
            for it in self.ops[name]:
                if it[0] == "wait":
                    e.wait_ge(self.sem_of(it[1]), it[2])
                elif it[0] == "op":
                    ins = it[1](e)
                    if it[2]:
                        ins.then_inc(sem, 1)
                else:
                    e.dma_start(out=it[1], in_=it[2]).then_inc(self.sem_of(it[3]), 16)

        with nc.Block() as block:
            @block.sync
            def _(e):
                run("sp", e)

            @block.tensor
            def _(e):
                run("pe", e)

            @block.scalar
            def _(e):
                run("act", e)

            @block.vector
            def _(e):
                run("dve", e)

            @block.gpsimd
            def _(e):
                run("pool", e)


class Arena:
    def __init__(self, nc, es, kbytes):
        self.t = es.enter_context(nc.sbuf_tensor("arena", [128, kbytes * 256], F32))
        self.cap = kbytes * 1024
        self.off = 0

    def alloc(self, shape, dt):
        esz = 2 if dt == BF16 else 4
        n = 1
        for s in shape:
            n *= s
        nb = (n * esz + 63) // 64 * 64
        assert self.off + nb <= self.cap, "SBUF arena overflow %d + %d" % (self.off, nb)
        ap = self.t[:, self.off // 4:(self.off + nb) // 4]
        self.off += nb
        if dt == BF16:
            ap = ap.bitcast(BF16)
        ap = ap[:, 0:n]
        if len(shape) == 2:
            ap = ap.rearrange("p (a b) -> p a b", a=shape[0])
        elif len(shape) == 3:
            ap = ap.rearrange("p (a b c) -> p a b c", a=shape[0], b=shape[1])
        elif len(shape) == 4:
            ap = ap.rearrange("p (a b c d) -> p a b c d", a=shape[0], b=shape[1], c=shape[2])
        return ap

    def mark(self):
        return self.off

    def release(self, m):
        self.S.barrier()
        self.off = m


IN_SHAPES = {
    "xin": [NT, D], "mem": [256, D], "st_wkv": [16, 32, 64, 64], "st_shift": [16, SHIFT_W], "st_pool": [16, 15, RW],
    "ck": [16, 256, 512], "cv": [16, 256, 512], "w_in": [D, IN_W], "w_out": [D, D], "w_pool": [4, 512, 512],
    "w_decay_up": [96, RW], "w_a_up": [96, RW], "w_g_up": [256, RW], "w_xq": [D, 512], "w_mk": [D, 512], "w_mv": [D, 512],
    "w_xo": [512, D], "w_up": [D, 4 * D], "w_down": [4 * D, D],
    "norm_mix_g": [128, 32], "norm_xa_g": [128, 32], "norm_mem_g": [128, 32], "norm_ffn_g": [128, 32], "norm_final_g": [128, 32],
    "mu_shift": [128, 52], "w0": [128, 16], "a0": [128, 16], "k_k": [128, 16], "k_a": [128, 16], "r_k": [128, 16],
    "lnx_g": [128, 16], "lnx_b": [128, 16], "pool_scale": [128, 16],
    "c_identf": [128, 128], "c_identb": [128, 128], "c_masks": [128, 8, 64], "c_bones": [128, 128],
    "norm_final_row": [D], "c_scanmask": [128, 640], "c_invcnt": [4, OWN], "c_seqmask": [128, 8, 64], "c_tokmask": [128, 8], "c_qmask": [128, 16, 128],
}


class K:
    def __getattr__(self, name):
        if name in IN_SHAPES:
            ap = self.nc.dram_tensor(name, list(IN_SHAPES[name]), F32, kind="ExternalInput").ap()
            self.used_inputs.append(name)
            setattr(self, name, ap)
            return ap
        raise AttributeError(name)


def build_program(dbg=None, stop=99):
    nc = bass.Bass("TRN2", target_bir_lowering=False)
    g = K()
    g.nc = nc
    g.used_inputs = []
    g.stop = stop
    I = lambda name, shape, dt=F32: nc.dram_tensor(name, list(shape), dt, kind="ExternalInput").ap()
    O = lambda name, shape, dt=F32: nc.dram_tensor(name, list(shape), dt, kind="ExternalOutput").ap()
    T = lambda name, shape, dt=F32: nc.dram_tensor(name, list(shape), dt).ap()
    g.y_out = O("y_out", [OWN, D])
    g.o_wkv_p = O("o_wkv_p", [32, 64, 64])
    g.o_shift_p = O("o_shift_p", [SHIFT_W])
    g.o_pool_p = O("o_pool_p", [15, RW])
    g.o_mk = O("o_mk", [256, 512])
    g.o_mv = O("o_mv", [256, 512])
    g.o_wkv_s = O("o_wkv_s", [16, 32, 64, 64])
    g.o_shift_s = O("o_shift_s", [16, SHIFT_W])
    g.o_pool_s = O("o_pool_s", [16, 15, RW])
    g.zT = T("zT", [IN_W, NT])
    g.oTd = T("oTd", [D, OWN], BF16)
    g.xs2 = T("xs2", [OWN, D])
    g.xs = T("xs", [OWN, D])
    g.dbg = dbg
    if dbg in ("m1", "m1s"):
        g.d_zT = O("d_zT", [IN_W, NT])
        g.d_hT = O("d_hT", [128, 32, 128])

    with contextlib.ExitStack() as es:
        S = Sched(nc, es)
        g.S = S
        g.A = Arena(nc, es, 204)
        g.A.S = S
        g.ps = es.enter_context(nc.psum_tensor("ps", [128, 4096], F32))
        g.PB = [Buf("psum%d" % i) for i in range(8)]
        setup_consts(g)
        g.A0 = g.A.off
        phase_m1(g)
        if dbg in ("m1", "m1s"):
            S.barrier()
            S.dma("sp", g.d_zT, g.zT)
        else:
            phase_m2(g)
            S.barrier()
            g.A.off = g.A0
            if g.stop > 5:
                phase_m3(g)
            if g.stop > 6:
                phase_m4(g)
            if g.stop > 7:
                phase_x(g)
            if g.stop > 8:
                phase_f(g)
            if dbg == "mix":
                S.barrier()
                S.dma("sp", g.y_out, g.xs if g.stop <= 7 else g.xs2)
        S.emit()
    return nc, g


def bank(g, i, dt=F32):
    ap = g.ps[:, i * 512:(i + 1) * 512]
    if dt == BF16:
        ap = ap.bitcast(BF16)
    return ap


def pvec(ap1d, n_chunks):
    return ap1d.rearrange("(c p) -> p c", p=128)


def setup_consts(g):
    S, A = g.S, g.A
    g.CB = Buf("consts")
    g.identf = A.alloc([128], F32)
    g.identb = A.alloc([128], BF16)
    g.bones = A.alloc([128], BF16)
    g.masks = A.alloc([8, 64], F32)
    S.dma("sp", g.identf, g.c_identf, writes=[g.CB])
    S.dma("pool", g.identb, g.c_identb, writes=[g.CB])
    S.dma("pool", g.bones, g.c_bones, writes=[g.CB])
    S.dma("sp", g.masks, g.c_masks, writes=[g.CB])


def norm_transpose(g, src_rows, ntiles, gvec, gB, hT, hB, tok0=0):
    S, A = g.S, g.A
    m = A.mark()
    xt = [A.alloc([D], F32) for _ in range(2)]
    XB = [Buf("xt0"), Buf("xt1")]
    junk = A.alloc([D], BF16)
    JB = Buf("junk")
    xsb = [A.alloc([D], BF16) for _ in range(2)]
    XS = [Buf("xs0"), Buf("xs1")]
    st = A.alloc([ntiles, 4], F32)
    SB = [Buf("st%d" % i) for i in range(ntiles)]
    for i in range(ntiles):
        b = i % 2
        S.dma("sp", xt[b], src_rows[i], writes=[XB[b]])
        S.op("act", lambda e, b=b, i=i: e.activation(junk, xt[b], AF.Square, accum_out=st[:, i, 0:1]),
             reads=[XB[b]], writes=[JB, SB[i]])
        S.op("act", lambda e, i=i: e.activation(st[:, i, 1:2], st[:, i, 0:1], AF.Sqrt, scale=1.0 / D, bias=1e-6),
             reads=[SB[i]], writes=[SB[i]])
        S.op("dve", lambda e, i=i: e.reciprocal(st[:, i, 2:3], st[:, i, 1:2]), reads=[SB[i]], writes=[SB[i]])
        S.op("act", lambda e, b=b, i=i: e.activation(xsb[b], xt[b], AF.Copy, scale=st[:, i, 2:3]),
             reads=[XB[b], SB[i]], writes=[XS[b]])
        for q in range(4):
            pb = 4 + (i * 4 + q) % 4
            pt = bank(g, pb, BF16)
            S.group("pe", [lambda e, b=b, q=q, j=j, pt=pt: e.transpose(pt[:, j * 128:(j + 1) * 128], xsb[b][:, (q * 8 + j) * 128:(q * 8 + j + 1) * 128], g.identb)
                           for j in range(8)], reads=[XS[b], g.CB], writes=[g.PB[pb]])
            S.op("dve", lambda e, q=q, i=i, pt=pt: e.tensor_tensor(
                hT[:, q * 8:(q + 1) * 8, tok0 + i * 128:tok0 + (i + 1) * 128],
                pt.rearrange("p (a b) -> p a b", a=8),
                gvec[:, q * 8:(q + 1) * 8].unsqueeze(2).to_broadcast([128, 8, 128]), ALU.mult),
                reads=[g.PB[pb], gB], writes=[hB])
    A.release(m)


def col_chunks(lo, hi):
    out = []
    c = lo
    while c < hi:
        if c == 6144 or c == 6240:
            sz = 96
        else:
            sz = 128
        out.append((c, sz))
        c += sz
    return out


def gemm_zT(g, hT, hB, ntok, tok_col0, col_blocks, tgroups):
    S, A = g.S, g.A
    m = A.mark()
    wb = [A.alloc([NCH, 512], BF16) for _ in range(2)]
    WB = [Buf("wb0"), Buf("wb1")]
    stg = [A.alloc([512], F32) for _ in range(4)]
    SG = [Buf("stg%d" % i) for i in range(4)]
    w_v = g.w_in.rearrange("(kc p) c -> p kc c", p=128)
    si = 0
    pi = 0
    for bi, (lo, hi) in enumerate(col_blocks):
        b = bi % 2
        nb = hi - lo
        for q4 in range(4):
            S.dma("pool", wb[b][:, q4 * 8:(q4 + 1) * 8, 0:nb], w_v[:, q4 * 8:(q4 + 1) * 8, lo:hi], writes=[WB[b]])
        for (c0, csz) in col_chunks(lo, hi):
            for (t0, tn) in tgroups:
                pb = pi % 4
                pi += 1
                pt = bank(g, pb)
                S.group("pe", [lambda e, b=b, kc=kc, c0=c0, csz=csz, t0=t0, tn=tn, pt=pt, lo=lo: e.matmul(
                    pt[0:csz, 0:tn], wb[b][:, kc, c0 - lo:c0 - lo + csz], hT[:, kc, t0:t0 + tn],
                    start=(kc == 0), stop=(kc == NCH - 1)) for kc in range(NCH)],
                    reads=[WB[b], hB], writes=[g.PB[pb]])
                s = si % 4
                si += 1
                eng = "act" if s % 2 == 0 else "dve"
                if eng == "act":
                    S.op("act", lambda e, s=s, csz=csz, tn=tn, pt=pt: e.activation(stg[s][0:csz, 0:tn], pt[0:csz, 0:tn], AF.Copy),
                         reads=[g.PB[pb]], writes=[SG[s]])
                else:
                    S.op("dve", lambda e, s=s, csz=csz, tn=tn, pt=pt: e.tensor_copy(stg[s][0:csz, 0:tn], pt[0:csz, 0:tn]),
                         reads=[g.PB[pb]], writes=[SG[s]])
                S.dma("sp", g.zT[c0:c0 + csz, tok_col0 + t0:tok_col0 + t0 + tn], stg[s][0:csz, 0:tn], reads=[SG[s]], writes=[Buf("zw")])
    A.release(m)


def phase_m1(g):
    S, A = g.S, g.A
    g.ZB = Buf("zT")
    m = A.mark()
    gv = A.alloc([NCH], F32)
    GV = Buf("gv")
    S.dma("sp", gv, g.norm_mix_g, writes=[GV])
    hT = A.alloc([NCH, TB], BF16)
    hB = Buf("hT")
    xin_t = g.xin.rearrange("(n p) d -> n p d", p=128)
    if g.dbg == "m1s":
        norm_transpose(g, [xin_t[7]], 1, gv, GV, hT, hB, 0)
        hf = A.alloc([32, 128], F32)
        HF = Buf("hf")
        S.op("dve", lambda e: e.tensor_copy(hf, hT[:, :, 0:128]), reads=[hB], writes=[HF])
        S.dma("sp", g.d_hT, hf, reads=[HF])
        gemm_zT(g, hT, hB, 128, TA, [(0, 512)], [(0, 128)])
        return
    norm_transpose(g, [xin_t[i] for i in range(7)], 7, gv, GV, hT, hB, 0)
    gemm_zT(g, hT, hB, TA, 0, [(2048, 2560), (2560, 3072), (3072, 3584), (3584, 4096),
                               (4096, 4608), (4608, 5120), (5120, 5632), (5632, 6144), (6144, 6336)],
            [(0, 512), (512, 384)])
    norm_transpose(g, [xin_t[7 + i] for i in range(10)], 10, gv, GV, hT, hB, 0)
    blocks = [(i * 512, (i + 1) * 512) for i in range(12)] + [(6144, 6592)] + [(6592 + i * 512, 6592 + (i + 1) * 512) for i in range(4)]
    gemm_zT(g, hT, hB, TB, TA, blocks, [(0, 512), (512, 512), (1024, 256)])
    A.release(m)


def make_consts():
    c = {}
    c["c_identf"] = np.eye(128, dtype=np.float32)
    c["c_identb"] = np.eye(128, dtype=np.float32)
    p = np.arange(128)[:, None]
    t = np.arange(64)[None, :]
    i = p % 64
    msu = (i < t).astype(np.float32)
    miu = (i <= t).astype(np.float32)
    msl = (i > t).astype(np.float32)
    blk = ((i // 8) == (t // 8)).astype(np.float32)
    c["c_masks"] = np.stack([msu, miu, -msu, -msl, msu * blk, miu * blk, -msu * blk, -msl * blk], axis=1).astype(np.float32)
    bo = np.zeros((128, 128), np.float32)
    bo[:64, :64] = 1
    bo[64:, 64:] = 1
    c["c_bones"] = bo
    sm = np.ones((128, 640), np.float32)
    sm[:, 0:512:64] = 0
    sm[:, 512::8] = 0
    c["c_scanmask"] = sm
    s8 = np.arange(8)[None, :, None]
    tt = np.arange(64)[None, None, :]
    c["c_seqmask"] = np.broadcast_to((tt // 8 == s8), (128, 8, 64)).astype(np.float32).copy()
    c["c_tokmask"] = ((i // 8) == np.arange(8)[None, :]).astype(np.float32)
    s16 = np.arange(16)[None, :, None]
    t128 = np.arange(128)[None, None, :]
    c["c_qmask"] = np.broadcast_to((t128 // 8 == s16), (128, 16, 128)).astype(np.float32).copy()
    return c


def make_in_map(inp, c, consts):
    b, half = c // 2, c % 2
    f = lambda a: np.ascontiguousarray(a, dtype=np.float32)
    m = dict(consts)
    xin = np.zeros((NT, D), np.float32)
    xp = inp["x_prompt"][b]
    if half == 1:
        xin[0:1024] = xp[0:1024]
    xin[1024:2048] = xp[half * 1024:(half + 1) * 1024]
    xin[2048:2176] = inp["x_sample"][16 * c:16 * c + 16].reshape(128, D)
    m["xin"] = xin
    m["mem"] = f(inp["mem_prompt"][b])
    m["st_wkv"] = f(inp["state_wkv"][0, 16 * c:16 * c + 16])
    m["st_shift"] = f(inp["state_shift"][0, 16 * c:16 * c + 16])
    m["st_pool"] = f(inp["state_pool"][0, 16 * c:16 * c + 16])
    m["ck"] = f(inp["cache_mem_k"][0, 16 * c:16 * c + 16].reshape(16, 256, 512))
    m["cv"] = f(inp["cache_mem_v"][0, 16 * c:16 * c + 16].reshape(16, 256, 512))
    for nm in ["w_in", "w_out", "w_pool", "w_decay_up", "w_a_up", "w_g_up", "w_xq", "w_mk", "w_mv", "w_xo", "w_up", "w_down",
               ]:
        m[nm] = f(inp[nm][0])
    pv = lambda a: np.ascontiguousarray(np.asarray(a, np.float32).reshape(-1, 128).T)
    for nm in ["norm_mix_g", "norm_xa_g", "norm_mem_g", "norm_ffn_g", "w0", "a0", "k_k", "k_a", "lnx_g", "lnx_b", "pool_scale", "r_k"]:
        m[nm] = pv(inp[nm][0])
    m["norm_final_g"] = pv(inp["norm_final_g"])
    m["norm_final_row"] = f(inp["norm_final_g"])
    mu = np.asarray(inp["mu_shift"][0], np.float32)
    mup = np.zeros((128, 52), np.float32)
    mup[:, 0:48] = mu[0:6144].reshape(48, 128).T
    mup[0:96, 48] = mu[6144:6240]
    mup[0:96, 49] = mu[6240:6336]
    mup[:, 50:52] = mu[6336:6592].reshape(2, 128).T
    m["mu_shift"] = mup
    pos0 = half * 1024
    ic = np.zeros((4, OWN), np.float32)
    for gi, w in enumerate((2, 4, 8, 16)):
        ic[gi, :1024] = 1.0 / np.minimum(w, pos0 + np.arange(1024) + 1)
        ic[gi, 1024:] = 1.0 / w
    m["c_invcnt"] = ic
    return m


def bc(ap, shape):
    return ap.to_broadcast(list(shape))


def phase_m2(g):
    S, A = g.S, g.A
    ps = g.ps
    g.OTB = Buf("oTd")
    CB = Buf("m2consts")
    ld = lambda shape, src, dt=F32, q="sp": (lambda t: (S.dma(q if dt == F32 else "pool", t, src, writes=[CB]), t)[1])(A.alloc(shape, dt))
    scanmask = ld([640], g.c_scanmask)
    seqmask = ld([8, 64], g.c_seqmask)
    tokmask = ld([8], g.c_tokmask)
    mu = ld([52], g.mu_shift)
    w0 = ld([16], g.w0)
    a0 = ld([16], g.a0)
    k_k = ld([16], g.k_k)
    k_a = ld([16], g.k_a)
    r_k = ld([16], g.r_k)
    lnx_g = ld([16], g.lnx_g)
    lnx_b = ld([16], g.lnx_b)
    omka = A.alloc([16], F32)
    S.op("dve", lambda e: e.tensor_scalar(omka, k_a, -1.0, 1.0, ALU.mult, ALU.add), reads=[CB], writes=[CB])
    wdu = A.alloc([RW], BF16)
    wau = A.alloc([RW], BF16)
    wgu = A.alloc([2, RW], BF16)
    S.dma("pool", wdu[0:96, :], g.w_decay_up, writes=[CB])
    S.dma("pool", wau[0:96, :], g.w_a_up, writes=[CB])
    S.dma("pool", wgu, g.w_g_up.rearrange("(c p) n -> p c n", p=128), writes=[CB])
    identf, identb, bones, masks = g.identf, g.identb, g.bones, g.masks
    CBs = [CB, g.CB]
    shst = A.alloc([2, 3, 128], F32)
    OSB = Buf("shst")
    cl = col_chunks(0, SHIFT_W)

    stT = A.alloc([52, 16], F32)
    STB = Buf("stT")
    mk0 = A.mark()
    strow = A.alloc([SHIFT_W], F32)
    SR = Buf("strow")
    S.dma("sp", strow[0:16, :], g.st_shift, writes=[SR])
    for half, (j0, j1) in enumerate([(0, 32), (32, 52)]):
        S.group("pe", [lambda e, j=j, half=half, j0=j0: e.transpose(ps[0:cl[j][1], half * 512 + (j - j0) * 16: half * 512 + (j - j0 + 1) * 16],
                                                                   strow[0:16, cl[j][0]:cl[j][0] + cl[j][1]], identf[0:16, 0:16])
                       for j in range(j0, j1)], reads=[SR, g.CB], writes=[g.PB[half]])
    S.op("dve", lambda e: e.tensor_copy(stT[:, 0:32, :], ps[:, 0:512].rearrange("p (a b) -> p a b", b=16)), reads=[g.PB[0]], writes=[STB])
    S.op("dve", lambda e: e.tensor_copy(stT[:, 32:48, :], ps[:, 512:768].rearrange("p (a b) -> p a b", b=16)), reads=[g.PB[1]], writes=[STB])
    S.op("dve", lambda e: e.tensor_copy(stT[0:96, 48:50, :], ps[0:96, 768:800].rearrange("p (a b) -> p a b", b=16)), reads=[g.PB[1]], writes=[STB])
    S.op("dve", lambda e: e.tensor_copy(stT[:, 50:52, :], ps[:, 800:832].rearrange("p (a b) -> p a b", b=16)), reads=[g.PB[1]], writes=[STB])
    A.release(mk0)
    if g.stop <= 1:
        return

    tw = A.alloc([NT], BF16)
    adm = A.alloc([NT], BF16)
    sg = A.alloc([2, NT], BF16)
    LRB = Buf("lowrank")
    mk1 = A.mark()
    LP = 1 + 2048 + 144
    for j, (dst, func) in enumerate([(tw, AF.Tanh), (adm, AF.Copy), (sg[:, 0, :], AF.Sigmoid), (sg[:, 1, :], AF.Sigmoid)]):
        c0, csz = cl[48 + j]
        lz = A.alloc([LP], F32)
        dd = A.alloc([NT], F32)
        LZ = Buf("lz")
        DD = Buf("dd")
        S.op("dve", lambda e, lz=lz: e.memset(lz[:, 0:1], 0.0), writes=[LZ])
        S.dma("sp", lz[0:csz, 1:2049], g.zT[c0:c0 + csz, 0:2048], reads=[g.ZB], writes=[LZ])
        lzs = lz[:, 2049:LP].rearrange("p (s t) -> p s t", t=9)
        S.dma("sp", lzs[0:csz, :, 1:9], g.zT[c0:c0 + csz, 2048:NT].rearrange("p (s t) -> p s t", t=8), reads=[g.ZB], writes=[LZ])
        S.op("dve", lambda e, lzs=lzs, j=j, csz=csz: e.tensor_copy(lzs[0:csz, :, 0], stT[0:csz, 48 + j, :]), reads=[STB], writes=[LZ])
        pb = 2 + j % 2
        S.group("pe", [lambda e, lzs=lzs, csz=csz, pb=pb: e.transpose(ps[0:16, pb * 512:pb * 512 + csz], lzs[0:csz, :, 8], identf[0:csz, 0:csz]),
                       lambda e, lz=lz, csz=csz, pb=pb: e.transpose(ps[0:1, pb * 512 + 128:pb * 512 + 128 + csz], lz[0:csz, 2048:2049], identf[0:csz, 0:csz])],
                reads=[LZ, g.CB], writes=[g.PB[pb]])
        S.op("act", lambda e, c0=c0, csz=csz, pb=pb: e.activation(shst[0:16, 0, 0, 0:csz], ps[0:16, pb * 512:pb * 512 + csz], AF.Copy), reads=[g.PB[pb]], writes=[OSB])
        S.op("act", lambda e, c0=c0, csz=csz, pb=pb: e.activation(shst[0:1, 1, 0, 0:csz], ps[0:1, pb * 512 + 128:pb * 512 + 128 + csz], AF.Copy), reads=[g.PB[pb]], writes=[OSB])
        S.dma("sp", g.o_shift_s[:, c0:c0 + csz], shst[0:16, 0, 0, 0:csz], reads=[OSB])
        S.dma("sp", g.o_shift_p.rearrange("(o n) -> o n", o=1)[:, c0:c0 + csz], shst[0:1, 1, 0, 0:csz], reads=[OSB])
        S.op("dve", lambda e, lz=lz, dd=dd, csz=csz: e.tensor_tensor(dd[0:csz, 0:2048], lz[0:csz, 0:2048], lz[0:csz, 1:2049], ALU.subtract), reads=[LZ], writes=[DD])
        S.op("dve", lambda e, lzs=lzs, dd=dd, csz=csz: e.tensor_tensor(dd[0:csz, 2048:NT].rearrange("p (s t) -> p s t", t=8), lzs[0:csz, :, 0:8], lzs[0:csz, :, 1:9], ALU.subtract), reads=[LZ], writes=[DD])
        S.op("dve", lambda e, lz=lz, dd=dd, csz=csz, j=j: e.scalar_tensor_tensor(dd[0:csz, 0:2048], dd[0:csz, 0:2048], mu[0:csz, 48 + j:49 + j], lz[0:csz, 1:2049], ALU.mult, ALU.add), reads=[LZ, DD, CB], writes=[DD])
        S.op("dve", lambda e, lzs=lzs, dd=dd, csz=csz, j=j: e.scalar_tensor_tensor(dd[0:csz, 2048:NT].rearrange("p (s t) -> p s t", t=8), dd[0:csz, 2048:NT].rearrange("p (s t) -> p s t", t=8), mu[0:csz, 48 + j:49 + j], lzs[0:csz, :, 1:9], ALU.mult, ALU.add), reads=[LZ, DD, CB], writes=[DD])
        S.op("act", lambda e, dst=dst, dd=dd, csz=csz, func=func: e.activation(dst[0:csz, :], dd[0:csz, :], func), reads=[DD], writes=[LRB])
        A.release(A.mark())
        A.off = mk1
    A.release(mk1)
    if g.stop <= 2:
        return

    N_OWN = OWN
    ybuf = A.alloc([N_OWN], F32)
    YB = Buf("y")
    gbuf = A.alloc([N_OWN], F32)
    GB = Buf("g")
    bonus = A.alloc([N_OWN], F32)
    BNB = Buf("bonus")
    Sf = A.alloc([64], F32)
    SFB = Buf("Sf")
    Sb = [A.alloc([128], BF16) for _ in range(2)]
    SBB = [Buf("Sb0"), Buf("Sb1")]
    S0T = A.alloc([16, 64], F32)
    S0B = Buf("S0T")
    Snew = S0T
    SNB = S0B
    Sb8 = A.alloc([8, 128], BF16)
    SB8 = Buf("Sb8")
    Zp4 = [A.alloc([4, 2, 128], BF16) for _ in range(2)]
    ZPB = [Buf("Zp0"), Buf("Zp1")]
    Wp4 = [A.alloc([4, 2, 128], BF16) for _ in range(2)]
    WPB = [Buf("Wp0"), Buf("Wp1")]
    Vp4 = [A.alloc([4, 2, 64], BF16) for _ in range(2)]
    VPB = [Buf("Vp0"), Buf("Vp1")]
    NaK = [A.alloc([8, 64], BF16) for _ in range(2)]
    M12 = [A.alloc([8, 64], BF16) for _ in range(2)]
    NKB = [Buf("NaK0"), Buf("NaK1")]
    Tinv = [A.alloc([8, 64], BF16) for _ in range(2)]
    TIB = [Buf("Ti0"), Buf("Ti1")]
    Xb = [A.alloc([8, 64], BF16) for _ in range(2)]
    XTb = [A.alloc([8, 64], BF16) for _ in range(2)]
    XB_ = [Buf("X0"), Buf("X1")]
    Pb = [A.alloc([8, 64], BF16) for _ in range(2)]
    PBF = [Buf("P0"), Buf("P1")]
    Bsb = A.alloc([128], BF16)
    BSB = Buf("Bsb")
    Kmask = A.alloc([8, 64], BF16)
    Rmask = A.alloc([8, 64], BF16)
    KMB = Buf("KRmask")
    Wm = A.alloc([2, 8, 64], BF16)
    WMB = Buf("Wm")
    for t, b in [(Sb[0], SBB[0]), (Sb[1], SBB[1]), (Sb8, SB8), (Zp4[0], ZPB[0]), (Zp4[1], ZPB[1]), (Wp4[0], WPB[0]), (Wp4[1], WPB[1]),
                 (Vp4[0], VPB[0]), (Vp4[1], VPB[1])]:
        S.op("dve", lambda e, t=t: e.memset(t, 0.0), writes=[b])
    KR = [A.alloc([8, 2, 64], BF16) for _ in range(2)]
    AKm = [[A.alloc([8, 2, 64], BF16) for _ in range(2)] for _ in range(2)]
    AKd = [A.alloc([8, 2, 64], BF16) for _ in range(2)]
    mx = [A.alloc([3, 64 + 512], F32) for _ in range(2)]
    Gc = [A.alloc([16], F32) for _ in range(2)]
    mvb = [A.alloc([64 + 512], BF16) for _ in range(2)]
    SIB = [Buf("scanin0"), Buf("scanin1")]
    for i in range(2):
        S.op("dve", lambda e, i=i: e.memset(mx[i], 0.0), writes=[SIB[i]])
        S.op("dve", lambda e, i=i: e.memset(mvb[i], 0.0), writes=[SIB[i]])
        for h in range(2):
            S.op("dve", lambda e, i=i, h=h: e.memset(AKm[i][h], 0.0), writes=[SIB[i]])
    zoff = A.off
    zb = A.alloc([3, 513], F32)
    dt_ = A.alloc([3, 512], F32)
    eoff = A.off
    A.off = zoff
    stage = A.alloc([16, 128], F32)
    A.off = eoff
    ZBB = Buf("zb+d+stage")
    DTB = ZBB
    STG = ZBB
    tmp = [A.alloc([512], F32) for _ in range(11)]
    TB_ = [Buf("tmp%d" % i) for i in range(11)]
    oTs = [A.alloc([OWN], BF16) for _ in range(2)]
    OSG = [Buf("oTs0"), Buf("oTs1")]
    sqb = A.alloc([512], BF16)
    SQB = Buf("sq")

    zT3 = g.zT[0:6144, :].rearrange("(j q) t -> q j t", q=2048)
    segs = [(0, 512, "pre"), (512, 512, "pre"), (1024, 512, "own"), (1536, 512, "own"), (2048, 128, "smp")]
    st = {"si": 0, "cur": 0, "hk": 0, "done": 0}

    def do_pair(p):
        prow = slice(p * 128, (p + 1) * 128)
        for sq_ in range(16):
            S.dma("sp", stage[0:64, sq_, :].rearrange("p (h k) -> p h k", h=2),
                  g.st_wkv[sq_, 2 * p:2 * p + 2, :, :].rearrange("h v k -> v h k"), writes=[STG])
        for q in range(2):
            S.group("pe", [lambda e, q=q, s=s: e.transpose(ps[:, 2560 + q * 512 + s * 64:2560 + q * 512 + (s + 1) * 64], stage[0:64, q * 8 + s, :], identf[0:64, 0:64])
                           for s in range(8)], reads=[STG, g.CB], writes=[g.PB[5 + q]])
            S.op("act", lambda e, q=q: e.activation(S0T[:, q * 8:(q + 1) * 8, :], ps[:, 2560 + q * 512:2560 + (q + 1) * 512].rearrange("p (s v) -> p s v", v=64), AF.Copy),
                 reads=[g.PB[5 + q]], writes=[S0B])
        S.op("dve", lambda e: e.memset(Sf, 0.0), writes=[SFB])
        S.op("dve", lambda e: e.memset(Sb[0][0:64, 0:64], 0.0), writes=[SBB[0]])
        S.op("dve", lambda e: e.memset(Sb[0][64:128, 64:128], 0.0), writes=[SBB[0]])
        st["cur"] = 0

        def do_seg(t0, N, kind):
            yield ("WAIT", st["hk"] - 2)
            b = st["si"] % 2
            st["si"] += 1
            ncn = N // 64
            own = kind != "pre"
            smp = kind == "smp"
            oo = t0 - 1024
            if not smp:
                if t0 == 0:
                    S.op("dve", lambda e: e.memset(zb[:, :, 0:1], 0.0), writes=[ZBB])
                    S.dma("sp", zb[:, :, 1:513], zT3[prow, :, 0:512], reads=[g.ZB], writes=[ZBB])
                else:
                    S.dma("sp", zb[:, :, 0:513], zT3[prow, :, t0 - 1:t0 + 512], reads=[g.ZB], writes=[ZBB])
                zprev = zb[:, :, 0:512]
                zcur = zb[:, :, 1:513]
                shp3 = lambda ap: ap
            else:
                zv = zb[:, :, 0:144].rearrange("p j (s t) -> p j s t", t=9)
                for j in range(3):
                    S.dma("sp", zv[:, j, :, 1:9], zT3[prow, j, 2048:NT].rearrange("p (s t) -> p s t", t=8), reads=[g.ZB], writes=[ZBB])
                    S.op("dve", lambda e, j=j, zv=zv: e.tensor_copy(zv[:, j, :, 0], stT[:, j * 16 + p, :]), reads=[STB], writes=[ZBB])
                zprev = zv[:, :, :, 0:8]
                zcur = zv[:, :, :, 1:9]
                shp3 = lambda ap: ap.rearrange("p j (s t) -> p j s t", t=8)
            if smp or t0 == 1536:
                pbk = 7
                if smp:
                    fns = [lambda e, j=j: e.transpose(ps[0:16, 3584 + j * 128:3584 + (j + 1) * 128], zv[:, j, :, 8], identf) for j in range(3)]
                    S.group("pe", fns, reads=[ZBB, g.CB], writes=[g.PB[7]])
                    for j in range(3):
                        S.op("act", lambda e, j=j: e.activation(shst[0:16, 0, j, :], ps[0:16, 3584 + j * 128:3584 + (j + 1) * 128], AF.Copy),
                             reads=[g.PB[7]], writes=[OSB])
                        S.dma("sp", g.o_shift_s[:, j * 2048 + p * 128:j * 2048 + (p + 1) * 128], shst[0:16, 0, j, :], reads=[OSB])
                else:
                    fns = [lambda e, j=j: e.transpose(ps[0:1, 3584 + j * 128:3584 + (j + 1) * 128], zb[:, j, 512:513], identf) for j in range(3)]
                    S.group("pe", fns, reads=[ZBB, g.CB], writes=[g.PB[7]])
                    for j in range(3):
                        S.op("act", lambda e, j=j: e.activation(shst[0:1, 1, j, :], ps[0:1, 3584 + j * 128:3584 + (j + 1) * 128], AF.Copy),
                             reads=[g.PB[7]], writes=[OSB])
                        S.dma("sp", g.o_shift_p.rearrange("(o n) -> o n", o=1)[:, j * 2048 + p * 128:j * 2048 + (p + 1) * 128], shst[0:1, 1, j, :], reads=[OSB])
            dv = shp3(dt_[:, :, 0:N])
            mv3 = shp3(mx[b][:, :, 64:64 + N])
            mu3 = mu[:, p:48:16]
            S.op("dve", lambda e, dv=dv, zprev=zprev, zcur=zcur: e.tensor_tensor(dv, zprev, zcur, ALU.subtract), reads=[ZBB], writes=[DTB])
            if smp:
                mub = mu3.unsqueeze(2).unsqueeze(3).to_broadcast([128, 3, 16, 8])
            else:
                mub = mu3.unsqueeze(2).to_broadcast([128, 3, N])
            S.op("dve", lambda e, dv=dv, mub=mub: e.tensor_tensor(dv, dv, mub, ALU.mult), reads=[DTB, CB], writes=[DTB])
            S.op("dve", lambda e, dv=dv, mv3=mv3, zcur=zcur: e.tensor_tensor(mv3, dv, zcur, ALU.add), reads=[DTB, ZBB], writes=[SIB[b]])
            S.op("act", lambda e: e.activation(mvb[b][:, 64:64 + N], mx[b][:, 2, 64:64 + N], AF.Copy), reads=[SIB[b]], writes=[SIB[b]])
            mr = mx[b][:, 0, 64:64 + N]
            mk_ = mx[b][:, 1, 64:64 + N]
            mv_ = mx[b][:, 2, 64:64 + N]
            T_ = lambda i: tmp[i][:, 0:N]
            yield 1
            pw = ps[:, 3584:3584 + N]
            S.op("pe", lambda e, pw=pw: e.matmul(pw, wdu[0:96, prow], tw[0:96, t0:t0 + N], start=True, stop=True), reads=[LRB, CB], writes=[g.PB[7]])
            S.op("act", lambda e, pw=pw: e.activation(T_(0), pw, AF.Sigmoid, bias=w0[:, p:p + 1]), reads=[g.PB[7], CB], writes=[TB_[0]])
            S.op("pe", lambda e, pw=pw: e.matmul(pw, wau[0:96, prow], adm[0:96, t0:t0 + N], start=True, stop=True), reads=[LRB, CB], writes=[g.PB[7]])
            S.op("act", lambda e, pw=pw: e.activation(T_(1), pw, AF.Sigmoid, bias=a0[:, p:p + 1]), reads=[g.PB[7], CB], writes=[TB_[1]])
            if own:
                S.group("pe", [lambda e, pw=pw, c=c: e.matmul(pw, wgu[:, c, prow], sg[:, c, t0:t0 + N], start=(c == 0), stop=(c == 1)) for c in range(2)],
                        reads=[LRB, CB], writes=[g.PB[7]])
                S.op("act", lambda e, pw=pw: e.activation(gbuf[:, oo:oo + N], pw, AF.Copy), reads=[g.PB[7]], writes=[GB])
            yield 1
            smk = scanmask[:, 512:640] if smp else scanmask[:, 0:N]
            S.op("dve", lambda e, smk=smk: e.tensor_tensor_scan(T_(2), smk, T_(0), 0.0, ALU.mult, ALU.add), reads=[TB_[0], CB], writes=[TB_[2]])
            cl_ = 8 if smp else 64
            nseg = N // cl_
            cs3 = T_(2).rearrange("p (c t) -> p c t", t=cl_)
            S.op("dve", lambda e, cs3=cs3, cl_=cl_, nseg=nseg: e.tensor_tensor(T_(3).rearrange("p (c t) -> p c t", t=cl_), cs3[:, :, cl_ - 1:cl_].to_broadcast([128, nseg, cl_]), cs3, ALU.subtract),
                 reads=[TB_[2]], writes=[TB_[3]])
            S.op("dve", lambda e: e.tensor_tensor(T_(4), T_(2), T_(0), ALU.subtract), reads=[TB_[2], TB_[0]], writes=[TB_[4]])
            yield 1
            S.op("act", lambda e: e.activation(T_(5), T_(2), AF.Exp, scale=-C0), reads=[TB_[2]], writes=[TB_[5]])
            S.op("act", lambda e: e.activation(T_(6), T_(2), AF.Exp, scale=C0), reads=[TB_[2]], writes=[TB_[6]])
            yield 1
            S.op("act", lambda e: e.activation(T_(4), T_(4), AF.Exp, scale=-C0), reads=[TB_[4]], writes=[TB_[4]])
            S.op("act", lambda e: e.activation(T_(3), T_(3), AF.Exp, scale=-C0), reads=[TB_[3]], writes=[TB_[3]])
            E5 = T_(5).rearrange("p (c t) -> p c t", t=cl_)
            S.op("dve", lambda e, E5=E5, b=b, nseg=nseg, cl_=cl_: e.tensor_copy(Gc[b][:, 0:nseg], E5[:, :, cl_ - 1]), reads=[TB_[5]], writes=[SIB[b]])
            yield 1
            S.op("dve", lambda e: e.tensor_scalar(T_(7), mk_, k_k[:, p:p + 1], None, ALU.mult), reads=[SIB[b], CB], writes=[TB_[7]])
            S.op("dve", lambda e: e.tensor_tensor(sqb[:, 0:N], T_(7), T_(7), ALU.mult), reads=[TB_[7]], writes=[SQB])
            S.op("pe", lambda e, pw=pw: e.matmul(pw, bones, sqb[:, 0:N], start=True, stop=True), reads=[SQB, g.CB], writes=[g.PB[7]])
            S.op("act", lambda e, pw=pw: e.activation(T_(8), pw, AF.Sqrt), reads=[g.PB[7]], writes=[TB_[8]])
            yield 1
            S.op("dve", lambda e: e.tensor_scalar(T_(8), T_(8), 1e-12, None, ALU.max), reads=[TB_[8]], writes=[TB_[8]])
            S.op("dve", lambda e: e.reciprocal(T_(8), T_(8)), reads=[TB_[8]], writes=[TB_[8]])
            S.op("dve", lambda e: e.tensor_tensor(T_(7), T_(7), T_(8), ALU.mult), reads=[TB_[7], TB_[8]], writes=[TB_[7]])
            yield 1
            S.op("dve", lambda e: e.tensor_scalar(T_(9), T_(1), k_a[:, p:p + 1], omka[:, p:p + 1], ALU.mult, ALU.add), reads=[TB_[1], CB], writes=[TB_[9]])
            S.op("dve", lambda e: e.tensor_tensor(T_(9), T_(9), mk_, ALU.mult), reads=[TB_[9], SIB[b]], writes=[TB_[9]])
            S.op("dve", lambda e: e.tensor_tensor(T_(10), T_(7), T_(1), ALU.mult), reads=[TB_[7], TB_[1]], writes=[TB_[10]])
            yield 1
            c64 = lambda ap: ap.rearrange("p (c t) -> p c t", t=64)
            S.op("dve", lambda e, b=b, ncn=ncn: e.tensor_tensor(KR[b][:, 0:ncn, 0, :], c64(T_(7)), c64(T_(4)), ALU.mult), reads=[TB_[7], TB_[4]], writes=[SIB[b]])
            S.op("dve", lambda e, b=b, ncn=ncn: e.tensor_tensor(KR[b][:, 0:ncn, 1, :], c64(mr), c64(T_(5)), ALU.mult), reads=[TB_[5]], writes=[SIB[b]])
            yield 1
            for h in range(2):
                hs = slice(h * 64, (h + 1) * 64)
                S.op("dve", lambda e, b=b, h=h, hs=hs, ncn=ncn: e.tensor_tensor(AKm[b][h][hs, 0:ncn, 0, :], c64(T_(10))[hs], c64(T_(6))[hs], ALU.mult),
                     reads=[TB_[10], TB_[6]], writes=[SIB[b]])
                S.op("dve", lambda e, b=b, h=h, hs=hs, ncn=ncn: e.tensor_tensor(AKm[b][h][hs, 0:ncn, 1, :], c64(T_(9))[hs], c64(T_(6))[hs], ALU.mult),
                     reads=[TB_[9], TB_[6]], writes=[SIB[b]])
            S.op("dve", lambda e, b=b, ncn=ncn: e.tensor_tensor(AKd[b][:, 0:ncn, 0, :], c64(T_(10)), c64(T_(3)), ALU.mult), reads=[TB_[10], TB_[3]], writes=[SIB[b]])
            S.op("dve", lambda e, b=b, ncn=ncn: e.tensor_tensor(AKd[b][:, 0:ncn, 1, :], c64(T_(9)), c64(T_(3)), ALU.mult), reads=[TB_[9], TB_[3]], writes=[SIB[b]])
            if own:
                S.op("dve", lambda e: e.scalar_tensor_tensor(sqb[:, 0:N], mr, r_k[:, p:p + 1], T_(9), ALU.mult, ALU.mult), reads=[SIB[b], TB_[9], CB], writes=[SQB])
                S.op("pe", lambda e, pw=pw: e.matmul(pw, bones, sqb[:, 0:N], start=True, stop=True), reads=[SQB, g.CB], writes=[g.PB[7]])
                S.op("dve", lambda e, pw=pw: e.tensor_tensor(bonus[:, oo:oo + N], pw, mv_, ALU.mult), reads=[g.PB[7], SIB[b]], writes=[BNB])
            yield 1
            mo = 4 if smp else 0
            nlev = 3 if smp else 5
            def do_half(hsx):
                cc0 = hsx * 4
                nch = min(4, ncn - cc0)
                nm = nch * 2
                yield ("WAIT", st["hk"] - 1)
                hb = st["hk"] % 2
                st["hk"] += 1
                last_half = hsx == (ncn + 3) // 4 - 1
                fns = []
                for c in range(nch):
                    for h in range(2):
                        mi = c * 2 + h
                        fns.append(lambda e, c=c, h=h, mi=mi: e.matmul(ps[:, mi * 128:(mi + 1) * 128], AKm[b][h][:, cc0 + c, :, :].rearrange("p a t -> p (a t)"),
                                                                       KR[b][:, cc0 + c, :, :].rearrange("p a t -> p (a t)"), start=True, stop=True))
                S.group("pe", fns, reads=[SIB[b]], writes=[g.PB[0], g.PB[1]])
                yield 1
                fns = []
                for c in range(nch):
                    for h in range(2):
                        mi = c * 2 + h
                        fns.append(lambda e, c=c, h=h, mi=mi: e.matmul(ps[0:64, 1024 + mi * 64:1024 + (mi + 1) * 64], KR[b][:, cc0 + c, 0, :], AKm[b][h][:, cc0 + c, 0, :], start=True, stop=True))
                S.group("pe", fns, reads=[SIB[b]], writes=[g.PB[2]])
                yield 1
                P1v = ps[:, 0:nm * 128].rearrange("p (m t) -> p m t", t=128)
                mb = lambda k, rows=slice(0, 128): masks[rows, mo + k:mo + k + 1, :].to_broadcast([rows.stop - rows.start, nm, 64])
                S.op("dve", lambda e, P1v=P1v, hb=hb, nm=nm: e.tensor_tensor(NaK[hb][:, 0:nm, :], P1v[:, :, 0:64], mb(0), ALU.mult), reads=[g.PB[0], g.PB[1], g.CB], writes=[NKB[hb]])
                S.op("dve", lambda e, P1v=P1v, hb=hb, nm=nm: e.tensor_tensor(M12[hb][:, 0:nm, :], P1v[:, :, 64:128], mb(1), ALU.mult), reads=[g.PB[0], g.PB[1], g.CB], writes=[NKB[hb]])
                S.op("dve", lambda e, P1v=P1v, nm=nm: e.tensor_tensor(Xb[0][0:64, 0:nm, :], P1v[0:64, :, 0:64], mb(2, slice(0, 64)), ALU.mult), reads=[g.PB[0], g.PB[1], g.CB], writes=[XB_[0]])
                S.op("dve", lambda e, nm=nm: e.tensor_tensor(XTb[0][0:64, 0:nm, :], ps[0:64, 1024:1024 + nm * 64].rearrange("p (m t) -> p m t", t=64), mb(3, slice(0, 64)), ALU.mult),
                     reads=[g.PB[2], g.CB], writes=[XB_[0]])
                S.op("dve", lambda e, nm=nm: e.tensor_tensor(Pb[0][0:64, 0:nm, :], Xb[0][0:64, 0:nm, :], identb[0:64, 0:64].unsqueeze(1).to_broadcast([64, nm, 64]), ALU.add),
                     reads=[XB_[0], g.CB], writes=[PBF[0]])
                yield 1
                xc, pc = 0, 0
                for lv in range(nlev):
                    last = lv == nlev - 1
                    xn = 1 - xc
                    fns = []
                    for mi in range(nm):
                        if not last:
                            fns.append(lambda e, mi=mi, xc=xc: e.matmul(ps[0:64, mi * 64:(mi + 1) * 64], XTb[xc][0:64, mi, :], Xb[xc][0:64, mi, :], start=True, stop=True))
                        fns.append(lambda e, mi=mi, xc=xc: e.matmul(ps[0:64, 512 + mi * 64:512 + (mi + 1) * 64], Xb[xc][0:64, mi, :], XTb[xc][0:64, mi, :], start=True, stop=True))
                    S.group("pe", fns, reads=[XB_[xc]], writes=[g.PB[0], g.PB[1]])
                    yield 1
                    if not last:
                        S.op("act", lambda e, xn=xn, nm=nm: e.activation(Xb[xn][0:64, 0:nm, :], ps[0:64, 0:nm * 64].rearrange("p (m t) -> p m t", t=64), AF.Copy), reads=[g.PB[0]], writes=[XB_[xn]])
                    S.op("dve", lambda e, xn=xn, nm=nm: e.tensor_copy(XTb[xn][0:64, 0:nm, :], ps[0:64, 512:512 + nm * 64].rearrange("p (m t) -> p m t", t=64)), reads=[g.PB[1]], writes=[XB_[xn]])
                    yield 1
                    pn = 1 - pc
                    S.group("pe", [lambda e, mi=mi, xn=xn, pc=pc: e.matmul(ps[0:64, 1024 + mi * 64:1024 + (mi + 1) * 64], XTb[xn][0:64, mi, :], Pb[pc][0:64, mi, :], start=True, stop=True)
                                   for mi in range(nm)], reads=[XB_[xn], PBF[pc]], writes=[g.PB[2]])
                    yield 1
                    dst = Tinv[hb] if last else Pb[pn]
                    dstB = TIB[hb] if last else PBF[pn]
                    S.op("dve", lambda e, dst=dst, pc=pc, nm=nm: e.tensor_tensor(dst[0:64, 0:nm, :], ps[0:64, 1024:1024 + nm * 64].rearrange("p (m t) -> p m t", t=64), Pb[pc][0:64, 0:nm, :], ALU.add),
                         reads=[g.PB[2], PBF[pc]], writes=[dstB])
                    xc, pc = xn, pn
                    yield 1
                pz = bank(g, 4, BF16)
                S.group("pe", [lambda e, c=c, pz=pz: e.transpose(pz[:, c * 128:(c + 1) * 128], AKd[b][:, cc0 + c, :, :].rearrange("p a t -> p (a t)"), identb) for c in range(nch)],
                        reads=[SIB[b], g.CB], writes=[g.PB[4]])
                zdst = lambda t, rows: bass.AP(t.tensor, t[rows, 0, 0, 0:1].offset, [list(t[rows, 0, 0, 0:1].ap[0]), [256, nch], [192, 2], [1, 64]])
                S.op("act", lambda e, pz=pz, hb=hb, nch=nch: e.activation(zdst(Zp4[hb], slice(0, 128)), pz[:, 0:nch * 128].rearrange("p (c h k) -> p c h k", h=2, k=64), AF.Copy),
                     reads=[g.PB[4]], writes=[ZPB[hb]])
                pz5 = bank(g, 5, BF16)
                S.group("pe", [lambda e, c=c, pz5=pz5: e.transpose(pz5[:, c * 128:(c + 1) * 128], mvb[b][:, (cc0 + c) * 64:(cc0 + c) * 64 + 128], identb) for c in range(nch)],
                        reads=[SIB[b], g.CB], writes=[g.PB[5]])
                pv5 = pz5[64:128, 0:nch * 128].rearrange("p (c h k) -> p c h k", h=2, k=64)
                S.op("act", lambda e, pv5=pv5, hb=hb: e.activation(zdst(Wp4[hb], slice(64, 128)), pv5, AF.Copy), reads=[g.PB[5]], writes=[WPB[hb]])
                S.op("act", lambda e, pv5=pv5, hb=hb, nch=nch: e.activation(Vp4[hb][64:128, 0:nch, :, :], pv5, AF.Copy), reads=[g.PB[5]], writes=[VPB[hb]])
                yield 1
                def do_chunk(c):
                    cur = st["cur"]
                    cg = cc0 + c
                    nxt = 1 - cur
                    pB = ps[0:64, 1536:1664]
                    if not smp:
                        fns = [lambda e, cg=cg, cur=cur, pB=pB: e.matmul(pB, KR[b][:, cg, 0, :], Sb[cur], start=True, stop=False)]
                        rd = [SIB[b], SBB[cur], NKB[hb], VPB[hb]]
                    else:
                        S.op("dve", lambda e, cg=cg: e.tensor_copy(Sb8[0:64, :, 0:64], S0T[0:64, cg * 8:(cg + 1) * 8, :]), reads=[S0B], writes=[SB8])
                        S.op("dve", lambda e, cg=cg: e.tensor_copy(Sb8[64:128, :, 64:128], S0T[64:128, cg * 8:(cg + 1) * 8, :]), reads=[S0B], writes=[SB8])
                        S.op("dve", lambda e, cg=cg: e.tensor_tensor(Kmask, KR[b][:, cg, 0:1, :].to_broadcast([128, 8, 64]), seqmask, ALU.mult), reads=[SIB[b], CB], writes=[KMB])
                        S.op("dve", lambda e, cg=cg: e.tensor_tensor(Rmask, KR[b][:, cg, 1:2, :].to_broadcast([128, 8, 64]), seqmask, ALU.mult), reads=[SIB[b], CB], writes=[KMB])
                        fns = [lambda e, s=s, pB=pB: e.matmul(pB, Kmask[:, s, :], Sb8[:, s, :], start=(s == 0), stop=False) for s in range(8)]
                        rd = [KMB, SB8, NKB[hb], VPB[hb]]
                    for h in range(2):
                        fns.append(lambda e, h=h, c=c, hb=hb: e.matmul(ps[0:64, 1536 + h * 64:1536 + (h + 1) * 64], NaK[hb][:, c * 2 + h, :], Vp4[hb][:, c, h, :], start=False, stop=True))
                    S.group("pe", fns, reads=rd, writes=[g.PB[3]])
                    yield 1
                    S.op("act", lambda e, pB=pB: e.activation(Bsb[0:64, :], pB, AF.Copy), reads=[g.PB[3]], writes=[BSB])
                    yield 1
                    S.group("pe", [lambda e, h=h, c=c, hb=hb: e.matmul(ps[0:64, 1664 + h * 64:1664 + (h + 1) * 64], Tinv[hb][0:64, c * 2 + h, :], Bsb[0:64, h * 64:(h + 1) * 64], start=True, stop=True)
                                   for h in range(2)], reads=[TIB[hb], BSB], writes=[g.PB[3]])
                    yield 1
                    wrow = Wp4[hb][0:64, c, 0, 0:1]
                    wdst = bass.AP(wrow.tensor, wrow.offset, [list(wrow.ap[0]), [192, 2], [1, 64]])
                    yield 1
                    S.op("dve", lambda e, wdst=wdst: e.tensor_scalar(wdst, ps[0:64, 1664:1792].rearrange("p (h v) -> p h v", h=2), -1.0, None, ALU.mult), reads=[g.PB[3]], writes=[WPB[hb]])
                    yield 1
                    if not smp:
                        S.group("pe", [lambda e, h=h, c=c, hb=hb: e.matmul(ps[:, 1792:1856], Zp4[hb][:, c, h, :], Wp4[hb][:, c, h, h * 64:(h + 1) * 64], start=(h == 0), stop=(h == 1)) for h in range(2)],
                                reads=[ZPB[hb], WPB[hb]], writes=[g.PB[3]])
                        yield 1
                    else:
                        for h in range(2):
                            S.op("dve", lambda e, h=h, c=c, hb=hb: e.tensor_tensor(Wm[:, h, :, :], Wp4[hb][:, c, h, h * 64:(h + 1) * 64].unsqueeze(1).to_broadcast([128, 8, 64]),
                                                                                   tokmask.unsqueeze(2).to_broadcast([128, 8, 64]), ALU.mult), reads=[WPB[hb], CB], writes=[WMB])
                        S.group("pe", [lambda e, h=h, c=c, hb=hb: e.matmul(ps[:, 3584:4096], Zp4[hb][:, c, h, :], Wm[:, h, :, :].rearrange("p s v -> p (s v)"), start=(h == 0), stop=(h == 1)) for h in range(2)],
                                reads=[ZPB[hb], WMB], writes=[g.PB[7]])
                    yield 1
                    if own:
                        ycol = 3072 + cg * 64
                        if not smp:
                            fns = [lambda e, cg=cg, cur=cur, ycol=ycol: e.matmul(ps[:, ycol:ycol + 64], Sb[cur], KR[b][:, cg, 1, :], start=True, stop=False)]
                            rd = [SIB[b], SBB[cur], NKB[hb], WPB[hb]]
                        else:
                            fns = [lambda e, s=s, ycol=ycol: e.matmul(ps[:, ycol:ycol + 64], Sb8[:, s, :], Rmask[:, s, :], start=(s == 0), stop=False) for s in range(8)]
                            rd = [KMB, SB8, NKB[hb], WPB[hb]]
                        for h in range(2):
                            fns.append(lambda e, h=h, c=c, hb=hb, ycol=ycol: e.matmul(ps[:, ycol:ycol + 64], Wp4[hb][:, c, h, :], M12[hb][:, c * 2 + h, :], start=False, stop=(h == 1)))
                        S.group("pe", fns, reads=rd, writes=[g.PB[6]])
                    if not smp:
                        S.op("dve", lambda e, cg=cg, b=b: e.scalar_tensor_tensor(Sf, Sf, Gc[b][:, cg:cg + 1], ps[:, 1792:1856], ALU.mult, ALU.add), reads=[SFB, SIB[b], g.PB[3]], writes=[SFB])
                        yield 1
                        S.op("act", lambda e, nxt=nxt: e.activation(Sb[nxt][0:64, 0:64], Sf[0:64, :], AF.Copy), reads=[SFB], writes=[SBB[nxt]])
                        S.op("act", lambda e, nxt=nxt: e.activation(Sb[nxt][64:128, 64:128], Sf[64:128, :], AF.Copy), reads=[SFB], writes=[SBB[nxt]])
                        st["cur"] = nxt
                    else:
                        S.op("dve", lambda e, cg=cg, b=b: e.tensor_tensor(Snew[:, cg * 8:(cg + 1) * 8, :], S0T[:, cg * 8:(cg + 1) * 8, :], Gc[b][:, cg * 8:(cg + 1) * 8].unsqueeze(2).to_broadcast([128, 8, 64]), ALU.mult),
                             reads=[S0B, SIB[b]], writes=[SNB])
                        S.op("dve", lambda e, cg=cg: e.tensor_tensor(Snew[:, cg * 8:(cg + 1) * 8, :], Snew[:, cg * 8:(cg + 1) * 8, :], ps[:, 3584:4096].rearrange("p (s v) -> p s v", v=64), ALU.add),
                             reads=[SNB, g.PB[7]], writes=[SNB])
                def seq_half():
                    for c_ in range(nch):
                        yield from do_chunk(c_)
                        yield 1
                    if own and last_half:
                        S.op("act", lambda e, oo=oo, N=N: e.activation(ybuf[:, oo:oo + N], ps[:, 3072:3072 + N], AF.Copy), reads=[g.PB[6]], writes=[YB])
                    st["done"] += 1
                seq_q.append(seq_half())
            for hsx_ in range((ncn + 3) // 4):
                yield from do_half(hsx_)

        seq_q = []

        def stream_b():
            for (t0_, N_, kind_) in segs:
                yield from do_seg(t0_, N_, kind_)

        B = stream_b()
        b_done = False
        b_wait = None
        a_cur = None
        while True:
            prog = False
            if a_cur is None and seq_q:
                a_cur = seq_q.pop(0)
            if a_cur is not None:
                for _ in range(2):
                    try:
                        next(a_cur)
                    except StopIteration:
                        a_cur = None
                        break
                prog = True
            if not b_done:
                if b_wait is not None and st["done"] >= b_wait:
                    b_wait = None
                if b_wait is None:
                    try:
                        r_ = next(B)
                        if isinstance(r_, tuple):
                            b_wait = r_[1]
                    except StopIteration:
                        b_done = True
                    prog = True
            if b_done and a_cur is None and not seq_q:
                break
            assert prog, "scheduler deadlock"
        S.op("pe", lambda e: e.transpose(ps[0:64, 2560:2688], Sf, identf), reads=[SFB, g.CB], writes=[g.PB[5]])
        S.op("act", lambda e: e.activation(stage[0:64, 0, :], ps[0:64, 2560:2688], AF.Copy), reads=[g.PB[5]], writes=[STG])
        S.dma("sp", g.o_wkv_p[2 * p:2 * p + 2, :, :].rearrange("h v k -> v h k"), stage[0:64, 0, :].rearrange("p (h k) -> p h k", h=2), reads=[STG])
        for q4 in range(4):
            pbq = 5 + q4 % 2
            S.group("pe", [lambda e, q4=q4, s=s, pbq=pbq: e.transpose(ps[0:64, pbq * 512 + s * 128:pbq * 512 + (s + 1) * 128], Snew[:, q4 * 4 + s, :], identf) for s in range(4)],
                    reads=[SNB, g.CB], writes=[g.PB[pbq]])
            S.op("act", lambda e, q4=q4, pbq=pbq: e.activation(stage[0:64, q4 * 4:(q4 + 1) * 4, :], ps[0:64, pbq * 512:(pbq + 1) * 512].rearrange("p (s k) -> p s k", k=128), AF.Copy),
                 reads=[g.PB[pbq]], writes=[STG])
            for s4 in range(4):
                S.dma("sp", g.o_wkv_s[q4 * 4 + s4, 2 * p:2 * p + 2, :, :].rearrange("h v k -> v h k"),
                      stage[0:64, q4 * 4 + s4, :].rearrange("p (h k) -> p h k", h=2), reads=[STG])
        for (o0, n) in [(0, 512), (512, 512), (1024, 128)]:
            pw = ps[:, 3584:3584 + n]
            yv = ybuf[:, o0:o0 + n]
            S.op("dve", lambda e, yv=yv, n=n: e.tensor_copy(sqb[:, 0:n], yv), reads=[YB], writes=[SQB])
            S.op("pe", lambda e, pw=pw, n=n: e.matmul(pw, bones, sqb[:, 0:n], start=True, stop=True), reads=[SQB, g.CB], writes=[g.PB[7]])
            S.op("dve", lambda e, pw=pw, yv=yv, n=n: e.scalar_tensor_tensor(tmp[0][:, 0:n], pw, -1.0 / 64, yv, ALU.mult, ALU.add), reads=[g.PB[7], YB], writes=[TB_[0]])
            S.op("act", lambda e, n=n: e.activation(sqb[:, 0:n], tmp[0][:, 0:n], AF.Square), reads=[TB_[0]], writes=[SQB])
            S.op("pe", lambda e, pw=pw, n=n: e.matmul(pw, bones, sqb[:, 0:n], start=True, stop=True), reads=[SQB, g.CB], writes=[g.PB[7]])
            S.op("act", lambda e, pw=pw, n=n: e.activation(tmp[1][:, 0:n], pw, AF.Sqrt, scale=1.0 / 64, bias=64e-5), reads=[g.PB[7]], writes=[TB_[1]])
            S.op("dve", lambda e, n=n: e.reciprocal(tmp[1][:, 0:n], tmp[1][:, 0:n]), reads=[TB_[1]], writes=[TB_[1]])
            S.op("dve", lambda e, n=n: e.tensor_tensor(tmp[0][:, 0:n], tmp[0][:, 0:n], tmp[1][:, 0:n], ALU.mult), reads=[TB_[0], TB_[1]], writes=[TB_[0]])
            S.op("dve", lambda e, n=n: e.tensor_scalar(tmp[0][:, 0:n], tmp[0][:, 0:n], lnx_g[:, p:p + 1], lnx_b[:, p:p + 1], ALU.mult, ALU.add), reads=[TB_[0], CB], writes=[TB_[0]])
            S.op("dve", lambda e, n=n, o0=o0: e.tensor_tensor(tmp[0][:, 0:n], tmp[0][:, 0:n], bonus[:, o0:o0 + n], ALU.add), reads=[TB_[0], BNB], writes=[TB_[0]])
            S.op("dve", lambda e, n=n, o0=o0: e.tensor_tensor(oTs[p % 2][:, o0:o0 + n], tmp[0][:, 0:n], gbuf[:, o0:o0 + n], ALU.mult), reads=[TB_[0], GB], writes=[OSG[p % 2]])
        S.dma("sp", g.oTd[p * 128:(p + 1) * 128, :], oTs[p % 2], reads=[OSG[p % 2]], writes=[g.OTB])

    for p_ in range(16 if g.stop > 4 else 1):
        do_pair(p_)


def phase_m3(g):
    S, A = g.S, g.A
    ps = g.ps
    identf = g.identf
    mk = A.mark()
    CB = Buf("m3c")
    pscale = A.alloc([16], F32)
    S.dma("sp", pscale, g.pool_scale, writes=[CB])
    spT = A.alloc([16, 240], F32)
    SPB = Buf("spT")
    strow = A.alloc([2, RW], F32)
    SRB = Buf("strow")
    spf = g.st_pool.rearrange("s i c -> (s i) c")
    S.dma("sp", strow[:, 0, :], spf[0:128, :], writes=[SRB])
    S.dma("sp", strow[0:112, 1, :], spf[128:240, :], writes=[SRB])
    for j in range(16):
        pb = j % 4
        S.group("pe", [lambda e, j=j, pb=pb: e.transpose(ps[:, pb * 512:pb * 512 + 128], strow[:, 0, j * 128:(j + 1) * 128], identf),
                       lambda e, j=j, pb=pb: e.transpose(ps[:, pb * 512 + 128:pb * 512 + 240], strow[0:112, 1, j * 128:(j + 1) * 128], identf[0:112, 0:112])],
                reads=[SRB, g.CB], writes=[g.PB[pb]])
        S.op("act", lambda e, j=j, pb=pb: e.activation(spT[:, j, :], ps[:, pb * 512:pb * 512 + 240], AF.Copy), reads=[g.PB[pb]], writes=[SPB])
    S.dma("sp", g.o_pool_s[:, 0:7, :], g.st_pool[:, 8:15, :])
    LA = 15 + 1024
    arr = [A.alloc([LA], F32) for _ in range(3)]
    ar2 = [A.alloc([16, 23], F32) for _ in range(3)]
    ARB = [Buf("arr%d" % i) for i in range(3)]
    invc = A.alloc([OWN], F32)
    IVB = Buf("invc")
    tmpf = A.alloc([OWN], F32)
    TMB = Buf("tmpf")
    dT4 = A.alloc([4, OWN], BF16)
    DTB = Buf("dT4")
    wp = A.alloc([4, 512], BF16)
    WPB = Buf("wp")
    ppst = A.alloc([RW], F32)
    psst = A.alloc([RW], F32)
    PST = Buf("poolstage")
    ost = [A.alloc([OWN], BF16) for _ in range(2)]
    OST = [Buf("ost0"), Buf("ost1")]
    oi = 0
    for j in range(16):
        gi = j // 4
        r0 = 6592 + j * 128
        a0_, a2 = arr[0], ar2[0]
        S.dma("sp", a0_, g.zT[r0:r0 + 128, 1009:2048], reads=[g.ZB], writes=[ARB[0]])
        S.dma("sp", a2[:, :, 15:23], g.zT[r0:r0 + 128, 2048:NT].rearrange("p (s t) -> p s t", t=8), reads=[g.ZB], writes=[ARB[0]])
        S.op("dve", lambda e, a2=a2, j=j: e.tensor_copy(a2[:, :, 0:15], spT[:, j, :].rearrange("p (s i) -> p s i", i=15)), reads=[SPB], writes=[ARB[0]])
        if gi != (j - 1) // 4 or j == 0:
            S.dma("sp", invc, g.c_invcnt[gi:gi + 1, :].partition_broadcast(128) if False else g.c_invcnt[gi, :].partition_broadcast(128), writes=[IVB])
        pb = 4 + j % 2
        S.op("dve", lambda e, a2=a2: e.tensor_copy(tmpf[:, 0:128].rearrange("p (s t) -> p s t", t=8), a2[:, :, 15:23]), reads=[ARB[0]], writes=[TMB])
        S.group("pe", [lambda e, a0_=a0_, pb=pb: e.transpose(ps[0:15, pb * 512:pb * 512 + 128], a0_[:, 1024:1039], identf),
                       lambda e, pb=pb: e.transpose(ps[:, pb * 512 + 128:pb * 512 + 256], tmpf[:, 0:128], identf)],
                reads=[ARB[0], TMB, g.CB], writes=[g.PB[pb]])
        S.op("act", lambda e, j=j, pb=pb: e.activation(ppst[0:15, j * 128:(j + 1) * 128], ps[0:15, pb * 512:pb * 512 + 128], AF.Copy), reads=[g.PB[pb]], writes=[PST])
        S.op("act", lambda e, j=j, pb=pb: e.activation(psst[:, j * 128:(j + 1) * 128], ps[:, pb * 512 + 128:pb * 512 + 256], AF.Copy), reads=[g.PB[pb]], writes=[PST])
        cur = 0
        for k in range(gi + 1):
            sh = 1 << k
            nxt = 1 + (k % 2)
            S.op("dve", lambda e, cur=cur, nxt=nxt, sh=sh: e.tensor_tensor(arr[nxt][:, sh:LA], arr[cur][:, sh:LA], arr[cur][:, 0:LA - sh], ALU.add), reads=[ARB[cur]], writes=[ARB[nxt]])
            S.op("dve", lambda e, cur=cur, nxt=nxt, sh=sh: e.tensor_tensor(ar2[nxt][:, :, sh:23], ar2[cur][:, :, sh:23], ar2[cur][:, :, 0:23 - sh], ALU.add), reads=[ARB[cur]], writes=[ARB[nxt]])
            cur = nxt
        S.op("dve", lambda e, cur=cur: e.tensor_tensor(tmpf[:, 0:1024], arr[cur][:, 15:LA], invc[:, 0:1024], ALU.mult), reads=[ARB[cur], IVB], writes=[TMB])
        S.op("dve", lambda e, cur=cur: e.tensor_tensor(tmpf[:, 1024:OWN].rearrange("p (s t) -> p s t", t=8), ar2[cur][:, :, 15:23], invc[:, 1024:OWN].rearrange("p (s t) -> p s t", t=8), ALU.mult),
             reads=[ARB[cur], IVB], writes=[TMB])
        S.op("dve", lambda e, j=j, a0_=a0_: e.tensor_tensor(dT4[:, j % 4, 0:1024], tmpf[:, 0:1024], a0_[:, 15:LA], ALU.subtract), reads=[TMB, ARB[0]], writes=[DTB])
        S.op("dve", lambda e, j=j, a2=a2: e.tensor_tensor(dT4[:, j % 4, 1024:OWN].rearrange("p (s t) -> p s t", t=8), tmpf[:, 1024:OWN].rearrange("p (s t) -> p s t", t=8), a2[:, :, 15:23], ALU.subtract),
             reads=[TMB, ARB[0]], writes=[DTB])
        if j % 4 == 3:
            S.dma("pool", wp, g.w_pool[gi].rearrange("(c p) e -> p c e", p=128), writes=[WPB])
            for e_ in range(4):
                ob = oi % 2
                oi += 1
                for ti, (t0, tn) in enumerate([(0, 512), (512, 512), (1024, 128)]):
                    pb = ti % 4
                    S.group("pe", [lambda e, cc=cc, e_=e_, t0=t0, tn=tn, pb=pb: e.matmul(ps[:, pb * 512:pb * 512 + tn], wp[:, cc, e_ * 128:(e_ + 1) * 128], dT4[:, cc, t0:t0 + tn],
                                                                                         start=(cc == 0), stop=(cc == 3)) for cc in range(4)],
                            reads=[WPB, DTB], writes=[g.PB[pb]])
                    S.op("act", lambda e, e_=e_, t0=t0, tn=tn, pb=pb, ob=ob, gi=gi: e.activation(ost[ob][:, t0:t0 + tn], ps[:, pb * 512:pb * 512 + tn], AF.Copy, scale=pscale[:, gi * 4 + e_:gi * 4 + e_ + 1]),
                         reads=[g.PB[pb], CB], writes=[OST[ob]])
                rr = 2048 + (gi * 4 + e_) * 128
                S.dma("sp", g.oTd[rr:rr + 128, :], ost[ob], reads=[OST[ob]], writes=[g.OTB])
    S.dma("sp", g.o_pool_p, ppst[0:15, :], reads=[PST])
    for s in range(16):
        S.dma("sp", g.o_pool_s[s, 7:15, :], psst[s * 8:(s + 1) * 8, :], reads=[PST])
    A.release(mk)


def gemm_resid(g, actT, AB, nk, w_ap, x_src_fn, x_dst_fn, XR, XW, col_block=512):
    S, A = g.S, g.A
    ps = g.ps
    mk = A.mark()
    wb = [A.alloc([nk, col_block], BF16) for _ in range(2)]
    WB = [Buf("wo0"), Buf("wo1")]
    xt = [A.alloc([col_block], F32) for _ in range(4)]
    XT = [Buf("xt%d" % i) for i in range(4)]
    w_v = w_ap.rearrange("(kc p) c -> p kc c", p=128)
    ncb = D // col_block
    xi = 0
    for cb in range(ncb):
        b = cb % 2
        cs = slice(cb * col_block, (cb + 1) * col_block)
        step = max(1, nk // 4)
        for q0 in range(0, nk, step):
            S.dma("pool", wb[b][:, q0:q0 + step, :], w_v[:, q0:q0 + step, cs], writes=[WB[b]])
        for i in range(9):
            x = xi % 4
            xi += 1
            pb = xi % 4
            S.dma("sp", xt[x], x_src_fn(i, cs), reads=[XR], writes=[XT[x]])
            S.group("pe", [lambda e, kc=kc, i=i, b=b, pb=pb: e.matmul(ps[:, pb * 512:pb * 512 + col_block], actT[:, kc, i * 128:(i + 1) * 128], wb[b][:, kc, :],
                                                                      start=(kc == 0), stop=(kc == nk - 1)) for kc in range(nk)],
                    reads=[AB, WB[b]], writes=[g.PB[pb]])
            S.op("dve", lambda e, x=x, pb=pb: e.tensor_tensor(xt[x], xt[x], ps[:, pb * 512:pb * 512 + col_block], ALU.add), reads=[XT[x], g.PB[pb]], writes=[XT[x]])
            S.dma("sp", x_dst_fn(i, cs), xt[x], reads=[XT[x]], writes=[Buf("xw")])
    A.release(mk)


def phase_m4(g):
    S, A = g.S, g.A
    mk = A.mark()
    oT = A.alloc([NCH, OWN], BF16)
    OB = Buf("oT")
    ov = g.oTd.rearrange("(kc p) t -> p kc t", p=128)
    for q in range(4):
        S.dma("sp", oT[:, q * 8:(q + 1) * 8, :], ov[:, q * 8:(q + 1) * 8, :], reads=[g.OTB], writes=[OB])
    g.XSB = Buf("xs")
    XIN = Buf("xin")
    gemm_resid(g, oT, OB, NCH, g.w_out, lambda i, cs: g.xin[1024 + i * 128:1024 + (i + 1) * 128, cs],
               lambda i, cs: g.xs[i * 128:(i + 1) * 128, cs], XIN, g.XSB)
    A.release(mk)


def proj_T(g, hT, HB, ntok, w_ap, col_lo, ncols, evac, blk=256):
    S, A = g.S, g.A
    ps = g.ps
    mk = A.mark()
    wb = [A.alloc([NCH, blk], BF16) for _ in range(2)]
    WB = [Buf("pw0"), Buf("pw1")]
    w_v = w_ap.rearrange("(kc p) c -> p kc c", p=128)
    tg = []
    t = 0
    while t < ntok:
        n = min(512, ntok - t)
        tg.append((t, n))
        t += n
    pi = 0
    for bi in range(ncols // blk):
        b = bi % 2
        lo = col_lo + bi * blk
        for q in range(4):
            S.dma("pool", wb[b][:, q * 8:(q + 1) * 8, :], w_v[:, q * 8:(q + 1) * 8, lo:lo + blk], writes=[WB[b]])
        for cj in range(blk // 128):
            j = (bi * blk) // 128 + cj
            for (t0, tn) in tg:
                pb = pi % 4
                pi += 1
                S.group("pe", [lambda e, kc=kc, b=b, cj=cj, t0=t0, tn=tn, pb=pb: e.matmul(ps[:, pb * 512:pb * 512 + tn], wb[b][:, kc, cj * 128:(cj + 1) * 128], hT[:, kc, t0:t0 + tn],
                                                                                     start=(kc == 0), stop=(kc == NCH - 1)) for kc in range(NCH)],
                        reads=[WB[b], HB], writes=[g.PB[pb]])
                evac(j, t0, tn, ps[:, pb * 512:pb * 512 + tn], g.PB[pb])
    A.release(mk)


def phase_x(g):
    S, A = g.S, g.A
    ps = g.ps
    identb = g.identb
    mk = A.mark()
    gv = A.alloc([NCH], F32)
    gm = A.alloc([NCH], F32)
    GV = Buf("gvx")
    S.dma("sp", gv, g.norm_xa_g, writes=[GV])
    S.dma("sp", gm, g.norm_mem_g, writes=[GV])
    qmask = A.alloc([16, 128], BF16)
    S.dma("pool", qmask, g.c_qmask, writes=[GV])
    qT = A.alloc([4, OWN], BF16)
    QB = Buf("qT")
    mkT = A.alloc([4, 256], BF16)
    KB = Buf("mkT")
    mvb = A.alloc([2, 512], BF16)
    VB = Buf("mvb")
    mkh = A.mark()
    hT = A.alloc([NCH, OWN], BF16)
    HB = Buf("hT2")
    xs_t = g.xs.rearrange("(n p) d -> n p d", p=128)
    norm_transpose(g, [xs_t[i] for i in range(9)], 9, gv, GV, hT, HB, 0)
    sc_ = 128.0 ** -0.5

    def ev_q(j, t0, tn, pap, PBf):
        S.op("act", lambda e: e.activation(qT[:, j, t0:t0 + tn], pap, AF.Copy, scale=sc_), reads=[PBf], writes=[QB])
    proj_T(g, hT, HB, OWN, g.w_xq, 0, 512, ev_q)
    A.release(mkh)
    if g.stop <= 7.1:
        return
    mT = A.alloc([NCH, 256], BF16)
    MB = Buf("mT")
    mem_t = g.mem.rearrange("(n p) d -> n p d", p=128)
    norm_transpose(g, [mem_t[i] for i in range(2)], 2, gm, GV, mT, MB, 0)
    if g.stop <= 7.11:
        return

    def ev_k(j, t0, tn, pap, PBf):
        S.op("act", lambda e: e.activation(mkT[:, j, t0:t0 + tn], pap, AF.Copy), reads=[PBf], writes=[KB])
    proj_T(g, mT, MB, 256, g.w_mk, 0, 512, ev_k)
    if g.stop <= 7.12:
        return
    mk2 = A.mark()
    wb = A.alloc([NCH, 512], BF16)
    WBF = Buf("wkv")
    stf = [A.alloc([512], F32) for _ in range(2)]
    SF = [Buf("stf0"), Buf("stf1")]
    for wi, (w_ap, o_ap) in enumerate([(g.w_mk, g.o_mk), (g.w_mv, g.o_mv)]):
        w_v = w_ap.rearrange("(kc p) c -> p kc c", p=128)
        for q in range(4):
            S.dma("pool", wb[:, q * 8:(q + 1) * 8, :], w_v[:, q * 8:(q + 1) * 8, :], writes=[WBF])
        for mt in range(2):
            pb = mt
            S.group("pe", [lambda e, kc=kc, mt=mt, pb=pb: e.matmul(ps[:, pb * 512:(pb + 1) * 512], mT[:, kc, mt * 128:(mt + 1) * 128], wb[:, kc, :], start=(kc == 0), stop=(kc == NCH - 1))
                           for kc in range(NCH)], reads=[MB, WBF], writes=[g.PB[pb]])
            S.op("act", lambda e, mt=mt, pb=pb: e.activation(stf[mt], ps[:, pb * 512:(pb + 1) * 512], AF.Copy), reads=[g.PB[pb]], writes=[SF[mt]])
            if wi == 1:
                S.op("dve", lambda e, mt=mt: e.tensor_copy(mvb[:, mt, :], stf[mt]), reads=[SF[mt]], writes=[VB])
            S.dma("sp", o_ap[mt * 128:(mt + 1) * 128, :], stf[mt], reads=[SF[mt]])
        if g.stop <= 7.13:
            return
    A.release(mkh)
    if g.stop <= 7.2:
        return
    kT = A.alloc([16, 4, 256], BF16)
    KTB = Buf("kT")
    vb = A.alloc([16, 2, 512], BF16)
    VSB = Buf("vb")
    mk3 = A.mark()
    kb = A.alloc([16, 2, 512], BF16)
    KBB = Buf("kb")
    for s in range(16):
        S.dma("pool", kb[:, s, :, :], g.ck[s].rearrange("(c p) e -> p c e", p=128), writes=[KBB])
        S.dma("pool", vb[:, s, :, :], g.cv[s].rearrange("(c p) e -> p c e", p=128), writes=[VSB])
    for s in range(16):
        pb = 4 + s % 2
        pz = bank(g, pb, BF16)
        S.group("pe", [lambda e, s=s, hd=hd, mc=mc, pz=pz: e.transpose(pz[:, (hd * 2 + mc) * 128:(hd * 2 + mc + 1) * 128], kb[:, s, mc, hd * 128:(hd + 1) * 128], identb)
                       for hd in range(4) for mc in range(2)], reads=[KBB, g.CB], writes=[g.PB[pb]])
        S.op("act" if s % 2 == 0 else "dve",
             (lambda e, s=s, pz=pz: e.activation(kT[:, s, :, :], pz.rearrange("p (h m) -> p h m", h=4), AF.Copy)) if s % 2 == 0 else
             (lambda e, s=s, pz=pz: e.tensor_copy(kT[:, s, :, :], pz.rearrange("p (h m) -> p h m", h=4))), reads=[g.PB[pb]], writes=[KTB])
    A.release(mk3)
    if g.stop <= 7.3:
        return
    aoT = A.alloc([4, OWN], BF16)
    AOB = Buf("aoT")
    qm = A.alloc([4, 16, 128], BF16)
    QMB = Buf("qm")
    Pf = A.alloc([4, 256], F32)
    PFB = Buf("Pf")
    Pb_ = A.alloc([4, 256], BF16)
    PBB = Buf("Pb")
    PT = A.alloc([4, 2, 128], BF16)
    PTB = Buf("PT")
    stt = A.alloc([16], F32)
    STB = Buf("stt")
    for i in range(9):
        ts = slice(i * 128, (i + 1) * 128)
        smp = i == 8
        if g.stop <= 7.4 and smp:
            return
        if smp:
            S.op("dve", lambda e: e.tensor_tensor(qm, qT[:, :, 1024:1152].unsqueeze(2).to_broadcast([128, 4, 16, 128]), qmask.unsqueeze(1).to_broadcast([128, 4, 16, 128]), ALU.mult),
                 reads=[QB, GV], writes=[QMB])
        for hd in range(4):
            pb = hd // 2
            o = pb * 512 + (hd % 2) * 256
            if not smp:
                S.op("pe", lambda e, hd=hd, ts=ts, o=o: e.matmul(ps[:, o:o + 256], qT[:, hd, ts], mkT[:, hd, :], start=True, stop=True), reads=[QB, KB], writes=[g.PB[pb]])
            else:
                S.group("pe", [lambda e, hd=hd, s=s, o=o: e.matmul(ps[:, o:o + 256], qm[:, hd, s, :], kT[:, s, hd, :], start=(s == 0), stop=(s == 15)) for s in range(16)],
                        reads=[QMB, KTB], writes=[g.PB[pb]])
        scv = ps[:, 0:1024].rearrange("p (h m) -> p h m", h=4)
        S.op("dve", lambda e, scv=scv: e.tensor_reduce(stt[:, 0:4], scv, AX.X, ALU.max), reads=[g.PB[0], g.PB[1]], writes=[STB])
        S.op("dve", lambda e: e.tensor_scalar(stt[:, 4:8], stt[:, 0:4], -1.0, None, ALU.mult), reads=[STB], writes=[STB])
        for hd in range(4):
            S.op("act", lambda e, hd=hd, scv=scv: e.activation(Pf[:, hd, :], scv[:, hd, :], AF.Exp, bias=stt[:, 4 + hd:5 + hd], accum_out=stt[:, 8 + hd:9 + hd]),
                 reads=[g.PB[0], g.PB[1], STB], writes=[PFB, STB])
        S.op("dve", lambda e: e.reciprocal(stt[:, 12:16], stt[:, 8:12]), reads=[STB], writes=[STB])
        S.op("dve", lambda e: e.tensor_tensor(Pb_, Pf, stt[:, 12:16].unsqueeze(2).to_broadcast([128, 4, 256]), ALU.mult), reads=[PFB, STB], writes=[PBB])
        pz = bank(g, 2, BF16)
        S.group("pe", [lambda e, hd=hd, mc=mc, pz=pz: e.transpose(pz[:, (hd * 2 + mc) * 128:(hd * 2 + mc + 1) * 128], Pb_[:, hd, mc * 128:(mc + 1) * 128], identb)
                       for hd in range(4) for mc in range(2)], reads=[PBB, g.CB], writes=[g.PB[2]])
        S.op("act", lambda e, pz=pz: e.activation(PT, pz.rearrange("p (h c t) -> p h c t", h=4, c=2), AF.Copy), reads=[g.PB[2]], writes=[PTB])
        po = ps[:, 1536:2048]
        if not smp:
            fns = [lambda e, hd=hd, mc=mc: e.matmul(ps[:, 1536 + hd * 128:1536 + (hd + 1) * 128], mvb[:, mc, hd * 128:(hd + 1) * 128], PT[:, hd, mc, :], start=(mc == 0), stop=(mc == 1))
                   for hd in range(4) for mc in range(2)]
            S.group("pe", fns, reads=[VB, PTB], writes=[g.PB[3]])
        else:
            fns = [lambda e, hd=hd, mc=mc, s=s: e.matmul(ps[:, 1536 + hd * 128 + s * 8:1536 + hd * 128 + (s + 1) * 8], vb[:, s, mc, hd * 128:(hd + 1) * 128], PT[:, hd, mc, s * 8:(s + 1) * 8],
                                                         start=(mc == 0), stop=(mc == 1)) for hd in range(4) for s in range(16) for mc in range(2)]
            S.group("pe", fns, reads=[VSB, PTB], writes=[g.PB[3]])
        S.op("act", lambda e, ts=ts, po=po: e.activation(aoT[:, :, ts], po.rearrange("p (h t) -> p h t", h=4), AF.Copy), reads=[g.PB[3]], writes=[AOB])
    if g.stop <= 7.5:
        return
    g.XS2B = Buf("xs2")
    gemm_resid(g, aoT, AOB, 4, g.w_xo, lambda i, cs: g.xs[i * 128:(i + 1) * 128, cs], lambda i, cs: g.xs2[i * 128:(i + 1) * 128, cs], g.XSB, g.XS2B)
    A.release(mk)


def phase_f(g):
    S, A = g.S, g.A
    ps = g.ps
    mk = A.mark()
    gv = A.alloc([NCH], F32)
    GV = Buf("gvf")
    S.dma("sp", gv, g.norm_ffn_g, writes=[GV])
    hT = A.alloc([NCH, OWN], BF16)
    HB = Buf("hT3")
    x2_t = g.xs2.rearrange("(n p) d -> n p d", p=128)
    norm_transpose(g, [x2_t[i] for i in range(9)], 9, gv, GV, hT, HB, 0)
    hid = A.alloc([NCH, OWN], BF16)
    HDB = Buf("hid")
    rl = [A.alloc([512], F32) for _ in range(3)]
    RLB = [Buf("rl%d" % i) for i in range(3)]
    cnt = [0]
    bufs = [(g.xs2, g.XS2B), (g.xs, g.XSB)]
    for q in range(4):
        def ev_h(j, t0, tn, pap, PBf):
            r = cnt[0] % 3
            cnt[0] += 1
            S.op("act", lambda e: e.activation(rl[r][:, 0:tn], pap, AF.Relu), reads=[PBf], writes=[RLB[r]])
            S.op("dve", lambda e: e.tensor_tensor(hid[:, j, t0:t0 + tn], rl[r][:, 0:tn], rl[r][:, 0:tn], ALU.mult), reads=[RLB[r]], writes=[HDB])
        proj_T(g, hT, HB, OWN, g.w_up, q * D, D, ev_h)
        (src, SB_), (dst, DB_) = bufs[q % 2], bufs[(q + 1) % 2]
        gemm_resid(g, hid, HDB, NCH, g.w_down[q * D:(q + 1) * D, :], lambda i, cs, src=src: src[i * 128:(i + 1) * 128, cs],
                   lambda i, cs, dst=dst: dst[i * 128:(i + 1) * 128, cs], SB_, DB_, col_block=256)
    A.release(mk)
    mk = A.mark()
    gb = A.alloc([D], F32)
    GB = Buf("gfinal")
    S.dma("sp", gb, g.norm_final_row.partition_broadcast(128), writes=[GB])
    xt = [A.alloc([D], F32) for _ in range(2)]
    XB = [Buf("fx0"), Buf("fx1")]
    junk = A.alloc([D], BF16)
    JB = Buf("fjunk")
    st = A.alloc([9, 4], F32)
    STB = Buf("fst")
    x_t = g.xs2.rearrange("(n p) d -> n p d", p=128)
    y_t = g.y_out.rearrange("(n p) d -> n p d", p=128)
    for i in range(9):
        b = i % 2
        S.dma("sp", xt[b], x_t[i], reads=[g.XS2B], writes=[XB[b]])
        S.op("act", lambda e, b=b, i=i: e.activation(junk, xt[b], AF.Square, accum_out=st[:, i, 0:1]), reads=[XB[b]], writes=[JB, STB])
        S.op("act", lambda e, i=i: e.activation(st[:, i, 1:2], st[:, i, 0:1], AF.Sqrt, scale=1.0 / D, bias=1e-6), reads=[STB], writes=[STB])
        S.op("dve", lambda e, i=i: e.reciprocal(st[:, i, 2:3], st[:, i, 1:2]), reads=[STB], writes=[STB])
        S.op("dve", lambda e, b=b, i=i: e.scalar_tensor_tensor(xt[b], xt[b], st[:, i, 2:3], gb, ALU.mult, ALU.mult), reads=[XB[b], STB, GB], writes=[XB[b]])
        S.dma("sp", y_t[i], xt[b], reads=[XB[b]])
    A.release(mk)


_CACHE = {}


def kernel(**inputs):
    inp = {k: np.asarray(v) for k, v in inputs.items()}
    if "prog" not in _CACHE:
        _CACHE["prog"] = build_program()
    nc, g = _CACHE["prog"]
    consts = make_consts()
    in_maps = []
    for c in range(8):
        m = make_in_map(inp, c, consts)
        in_maps.append({k: m[k] for k in g.used_inputs})
    res = run_bass_kernel_spmd(nc, in_maps, core_ids=list(range(8)))
    R = res.results
    f32 = np.float32
    y_prompt = np.zeros((4, 2048, D), f32)
    y_sample = np.zeros((128, 8, D), f32)
    wkv_p = np.zeros((1, 4, 32, 64, 64), f32)
    sh_p = np.zeros((1, 4, SHIFT_W), f32)
    pl_p = np.zeros((1, 4, 15, RW), f32)
    mk_p = np.zeros((1, 4, 256, 4, 128), f32)
    mv_p = np.zeros((1, 4, 256, 4, 128), f32)
    wkv_s = np.zeros((1, 128, 32, 64, 64), f32)
    sh_s = np.zeros((1, 128, SHIFT_W), f32)
    pl_s = np.zeros((1, 128, 15, RW), f32)
    for c in range(8):
        b, half = c // 2, c % 2
        r = R[c]
        y = np.asarray(r["y_out"], f32)
        y_prompt[b, half * 1024:(half + 1) * 1024] = y[:1024]
        y_sample[16 * c:16 * c + 16] = y[1024:].reshape(16, 8, D)
        if half == 1:
            wkv_p[0, b] = r["o_wkv_p"]
            sh_p[0, b] = r["o_shift_p"]
            pl_p[0, b] = r["o_pool_p"]
        else:
            mk_p[0, b] = np.asarray(r["o_mk"]).reshape(256, 4, 128)
            mv_p[0, b] = np.asarray(r["o_mv"]).reshape(256, 4, 128)
        wkv_s[0, 16 * c:16 * c + 16] = r["o_wkv_s"]
        sh_s[0, 16 * c:16 * c + 16] = r["o_shift_s"]
        pl_s[0, 16 * c:16 * c + 16] = r["o_pool_s"]
    return (y_prompt, y_sample, wkv_p, sh_p, pl_p, mk_p, mv_p, wkv_s, sh_s, pl_s)
```
